# Optimizing a Trainium2 kernel written in Bass

```python
import math
import jax, jax.numpy as jnp
from jax import lax
import numpy as np

D_MODEL = 2048
BATCH = 1
SEQ = 16384
DEPTH = 1
DEC_BATCH = 128
DEC_SEQ = 1
PAST_LEN = 16384
PAGE_SIZE = 128

D_MIX = D_MODEL
ATT_W = D_MIX // 2
HEAD_DIM = 64
N_HEADS = ATT_W // HEAD_DIM
N_KV_HEADS = 4
Q_PER_KV = N_HEADS // N_KV_HEADS
WINDOW = 128
ATT_BLOCK = WINDOW
SSM_W = D_MIX - ATT_W
SSM_HEAD_DIM = 64
N_SSM_HEADS = SSM_W // SSM_HEAD_DIM
N_SSM_GROUPS = 2
SSM_HEADS_PER_GROUP = N_SSM_HEADS // N_SSM_GROUPS
D_STATE = 128
CONV_W = 4
CONV_DIM = SSM_W + 2 * N_SSM_GROUPS * D_STATE
SSD_CHUNK = 128
D_FF = -(-8 * D_MODEL // (3 * 256)) * 256
IN_W = ATT_W + 2 * N_KV_HEADS * HEAD_DIM + SSM_W + CONV_DIM + N_SSM_HEADS
EPS = 1e-6

kernel_name = 'hybrid_swa_sink_ssd_adaln_step'


def _rmsnorm(x, g):
    xf = x.astype(jnp.float32)
    y = xf * lax.rsqrt(jnp.mean(xf * xf, axis=-1, keepdims=True) + EPS)
    return (y * g.astype(jnp.float32)).astype(x.dtype)


def _alibi_slopes():
    s = 2.0 ** (-8.0 * np.arange(1, N_HEADS + 1) / N_HEADS)
    return jnp.asarray(s.astype(np.float32)).reshape(N_KV_HEADS, Q_PER_KV)


def _sink_attend(q, k, v, dist, valid, sinks):
    f32 = jnp.float32
    s = jnp.einsum('...tkgd,...skd->...kgts', q.astype(f32), k.astype(f32)) * (HEAD_DIM ** -0.5)
    s = s - _alibi_slopes()[:, :, None, None] * dist[..., None, None, :, :].astype(f32)
    s = jnp.where(valid[..., None, None, :, :], s, -jnp.inf)
    sink = sinks.astype(f32)[:, :, None, None]
    m = jnp.maximum(jnp.max(s, axis=-1, keepdims=True), sink)
    p = jnp.exp(s - m)
    denom = jnp.sum(p, axis=-1, keepdims=True) + jnp.exp(sink - m)
    return jnp.einsum('...kgts,...skd->...tkgd', p / denom, v.astype(f32))


def _attn_prompt(q, k, v, sinks):
    b, L = q.shape[:2]
    nb = L // ATT_BLOCK
    qb = q.reshape(b, nb, ATT_BLOCK, N_KV_HEADS, Q_PER_KV, HEAD_DIM)

    def with_prev(z):
        z = z.reshape(b, nb, ATT_BLOCK, N_KV_HEADS, HEAD_DIM)
        prev = jnp.pad(z[:, :-1], ((0, 0), (1, 0), (0, 0), (0, 0), (0, 0)))
        return jnp.concatenate([prev, z], axis=2)

    a = jnp.arange(ATT_BLOCK)[:, None]
    j = jnp.arange(2 * ATT_BLOCK)[None, :]
    dist = a + ATT_BLOCK - j
    blk = jnp.arange(nb)[:, None, None]
    valid = (dist >= 0) & (dist < WINDOW) & ((blk > 0) | (j >= ATT_BLOCK))
    o = _sink_attend(qb, with_prev(k), with_prev(v), dist[None], valid, sinks)
    return o.reshape(b, L, ATT_W)


def _attn_sample(q, k, v, past_k, past_v, sinks):
    db, L = q.shape[:2]
    wb = past_k.shape[1]
    kk = jnp.concatenate([past_k.astype(k.dtype), k], axis=1)
    vv = jnp.concatenate([past_v.astype(v.dtype), v], axis=1)
    dist = jnp.arange(L)[:, None] + wb - jnp.arange(wb + L)[None, :]
    valid = (dist >= 0) & (dist < WINDOW)
    o = _sink_attend(q, kk, vv, dist, valid, sinks)
    return o.reshape(db, L, ATT_W), kk[:, L:], vv[:, L:]


def _causal_conv(xbc, prefix, w, b):
    L = xbc.shape[1]
    xp = jnp.concatenate([prefix.astype(xbc.dtype), xbc], axis=1)
    y = sum(xp[:, i:i + L] * w[i] for i in range(CONV_W)) + b
    return jax.nn.silu(y), xp[:, L:]


def _ssd(x, dt, a, bmat, cmat, h0):
    b, L = x.shape[:2]
    T = min(SSD_CHUNK, L)
    pad = -(-L // T) * T - L
    if pad:
        x = jnp.pad(x, ((0, 0), (0, pad), (0, 0), (0, 0)))
        dt = jnp.pad(dt, ((0, 0), (0, pad), (0, 0)))
        bmat = jnp.pad(bmat, ((0, 0), (0, pad), (0, 0), (0, 0)))
        cmat = jnp.pad(cmat, ((0, 0), (0, pad), (0, 0), (0, 0)))
    Lp = L + pad
    nc = Lp // T
    G, R, P, N = N_SSM_GROUPS, SSM_HEADS_PER_GROUP, SSM_HEAD_DIM, D_STATE
    X = (x * dt[..., None]).reshape(b, nc, T, G, R, P)
    acum = jnp.cumsum((dt * a).reshape(b, nc, T, G, R), axis=2)
    Bc = bmat.reshape(b, nc, T, G, N)
    Cc = cmat.reshape(b, nc, T, G, N)
    acum_t = jnp.moveaxis(acum, 2, -1)
    seg = acum_t[..., :, None] - acum_t[..., None, :]
    causal = jnp.tril(jnp.ones((T, T), dtype=bool))
    lmat = jnp.exp(jnp.where(causal, seg, -jnp.inf))
    cb = jnp.einsum('bclgn,bcsgn->bcgls', Cc, Bc)
    y_diag = jnp.einsum('bcgrls,bcsgrp->bclgrp', cb[:, :, :, None] * lmat, X)
    decay_end = jnp.exp(acum[:, :, -1:] - acum)
    chunk_states = jnp.einsum('bclgn,bclgrp->bcgrpn', Bc, X * decay_end[..., None])
    chunk_decay = jnp.exp(acum[:, :, -1])

    def step(h, inp):
        s_c, d_c = inp
        return h * d_c[..., None, None] + s_c, h

    h_final, h_prev = lax.scan(step, h0.reshape(b, G, R, P, N),
                               (jnp.moveaxis(chunk_states, 1, 0), jnp.moveaxis(chunk_decay, 1, 0)))
    h_prev = jnp.moveaxis(h_prev, 0, 1)
    y_off = jnp.einsum('bclgn,bcgrpn->bclgrp', Cc, h_prev) * jnp.exp(acum)[..., None]
    y = (y_diag + y_off).reshape(b, Lp, N_SSM_HEADS, P)[:, :L]
    return y, h_final.reshape(b, N_SSM_HEADS, P, N)


def _mixer(u, past_k, past_v, conv_prefix, h0, w_in, attn_sinks, g_attn_out, conv_w, conv_b,
           dt_bias, a_log, d_skip, g_ssm_out, w_out):
    f32 = jnp.float32
    b, L, _ = u.shape
    kv_w = N_KV_HEADS * HEAD_DIM
    cuts = [ATT_W, ATT_W + kv_w, ATT_W + 2 * kv_w, ATT_W + 2 * kv_w + SSM_W,
            ATT_W + 2 * kv_w + SSM_W + CONV_DIM]
    q, k, v, z, xbc, dt_raw = jnp.split(u @ w_in, cuts, axis=-1)
    q = q.reshape(b, L, N_KV_HEADS, Q_PER_KV, HEAD_DIM)
    k = k.reshape(b, L, N_KV_HEADS, HEAD_DIM)
    v = v.reshape(b, L, N_KV_HEADS, HEAD_DIM)
    sinks = attn_sinks.reshape(N_KV_HEADS, Q_PER_KV)
    if past_k is None:
        att = _attn_prompt(q, k, v, sinks)
        wbuf = min(WINDOW, L)
        new_k, new_v = k[:, L - wbuf:], v[:, L - wbuf:]
    else:
        att, new_k, new_v = _attn_sample(q, k, v, past_k, past_v, sinks)
    att = _rmsnorm(att.astype(u.dtype), g_attn_out)
    xbc, new_conv = _causal_conv(xbc, conv_prefix, conv_w, conv_b)
    gn = N_SSM_GROUPS * D_STATE
    xs, bm, cm = jnp.split(xbc, [SSM_W, SSM_W + gn], axis=-1)
    xs = xs.astype(f32).reshape(b, L, N_SSM_HEADS, SSM_HEAD_DIM)
    dt = jax.nn.softplus(dt_raw.astype(f32) + dt_bias.astype(f32))
    a = -jnp.exp(a_log.astype(f32))
    y, h_new = _ssd(xs, dt, a,
                    bm.astype(f32).reshape(b, L, N_SSM_GROUPS, D_STATE),
                    cm.astype(f32).reshape(b, L, N_SSM_GROUPS, D_STATE),
                    h0.astype(f32))
    y = y + d_skip.astype(f32)[:, None] * xs
    y = y.reshape(b, L, SSM_W) * jax.nn.silu(z.astype(f32))
    ssm = _rmsnorm(y, g_ssm_out).astype(u.dtype)
    out = jnp.concatenate([att, ssm], axis=-1) @ w_out
    return out, new_k, new_v, new_conv, h_new


def _layer(x, c, past_k, past_v, conv_prefix, h0, w_ada, b_ada, g_pre_mix, g_post_mix, w_in,
           attn_sinks, g_attn_out, conv_w, conv_b, dt_bias, a_log, d_skip, g_ssm_out, w_out,
           g_pre_ffn, g_post_ffn, w_gate, w_up, w_down):
    mod = (jax.nn.silu(c) @ w_ada + b_ada)[:, None, :]
    sh1, sc1, gt1, sh2, sc2, gt2 = jnp.split(mod, 6, axis=-1)
    u = _rmsnorm(x, g_pre_mix) * (1 + sc1) + sh1
    mix, nk, nv, nconv, nh = _mixer(u, past_k, past_v, conv_prefix, h0, w_in, attn_sinks,
                                    g_attn_out, conv_w, conv_b, dt_bias, a_log, d_skip,
                                    g_ssm_out, w_out)
    x = x + gt1 * _rmsnorm(mix, g_post_mix)
    u = _rmsnorm(x, g_pre_ffn) * (1 + sc2) + sh2
    f = (jax.nn.silu(u @ w_gate) * (u @ w_up)) @ w_down
    x = x + gt2 * _rmsnorm(f, g_post_ffn)
    return x, nk, nv, nconv, nh


def setup_inputs(seed: int = 0) -> dict:
    key = jax.random.key(seed)
    ks = iter(jax.random.split(key, 40))
    f32 = jnp.float32

    def nrm(shape, scale):
        return jax.random.normal(next(ks), shape, f32) * scale

    def gain(shape):
        return 1.0 + nrm(shape, 0.02)

    wbuf = min(WINDOW, PAST_LEN)
    dt0 = jnp.exp(jax.random.uniform(next(ks), (DEPTH, N_SSM_HEADS), f32,
                                     math.log(1e-3), math.log(1e-1)))
    return {
        'x_prompt': nrm((BATCH, SEQ, D_MODEL), 1.0),
        'x_sample': nrm((DEC_BATCH, DEC_SEQ, D_MODEL), 1.0),
        'cache_k': nrm((DEPTH, DEC_BATCH, wbuf, N_KV_HEADS, HEAD_DIM), 1.0),
        'cache_v': nrm((DEPTH, DEC_BATCH, wbuf, N_KV_HEADS, HEAD_DIM), 1.0),
        'state_conv': nrm((DEPTH, DEC_BATCH, CONV_W - 1, CONV_DIM), 1.0),
        'state_ssm': nrm((DEPTH, DEC_BATCH, N_SSM_HEADS, SSM_HEAD_DIM, D_STATE), 0.5),
        'c_prompt': nrm((BATCH, D_MODEL), 1.0),
        'c_sample': nrm((DEC_BATCH, D_MODEL), 1.0),
        'w_ada': nrm((DEPTH, D_MODEL, 6 * D_MODEL), 0.5 * D_MODEL ** -0.5),
        'b_ada': nrm((DEPTH, 6 * D_MODEL), 0.01),
        'g_pre_mix': gain((DEPTH, D_MODEL)),
        'g_post_mix': gain((DEPTH, D_MODEL)),
        'w_in': nrm((DEPTH, D_MODEL, IN_W), D_MODEL ** -0.5),
        'attn_sinks': nrm((DEPTH, N_HEADS), 1.0),
        'g_attn_out': gain((DEPTH, ATT_W)),
        'conv_w': nrm((DEPTH, CONV_W, CONV_DIM), CONV_W ** -0.5),
        'conv_b': nrm((DEPTH, CONV_DIM), 0.02),
        'dt_bias': dt0 + jnp.log(-jnp.expm1(-dt0)),
        'a_log': jnp.log(jax.random.uniform(next(ks), (DEPTH, N_SSM_HEADS), f32, 1.0, 16.0)),
        'd_skip': 1.0 + nrm((DEPTH, N_SSM_HEADS), 0.1),
        'g_ssm_out': gain((DEPTH, SSM_W)),
        'w_out': nrm((DEPTH, D_MIX, D_MODEL), D_MIX ** -0.5),
        'g_pre_ffn': gain((DEPTH, D_MODEL)),
        'g_post_ffn': gain((DEPTH, D_MODEL)),
        'w_gate': nrm((DEPTH, D_MODEL, D_FF), D_MODEL ** -0.5),
        'w_up': nrm((DEPTH, D_MODEL, D_FF), D_MODEL ** -0.5),
        'w_down': nrm((DEPTH, D_FF, D_MODEL), D_FF ** -0.5),
    }


def reference(x_prompt, x_sample, cache_k, cache_v, state_conv, state_ssm, c_prompt, c_sample,
              w_ada, b_ada, g_pre_mix, g_post_mix, w_in, attn_sinks, g_attn_out, conv_w, conv_b,
              dt_bias, a_log, d_skip, g_ssm_out, w_out, g_pre_ffn, g_post_ffn, w_gate, w_up, w_down):
    weights = (w_ada, b_ada, g_pre_mix, g_post_mix, w_in, attn_sinks, g_attn_out, conv_w, conv_b,
               dt_bias, a_log, d_skip, g_ssm_out, w_out, g_pre_ffn, g_post_ffn, w_gate, w_up, w_down)
    yp, ys = x_prompt, x_sample
    bp = x_prompt.shape[0]
    kp_l, vp_l, cp_l, hp_l, ks_l, vs_l, cs_l, hs_l = [], [], [], [], [], [], [], []
    for l in range(DEPTH):
        lw = [w[l] for w in weights]
        conv0 = jnp.zeros((bp, CONV_W - 1, CONV_DIM), yp.dtype)
        h0 = jnp.zeros((bp, N_SSM_HEADS, SSM_HEAD_DIM, D_STATE), jnp.float32)
        yp, kp, vp, cp, hp = _layer(yp, c_prompt, None, None, conv0, h0, *lw)
        ys, ksm, vsm, csm, hsm = _layer(ys, c_sample, cache_k[l], cache_v[l], state_conv[l],
                                        state_ssm[l], *lw)
        kp_l.append(kp); vp_l.append(vp); cp_l.append(cp); hp_l.append(hp)
        ks_l.append(ksm); vs_l.append(vsm); cs_l.append(csm); hs_l.append(hsm)
    k_prompt = jnp.stack(kp_l).astype(cache_k.dtype)
    v_prompt = jnp.stack(vp_l).astype(cache_v.dtype)
    conv_prompt = jnp.stack(cp_l).astype(state_conv.dtype)
    ssm_prompt = jnp.stack(hp_l).astype(state_ssm.dtype)
    k_sample = jnp.stack(ks_l).astype(cache_k.dtype)
    v_sample = jnp.stack(vs_l).astype(cache_v.dtype)
    conv_sample = jnp.stack(cs_l).astype(state_conv.dtype)
    ssm_sample = jnp.stack(hs_l).astype(state_ssm.dtype)
    return (yp, ys, k_prompt, v_prompt, conv_prompt, ssm_prompt,
            k_sample, v_sample, conv_sample, ssm_sample)
```

```python
import contextlib
import numpy as np
import ml_dtypes
import concourse.bass as bass
import concourse.mybir as mybir
from concourse.bass_utils import run_bass_kernel_spmd

F32 = mybir.dt.float32
BF16 = mybir.dt.bfloat16
ALU = mybir.AluOpType
AF = mybir.ActivationFunctionType
AX = mybir.AxisListType

ENGS = ("pe", "act", "dve", "pool", "sp")


class Buf:
    __slots__ = ("name", "last_w", "readers")

    def __init__(self, name):
        self.name = name
        self.last_w = None
        self.readers = []


class Sched:
    def __init__(self, nc, n_dma_sp=40, n_dma_pool=12, self_wait=True):
        self.nc = nc
        self.q = {e: [] for e in ENGS}
        self.count = {e: 0 for e in ENGS}
        self.seen = {e: {} for e in ENGS}
        self.self_wait = self_wait
        self.ndma = {"sp": n_dma_sp, "pool": n_dma_pool, "act": 8}
        self.dma_next = {"sp": 0, "pool": 0, "act": 0}
        self.dma_cnt = {}
        self.sems = {}

    def _deps(self, eng, reads, writes):
        deps = {}

        def add(tok):
            if tok is None:
                return
            k, v = tok
            if deps.get(k, 0) < v:
                deps[k] = v
        for b in reads:
            add(b.last_w)
        for b in writes:
            add(b.last_w)
            for r in b.readers:
                add(r)
        out = []
        for k, v in deps.items():
            if k == eng and (eng == "pe" or not self.self_wait):
                continue
            if self.seen[eng].get(k, 0) >= v:
                continue
            self.seen[eng][k] = v
            out.append((k, v))
        return out

    def _commit(self, tok, reads, writes):
        for b in writes:
            b.last_w = tok
            b.readers = []
        for b in reads:
            if b not in writes:
                b.readers.append(tok)
                if len(b.readers) > 64:
                    best = {}
                    for k, v in b.readers:
                        if best.get(k, 0) < v:
                            best[k] = v
                    b.readers = list(best.items())

    def op(self, eng, fns, reads=(), writes=()):
        if callable(fns):
            fns = [fns]
        waits = self._deps(eng, reads, writes)
        self.count[eng] += 1
        tok = (eng, self.count[eng])
        self.q[eng].append(("op", waits, fns, tok))
        self._commit(tok, reads, writes)
        return tok

    def dma(self, eng, pairs, reads=(), writes=(), **kw):
        i = self.dma_next[eng]
        self.dma_next[eng] = (i + 1) % self.ndma[eng]
        key = "d_%s_%d" % (eng, i)
        prev = self.dma_cnt.get(key, 0)
        waits = self._deps(eng, reads, writes)
        if prev and self.seen[eng].get(key, 0) < prev:
            self.seen[eng][key] = prev
            waits.append((key, prev))
        val = prev + 16 * len(pairs)
        self.dma_cnt[key] = val
        tok = (key, val)
        self.q[eng].append(("dma", waits, pairs, tok, kw))
        self._commit(tok, reads, writes)
        return tok

    def barrier(self):
        targets = [(e, self.count[e]) for e in ENGS if self.count[e] > 0]
        targets += [(k, v) for k, v in self.dma_cnt.items()]
        for e in ENGS:
            waits = []
            for k, v in targets:
                if k == e:
                    continue
                if self.seen[e].get(k, 0) >= v:
                    continue
                self.seen[e][k] = v
                waits.append((k, v))
            if waits:
                self.q[e].append(("wait", waits))

    def final_wait(self, eng="sp"):
        waits = [(k, v) for k, v in self.dma_cnt.items()]
        waits += [(e, self.count[e]) for e in ENGS if self.count[e] > 0 and e != eng]
        self.q[eng].append(("wait", waits))

    def emit(self):
        nc = self.nc
        keys = [e for e in ENGS if self.count[e] > 0] + sorted(self.dma_cnt.keys())
        with contextlib.ExitStack() as st:
            for k in keys:
                self.sems[k] = st.enter_context(nc.semaphore("s_" + k))
            block = st.enter_context(nc.Block())
            sems = self.sems

            def run(engobj, items):
                for it in items:
                    for (k, v) in it[1]:
                        engobj.wait_ge(sems[k], v)
                    if it[0] == "op":
                        _, _, fns, tok = it
                        ins = None
                        for f in fns:
                            ins = f(engobj)
                        ins.then_inc(sems[tok[0]], 1)
                    elif it[0] == "dma":
                        _, _, pairs, tok, kw = it
                        for (o, i) in pairs:
                            engobj.dma_start(out=o, in_=i, **kw).then_inc(sems[tok[0]], 16)

            if self.q["pe"]:
                @block.tensor
                def _(e):
                    run(e, self.q["pe"])
            if self.q["act"]:
                @block.scalar
                def _(e):
                    run(e, self.q["act"])
            if self.q["dve"]:
                @block.vector
                def _(e):
                    run(e, self.q["dve"])
            if self.q["pool"]:
                @block.gpsimd
                def _(e):
                    run(e, self.q["pool"])
            if self.q["sp"]:
                @block.sync
                def _(e):
                    run(e, self.q["sp"])


class Tile:
    def __init__(self, ap, name):
        self.ap = ap
        self.b = Buf(name)

    def v(self, pat, **kw):
        return self.ap.rearrange(pat, **kw)


class Arena:
    def __init__(self, base, nwords):
        self.base = base
        self.n = nwords
        self.off = 0
        self.peak = 0

    def alloc(self, name, nelem, dt=F32):
        words = nelem if dt == F32 else (nelem + 1) // 2
        wal = (words + 7) // 8 * 8
        assert self.off + wal <= self.n, ("SBUF arena overflow", name, self.off, wal, self.n)
        ap = self.base[:, self.off:self.off + words]
        if dt != F32:
            ap = ap.bitcast(dt)
        self.off += wal
        self.peak = max(self.peak, self.off)
        return Tile(ap, name)

    def mark(self):
        return self.off

    def release(self, m):
        self.off = m


def mm(out, lhsT, rhs, start, stop):
    return lambda e: e.matmul(out, lhsT, rhs, start=start, stop=stop)


def tp(out, in_, ident):
    return lambda e: e.transpose(out, in_, ident)
D = 2048
KC = 16
NEXT = 16384
NOWN = 2048
EPS = 1e-6
NPRE_G = (NEXT - NOWN) // 512
INW = 4112
DFF = 5632
NJ = DFF // 128


def build(n_pre_groups=NPRE_G, stop_after=None, do_samples=True):
    nc = bass.Bass("TRN2", target_bir_lowering=False)

    def din(name, shape, dt=F32):
        return nc.dram_tensor(name, list(shape), dt, kind="ExternalInput").ap()

    def dout(name, shape, dt=F32):
        return nc.dram_tensor(name, list(shape), dt, kind="ExternalOutput").ap()

    def dscr(name, shape, dt=F32):
        return nc.dram_tensor(name, list(shape), dt, kind="Internal").ap()

    xext = din("xext", [NEXT, D])
    mrow = din("mrow", [1, NEXT])
    mtok = din("mtok", [128, NEXT // 128])
    cmat = din("cmat", [17, D])
    wada = din("wada", [D + 1, 6 * D])
    gvec = din("gvec", [4, D])
    w_in = din("w_in", [D, INW])
    w_out = din("w_out", [D, D])
    w_gate = din("w_gate", [D, DFF])
    w_up = din("w_up", [D, DFF])
    w_down = din("w_down", [DFF, D])
    convw = din("convw", [128, 48])
    convb = din("convb", [128, 12])
    hvec = din("hvec", [1, 64])
    gatt = din("gatt", [1, 1024])
    gssm = din("gssm", [1, 1024])
    cf32 = din("cf32", [128, 5 * 128])
    cb16 = din("cb16", [128, 2 * 128], BF16)
    abias0 = din("abias0", [128, 16 * 256])
    abias1 = din("abias1", [128, 16 * 256])
    y_out = dout("y_out", [NOWN, D])
    k_out = dout("k_out", [128, 256])
    v_out = dout("v_out", [128, 256])
    conv_out = dout("conv_out", [3, 1536])
    ssm_out = dout("ssm_out", [1024, 128])
    mod_d = dscr("mod_d", [17, 6 * D])
    cat_d = dscr("cat_d", [NOWN, D], BF16)
    x1_d = dscr("x1_d", [NOWN, D])
    hT_d = dscr("hT_d", [NJ, 128, NOWN], BF16)

    st = contextlib.ExitStack()
    NW = 47616
    big = st.enter_context(nc.sbuf_tensor("big", [128, NW], F32))
    ps = st.enter_context(nc.psum_tensor("ps", [128, 4096], F32))
    A = Arena(big, NW)
    S = Sched(nc)
    PB = [Buf("psb%d" % i) for i in range(8)]

    def bank(i, n=512, off=0):
        return ps[:, i * 512 + off:i * 512 + off + n]

    def bank16(i):
        return ps[:, i * 512:(i + 1) * 512].bitcast(BF16)

    MODD = Buf("mod_d")
    w_in_v = w_in.rearrange("(kc p) n -> p kc n", p=128)

    cF = A.alloc("cF", 5 * 128)
    cB = A.alloc("cB", 2 * 128, BF16)
    S.dma("sp", [(cF.ap, cf32)], writes=[cF.b])
    S.dma("sp", [(cB.ap, cb16)], writes=[cB.b])
    identf = cF.ap[:, 0:128]
    Umat = cF.ap[:, 128:256]
    UTmat = cF.ap[:, 256:384]
    onesf = cF.ap[:, 384:512]
    caus01 = cF.ap[:, 512:640]
    identb = cB.ap[:, 0:128]
    onesb = cB.ap[:, 128:256]
    hv = A.alloc("hv", 64)
    S.dma("sp", [(hv.ap, hvec.to_broadcast([128, 64]))], writes=[hv.b])
    dtb_bc = hv.ap[:, 0:16]
    dsk_bc = hv.ap[:, 32:48]
    sink_bc = hv.ap[:, 48:64]
    a_bc = A.alloc("a_bc", 16)
    S.op("act", lambda e: e.activation(a_bc.ap, hv.ap[:, 16:32], AF.Exp), reads=[hv.b], writes=[a_bc.b])
    S.op("dve", lambda e: e.tensor_scalar(a_bc.ap, a_bc.ap, -1.0, None, ALU.mult), reads=[a_bc.b], writes=[a_bc.b])
    cw = A.alloc("cw", 48)
    cbi = A.alloc("cbi", 12)
    S.dma("sp", [(cw.ap, convw)], writes=[cw.b])
    S.dma("sp", [(cbi.ap, convb)], writes=[cbi.b])
    mt = A.alloc("mt", NEXT // 128)
    S.dma("sp", [(mt.ap, mtok)], writes=[mt.b])

    def phase0():
        m0 = A.mark()
        cm = A.alloc("cm", D)
        S.dma("sp", [(cm.ap[0:17, :], cmat)], writes=[cm.b])
        ee = A.alloc("ee", D)
        S.op("act", lambda e: e.activation(ee.ap[0:17], cm.ap[0:17], AF.Exp, scale=-1.0), reads=[cm.b], writes=[ee.b])
        S.op("dve", lambda e: e.tensor_scalar(ee.ap[0:17], ee.ap[0:17], 1.0, None, ALU.add), reads=[ee.b], writes=[ee.b])
        S.op("dve", lambda e: e.reciprocal(ee.ap[0:17], ee.ap[0:17]), reads=[ee.b], writes=[ee.b])
        sc = A.alloc("sc", D, BF16)
        S.op("dve", lambda e: e.tensor_tensor(sc.ap[0:17], cm.ap[0:17], ee.ap[0:17], ALU.mult), reads=[cm.b, ee.b], writes=[sc.b])
        cT = A.alloc("cT", 16 * 32, BF16)
        cT3 = cT.v("p (k m) -> p k m", k=16)
        pb = bank16(0).rearrange("p (k m) -> p k m", m=32)
        S.op("pe", [tp(pb[:, kc, 0:17], sc.ap[0:17, kc * 128:(kc + 1) * 128], identb[0:17, 0:17]) for kc in range(16)],
             reads=[sc.b, cB.b], writes=[PB[0]])
        S.op("act", lambda e: e.activation(cT3[:, :, 0:17], pb[:, 0:16, 0:17], AF.Copy), reads=[PB[0]], writes=[cT.b])
        gv = A.alloc("gv", 4 * D)
        S.dma("sp", [(gv.ap[0:17, g * D:(g + 1) * D], gvec[g:g + 1, :].to_broadcast([17, D])) for g in range(4)], writes=[gv.b])
        mod = A.alloc("mod", 6 * D)
        slots = [A.alloc("wa%d" % i, 17 * 512, BF16) for i in range(2)]
        wada_v = wada[0:D, :].rearrange("(kc p) n -> p kc n", p=128)
        for pc in range(24):
            sl = slots[pc % 2]
            s3 = sl.v("p (k n) -> p k n", k=17)
            S.dma("pool", [(s3[:, 0:16, :], wada_v[:, :, pc * 512:(pc + 1) * 512]),
                           (s3[0:1, 16, :], wada[D:D + 1, pc * 512:(pc + 1) * 512])], writes=[sl.b])
            bi = 1 + pc % 2
            o = ps[0:17, bi * 512:(bi + 1) * 512]
            S.op("pe", [mm(o, cT3[:, kc, 0:17], s3[:, kc, :], kc == 0, False) for kc in range(16)]
                 + [mm(o, onesb[0:1, 0:17], s3[0:1, 16, :], False, True)],
                 reads=[cT.b, sl.b, cB.b], writes=[PB[bi]])
            ch, co = pc // 4, (pc % 4) * 512
            dst = mod.ap[0:17, pc * 512:(pc + 1) * 512]
            if ch in (0, 3):
                S.op("act", lambda e, dst=dst, o=o: e.activation(dst, o, AF.Copy), reads=[PB[bi]], writes=[mod.b])
            elif ch in (1, 4):
                g = gv.ap[0:17, (0 if ch == 1 else 2) * D + co:(0 if ch == 1 else 2) * D + co + 512]
                S.op("dve", lambda e, dst=dst, o=o, g=g: e.scalar_tensor_tensor(dst, o, 1.0, g, ALU.add, ALU.mult),
                     reads=[PB[bi], gv.b], writes=[mod.b])
            else:
                g = gv.ap[0:17, (1 if ch == 2 else 3) * D + co:(1 if ch == 2 else 3) * D + co + 512]
                S.op("dve", lambda e, dst=dst, o=o, g=g: e.tensor_tensor(dst, o, g, ALU.mult),
                     reads=[PB[bi], gv.b], writes=[mod.b])
        S.dma("sp", [(mod_d, mod.ap[0:17, :])], reads=[mod.b], writes=[MODD])
        S.barrier()
        A.release(m0)

    phase0()

    def load_mod_bc(tile, ch):
        S.dma("sp", [(tile.ap, mod_d[16:17, ch * D:(ch + 1) * D].to_broadcast([128, D]))], reads=[MODD], writes=[tile.b])

    halo = A.alloc("halo", 36)
    halo3 = halo.v("p (c i) -> p c i", i=3)
    S.op("dve", lambda e: e.memset(halo.ap, 0.0), writes=[halo.b])
    hst = A.alloc("hst", 1024)
    S.op("dve", lambda e: e.memset(hst.ap, 0.0), writes=[hst.b])
    hb = A.alloc("hb", 1024, BF16)
    S.op("dve", lambda e: e.memset(hb.ap, 0.0), writes=[hb.b])
    xb = [A.alloc("xb%d" % i, D) for i in range(2)]
    junk = A.alloc("junk", D, BF16)
    ub = A.alloc("ub", D, BF16)
    tmp32 = A.alloc("tmp32", D)
    sm = A.alloc("sm", 512)
    smc = [0]

    def small(n):
        if smc[0] + n > 512:
            smc[0] = 0
        a = sm.ap[:, smc[0]:smc[0] + n]
        smc[0] += n
        return a

    def rms_rstd(src_ap, n, rd, wr_extra=()):
        ssn = small(1)
        pp = src_ap.shape[0]
        S.op("act", lambda e: e.activation(junk.ap[0:pp, 0:n], src_ap, AF.Square, scale=float(n) ** -0.5, accum_out=ssn[0:pp]),
             reads=list(rd), writes=[junk.b, sm.b])
        S.op("dve", lambda e: e.tensor_scalar(ssn[0:pp], ssn[0:pp], EPS, None, ALU.add), reads=[sm.b], writes=[sm.b])
        S.op("act", lambda e: e.activation(ssn[0:pp], ssn[0:pp], AF.Ln), reads=[sm.b], writes=[sm.b])
        S.op("act", lambda e: e.activation(ssn[0:pp], ssn[0:pp], AF.Exp, scale=-0.5), reads=[sm.b], writes=[sm.b])
        return ssn

    def norm_mod_T(x_tile, gm, sh, dstT3, col0, np_=128, mask=None):
        rstd = rms_rstd(x_tile.ap[0:np_], D, [x_tile.b])
        S.op("dve", lambda e: e.scalar_tensor_tensor(tmp32.ap[0:np_], x_tile.ap[0:np_], rstd[0:np_], gm.ap[0:np_], ALU.mult, ALU.mult),
             reads=[x_tile.b, sm.b, gm.b], writes=[tmp32.b])
        if mask is None:
            S.op("dve", lambda e: e.tensor_tensor(ub.ap[0:np_], tmp32.ap[0:np_], sh.ap[0:np_], ALU.add), reads=[tmp32.b, sh.b], writes=[ub.b])
        else:
            S.op("dve", lambda e: e.scalar_tensor_tensor(ub.ap[0:np_], sh.ap[0:np_], mask, tmp32.ap[0:np_], ALU.mult, ALU.add),
                 reads=[tmp32.b, sh.b, mt.b], writes=[ub.b])
        transpose_to(ub, dstT3, col0, np_)

    def transpose_to(src_bf, dstT3, col0, np_=128, nk=16, kofs=0):
        for half in range(nk // 8):
            bi = 1
            pv = bank16(bi).rearrange("p (k m) -> p k m", m=128)
            S.op("pe", [tp(pv[:, k, 0:np_], src_bf.ap[0:np_, (half * 8 + k) * 128:(half * 8 + k + 1) * 128], identb[0:np_, 0:np_]) for k in range(8)],
                 reads=[src_bf.b, cB.b], writes=[PB[bi]])
            eng = "act" if half % 2 == 0 else "dve"
            dst = dstT3[:, kofs + half * 8:kofs + half * 8 + 8, col0:col0 + np_]
            if eng == "act":
                S.op("act", lambda e, dst=dst, pv=pv: e.activation(dst, pv[:, 0:8, 0:np_], AF.Copy), reads=[PB[bi]], writes=[dstT3_buf[id(dstT3)]])
            else:
                S.op("dve", lambda e, dst=dst, pv=pv: e.tensor_copy(dst, pv[:, 0:8, 0:np_]), reads=[PB[bi]], writes=[dstT3_buf[id(dstT3)]])

    dstT3_buf = {}

    def reg3(tile, k):
        v3 = tile.v("p (k n) -> p k n", k=k)
        dstT3_buf[id(v3)] = tile.b
        return v3

    def silu_to(dst, src, n, rd, wr, np_=128, tmp=None):
        t = tmp if tmp is not None else tmp32
        S.op("act", lambda e: e.activation(t.ap[0:np_, 0:n], src, AF.Exp, scale=-1.0), reads=list(rd), writes=[t.b])
        S.op("dve", lambda e: e.tensor_scalar(t.ap[0:np_, 0:n], t.ap[0:np_, 0:n], 1.0, None, ALU.add), reads=[t.b], writes=[t.b])
        S.op("dve", lambda e: e.reciprocal(t.ap[0:np_, 0:n], t.ap[0:np_, 0:n]), reads=[t.b], writes=[t.b])
        S.op("dve", lambda e: e.tensor_tensor(dst, src, t.ap[0:np_, 0:n], ALU.mult), reads=list(rd) + [t.b], writes=list(wr))

    xs_d = din("xs_d", [16, D])
    ck_d = din("ck_d", [16, 128, 256])
    cv_d = din("cv_d", [16, 128, 256])
    sconv_d = din("sconv_d", [16, 3 * 1536])
    sssm_d = din("sssm_d", [16, 1024, 128])
    convw_raw = din("convw_raw", [1, 4 * 1536])
    convb_raw = din("convb_raw", [1, 1536])
    sinkcol = din("sinkcol", [16, 1])
    sbias = din("sbias", [16, 128])
    selkv = din("selkv", [16, 4])
    dskrow = din("dskrow", [1, 1024])
    ys_out = dout("ys_out", [16, D])
    ks_out = dout("ks_out", [16, 128, 256])
    vs_out = dout("vs_out", [16, 128, 256])
    convs_out = dout("convs_out", [16, 3 * 1536])
    ssms_out = dout("ssms_out", [16, 1024, 128])
    att_d = dscr("att_d", [16, 1024])
    KSO = Buf("ks_out"); VSO = Buf("vs_out"); ATTD = Buf("att_d")
    catsT = A.alloc("catsT", 16 * 16, BF16); catsT3 = reg3(catsT, 16)
    u2sT = A.alloc("u2sT", 16 * 16, BF16); u2sT3 = reg3(u2sT, 16)
    hsT = A.alloc("hsT", NJ * 16, BF16); hsT3 = hsT.v("p (j b) -> p j b", j=NJ)
    x1s = A.alloc("x1s", D)

    def load_mod_rows(tile, ch):
        S.dma("sp", [(tile.ap[0:16, :], mod_d[0:16, ch * D:(ch + 1) * D])], reads=[MODD], writes=[tile.b])

    def phase_SM1():
        m = A.mark()
        smc[0] = 0
        load_mod_rows(gm1, 1)
        load_mod_rows(sh1, 0)
        xt = xb[0]
        S.dma("sp", [(xt.ap[0:16, :], xs_d)], writes=[xt.b])
        usT = A.alloc("usT", 16 * 16, BF16); usT3 = reg3(usT, 16)
        norm_mod_T(xt, gm1, sh1, usT3, 0, np_=16)
        pj = A.alloc("proj_s", INW)
        sel = A.alloc("sel", 16 * 128); sel3 = sel.v("p (b m) -> p b m", b=16)
        cs = A.alloc("cat_s", D)
        cs16 = A.alloc("cat_s16", D, BF16)
        gb = A.alloc("g_bc", 2048)
        xa = A.alloc("xbc_s", 1536)
        S.op("dve", lambda e: e.tensor_copy(sel3[0:16], identf[0:16, 0:16].unsqueeze(2).to_broadcast([16, 16, 128])), reads=[cF.b], writes=[sel.b])
        m1 = A.mark()
        wsl = [A.alloc("wss%d" % i, 16 * 512, BF16) for i in range(2)]
        w3s = [t.v("p (k n) -> p k n", k=16) for t in wsl]
        for pc in range(9):
            n = 512 if pc < 8 else 16
            i = pc % 2
            S.dma("pool", [(w3s[i][:, :, 0:n], w_in_v[:, :, pc * 512:pc * 512 + n])], writes=[wsl[i].b])
            bi = 4 + pc % 4
            o = ps[0:16, bi * 512:bi * 512 + n]
            S.op("pe", [mm(o, usT3[:, kc, 0:16], w3s[i][:, kc, 0:n], kc == 0, kc == 15) for kc in range(16)], reads=[usT.b, wsl[i].b], writes=[PB[bi]])
            S.op("act", lambda e, o=o, pc=pc, n=n: e.activation(pj.ap[0:16, pc * 512:pc * 512 + n], o, AF.Copy), reads=[PB[bi]], writes=[pj.b])
        P = pj.ap
        S.barrier()
        A.release(m1)
        S.dma("sp", [(ks_out[:, 0:127, :], ck_d[:, 1:128, :]), (ks_out[:, 127, :], P[0:16, 1024:1280])], reads=[pj.b], writes=[KSO])
        S.dma("sp", [(vs_out[:, 0:127, :], cv_d[:, 1:128, :]), (vs_out[:, 127, :], P[0:16, 1280:1536])], reads=[pj.b], writes=[VSO])
        S.dma("sp", [(convs_out[:, 0:3072], sconv_d[:, 1536:4608]), (convs_out[:, 3072:4608], P[0:16, 2560:4096])], reads=[pj.b])
        Ka = A.alloc("Ka", 16 * 256); Ka3 = Ka.v("p (b n) -> p b n", b=16)
        S.dma("sp", [(Ka3, ks_out.rearrange("b s n -> s b n"))], reads=[KSO], writes=[Ka.b])
        Vh = A.alloc("Vh", 16 * 256, BF16); Vh3 = Vh.v("p (b n) -> p b n", b=16)
        S.dma("pool", [(Vh3, vs_out.rearrange("b s n -> s b n"))], reads=[VSO], writes=[Vh.b])
        cst = A.alloc("scst", 128 + 8)
        S.dma("sp", [(cst.ap[0:16, 0:128], sbias), (cst.ap[0:16, 128:129], sinkcol), (cst.ap[0:16, 129:133], selkv)], writes=[cst.b])
        sk8c = cst.ap[0:16, 133:134]
        S.op("dve", lambda e: e.tensor_scalar(sk8c, cst.ap[0:16, 128:129], 8.0, None, ALU.mult), reads=[cst.b], writes=[cst.b])
        prod = A.alloc("prod", 1024)
        STt = A.alloc("STt", 16 * 16); ST3 = STt.v("p (b h) -> p b h", b=16)
        for b in range(16):
            for hf in range(2):
                S.op("pe", mm(bank(2 + hf), sel3[0:16, b, :], P[0:16, hf * 512:(hf + 1) * 512], True, True), reads=[sel.b, pj.b], writes=[PB[2 + hf]])
                S.op("dve", lambda e, b=b, hf=hf: e.tensor_tensor(prod.ap[:, hf * 512:(hf + 1) * 512].rearrange("p (k g d) -> p k g d", k=2, g=4),
                                                                 bank(2 + hf).rearrange("p (k g d) -> p k g d", k=2, g=4),
                                                                 Ka3[:, b, hf * 128:(hf + 1) * 128].rearrange("p (k d) -> p k d", k=2).unsqueeze(2).to_broadcast([128, 2, 4, 64]), ALU.mult),
                     reads=[PB[2 + hf], Ka.b], writes=[prod.b])
            S.op("dve", lambda e, b=b: e.tensor_reduce(ST3[:, b, :], prod.v("p (h d) -> p h d", h=16), AX.X, ALU.add), reads=[prod.b], writes=[STt.b])
        for b in range(16):
            S.op("pe", tp(ps[0:16, 4 * 512 + b * 128:4 * 512 + (b + 1) * 128], ST3[:, b, :], identf), reads=[STt.b, cF.b], writes=[PB[4 + b // 4]])
        tsm = A.alloc("tsm", 2048); t3 = tsm.v("p (b s) -> p b s", b=16)
        S.op("dve", lambda e: e.tensor_tensor(t3[0:16], ps[0:16, 2048:4096].rearrange("p (b s) -> p b s", b=16),
                                              cst.ap[0:16, 0:128].unsqueeze(1).to_broadcast([16, 16, 128]), ALU.add),
             reads=[PB[4], PB[5], PB[6], PB[7], cst.b], writes=[tsm.b])
        mxs = small(16); ngs = small(16); rss = small(16); dns = small(16)
        S.op("dve", lambda e: e.tensor_reduce(mxs[0:16], t3[0:16], AX.X, ALU.max), reads=[tsm.b], writes=[sm.b])
        S.op("dve", lambda e: e.tensor_scalar(mxs[0:16], mxs[0:16], sk8c, None, ALU.max), reads=[sm.b, cst.b], writes=[sm.b])
        S.op("dve", lambda e: e.tensor_scalar(ngs[0:16], mxs[0:16], -0.125, None, ALU.mult), reads=[sm.b], writes=[sm.b])
        S.op("dve", lambda e: e.scalar_tensor_tensor(t3[0:16], t3[0:16], 0.125, ngs[0:16].unsqueeze(2).to_broadcast([16, 16, 128]), ALU.mult, ALU.add),
             reads=[tsm.b, sm.b], writes=[tsm.b])
        S.op("act", lambda e: e.activation(tsm.ap[0:16], tsm.ap[0:16], AF.Exp), reads=[tsm.b], writes=[tsm.b])
        S.op("dve", lambda e: e.tensor_reduce(rss[0:16], t3[0:16], AX.X, ALU.add), reads=[tsm.b], writes=[sm.b])
        S.op("dve", lambda e: e.tensor_scalar(dns[0:16], ngs[0:16], cst.ap[0:16, 128:129], None, ALU.add), reads=[sm.b, cst.b], writes=[sm.b])
        S.op("act", lambda e: e.activation(dns[0:16], dns[0:16], AF.Exp), reads=[sm.b], writes=[sm.b])
        S.op("dve", lambda e: e.tensor_tensor(dns[0:16], dns[0:16], rss[0:16], ALU.add), reads=[sm.b], writes=[sm.b])
        S.op("dve", lambda e: e.reciprocal(dns[0:16], dns[0:16]), reads=[sm.b], writes=[sm.b])
        Pb = A.alloc("Pb", 2048, BF16); Pb3 = Pb.v("p (b s) -> p b s", b=16)
        S.op("dve", lambda e: e.tensor_tensor(Pb3[0:16], t3[0:16], dns[0:16].unsqueeze(2).to_broadcast([16, 16, 128]), ALU.mult), reads=[tsm.b, sm.b], writes=[Pb.b])
        pvb = bank16(1).rearrange("p (b h) -> p b h", h=16)
        S.op("pe", [tp(pvb[:, b, :], Pb3[0:16, b, :], identb[0:16, 0:16]) for b in range(16)], reads=[Pb.b, cB.b], writes=[PB[1]])
        PTs = A.alloc("PTs", 256, BF16); PTs3 = PTs.v("p (b h) -> p b h", b=16)
        S.op("act", lambda e: e.activation(PTs3, pvb[:, 0:16, :], AF.Copy), reads=[PB[1]], writes=[PTs.b])
        ah = A.alloc("ah", 16 * 64); ah3 = ah.v("p (b d) -> p b d", b=16)
        t4 = A.alloc("t4s", 8 * 256)
        for half in range(2):
            for bb in range(8):
                b = half * 8 + bb
                S.op("pe", mm(ps[0:16, 4 * 512 + bb * 256:4 * 512 + (bb + 1) * 256], PTs3[:, b, :], Vh3[:, b, :], True, True), reads=[PTs.b, Vh.b], writes=[PB[4 + bb // 2]])
            S.op("dve", lambda e: e.tensor_tensor(t4.ap[0:16].rearrange("p (b k d) -> p b k d", b=8, k=4),
                                                  ps[0:16, 2048:4096].rearrange("p (b k d) -> p b k d", b=8, k=4),
                                                  cst.ap[0:16, 129:133].unsqueeze(1).unsqueeze(3).to_broadcast([16, 8, 4, 64]), ALU.mult),
                 reads=[PB[4], PB[5], PB[6], PB[7], cst.b], writes=[t4.b])
            S.op("dve", lambda e, half=half: e.tensor_reduce(ah3[0:16, half * 8:(half + 1) * 8, :], t4.ap[0:16].rearrange("p (b k d) -> p b d k", b=8, k=4), AX.X, ALU.add),
                 reads=[t4.b], writes=[ah.b])
        S.dma("sp", [(att_d.rearrange("b (h d) -> h b d", h=16), ah3[0:16])], reads=[ah.b], writes=[ATTD])
        S.dma("sp", [(cs.ap[0:16, 0:1024], att_d)], reads=[ATTD], writes=[cs.b])
        S.barrier()
        A.release(m1)
        S.dma("sp", [(gb.ap[0:16, 0:1024], gatt.to_broadcast([16, 1024])), (gb.ap[0:16, 1024:2048], gssm.to_broadcast([16, 1024]))], writes=[gb.b])
        rstd = rms_rstd(cs.ap[0:16, 0:1024], 1024, [cs.b])
        S.op("dve", lambda e: e.scalar_tensor_tensor(cs16.ap[0:16, 0:1024], cs.ap[0:16, 0:1024], rstd[0:16], gb.ap[0:16, 0:1024], ALU.mult, ALU.mult),
             reads=[cs.b, sm.b, gb.b], writes=[cs16.b])
        cwb = A.alloc("cwb", 5 * 1536)
        S.dma("sp", [(cwb.ap[0:16, 0:6144], convw_raw.to_broadcast([16, 6144])), (cwb.ap[0:16, 6144:7680], convb_raw.to_broadcast([16, 1536]))], writes=[cwb.b])
        sc_ = A.alloc("sconv", 3 * 1536)
        S.dma("sp", [(sc_.ap[0:16, :], sconv_d)], writes=[sc_.b])
        xc = A.alloc("xc", 1536); xc2 = A.alloc("xc2", 1536)
        S.op("dve", lambda e: e.tensor_tensor(xc.ap[0:16], P[0:16, 2560:4096], cwb.ap[0:16, 3 * 1536:4 * 1536], ALU.mult), reads=[pj.b, cwb.b], writes=[xc.b])
        S.op("dve", lambda e: e.tensor_tensor(xc.ap[0:16], xc.ap[0:16], cwb.ap[0:16, 6144:7680], ALU.add), reads=[xc.b, cwb.b], writes=[xc.b])
        for i in range(3):
            S.op("dve", lambda e, i=i: e.tensor_tensor(xc2.ap[0:16], sc_.ap[0:16, i * 1536:(i + 1) * 1536], cwb.ap[0:16, i * 1536:(i + 1) * 1536], ALU.mult), reads=[sc_.b, cwb.b], writes=[xc2.b])
            S.op("dve", lambda e: e.tensor_tensor(xc.ap[0:16], xc.ap[0:16], xc2.ap[0:16], ALU.add), reads=[xc.b, xc2.b], writes=[xc.b])
        silu_to(xa.ap[0:16], xc.ap[0:16], 1536, [xc.b], [xa.b], np_=16, tmp=xc2)
        S.barrier()
        A.release(m1)
        tz = A.alloc("tmpz", 1536)
        x0 = small(16); ax = small(16); dts = small(16); dAs = small(16)
        S.op("dve", lambda e: e.tensor_tensor(x0[0:16], P[0:16, 4096:4112], dtb_bc[0:16], ALU.add), reads=[pj.b, hv.b], writes=[sm.b])
        S.op("act", lambda e: e.activation(ax[0:16], x0[0:16], AF.Abs), reads=[sm.b], writes=[sm.b])
        S.op("act", lambda e: e.activation(ax[0:16], ax[0:16], AF.Exp, scale=-1.0), reads=[sm.b], writes=[sm.b])
        S.op("dve", lambda e: e.tensor_scalar(ax[0:16], ax[0:16], 1.0, None, ALU.add), reads=[sm.b], writes=[sm.b])
        S.op("act", lambda e: e.activation(ax[0:16], ax[0:16], AF.Ln), reads=[sm.b], writes=[sm.b])
        S.op("dve", lambda e: e.scalar_tensor_tensor(dts[0:16], x0[0:16], 0.0, ax[0:16], ALU.max, ALU.add), reads=[sm.b], writes=[sm.b])
        S.op("dve", lambda e: e.tensor_tensor(dAs[0:16], dts[0:16], a_bc.ap[0:16], ALU.mult), reads=[sm.b, a_bc.b], writes=[sm.b])
        S.op("act", lambda e: e.activation(dAs[0:16], dAs[0:16], AF.Exp), reads=[sm.b], writes=[sm.b])
        XE = A.alloc("XE", 2048)
        S.op("dve", lambda e: e.tensor_tensor(XE.ap[0:16, 0:1024].rearrange("p (h d) -> p h d", h=16), xa.ap[0:16, 0:1024].rearrange("p (h d) -> p h d", h=16),
                                              dts[0:16].unsqueeze(2).to_broadcast([16, 16, 64]), ALU.mult), reads=[xa.b, sm.b], writes=[XE.b])
        S.op("dve", lambda e: e.tensor_copy(XE.ap[0:16, 1024:2048].rearrange("p (h d) -> p h d", h=16), dAs[0:16].unsqueeze(2).to_broadcast([16, 16, 64])),
             reads=[sm.b], writes=[XE.b])
        XT = A.alloc("XT", 16 * 16); XT3 = XT.v("p (j b) -> p j b", j=16)
        S.op("pe", [tp(bank(0, 16, j * 16), XE.ap[0:16, j * 128:(j + 1) * 128], identf[0:16, 0:16]) for j in range(16)], reads=[XE.b, cF.b], writes=[PB[0]])
        S.op("act", lambda e: e.activation(XT.ap, bank(0, 256), AF.Copy), reads=[PB[0]], writes=[XT.b])
        yT = A.alloc("yT", 8 * 16); yT3 = yT.v("p (j b) -> p j b", j=8)
        h0 = [A.alloc("h0_%d" % i, 1024) for i in range(2)]
        h1 = [A.alloc("h1_%d" % i, 1024) for i in range(2)]
        for b in range(16):
            ht = h0[b % 2]; hn = h1[b % 2]
            S.dma("sp", [(ht.v("p (j n) -> p j n", j=8), sssm_d[b].rearrange("(j q) n -> q j n", q=128))], writes=[ht.b])
            S.op("pe", mm(bank(3), sel3[0:16, b, :], xa.ap[0:16, 1024:1536], True, True), reads=[sel.b, xa.b], writes=[PB[3]])
            S.op("dve", lambda e, ht=ht, b=b: e.tensor_tensor(ht.v("p (j n) -> p j n", j=8), ht.v("p (j n) -> p j n", j=8),
                                                             XT3[:, 8:16, b].unsqueeze(2).to_broadcast([128, 8, 128]), ALU.mult), reads=[ht.b, XT.b], writes=[ht.b])
            S.op("dve", lambda e, hn=hn, b=b: e.tensor_tensor(hn.v("p (g r n) -> p g r n", g=2, r=4),
                                                             bank(3, 256).rearrange("p (g n) -> p g n", g=2).unsqueeze(2).to_broadcast([128, 2, 4, 128]),
                                                             XT3[:, 0:8, b].rearrange("p (g r) -> p g r", g=2).unsqueeze(3).to_broadcast([128, 2, 4, 128]), ALU.mult),
                 reads=[PB[3], XT.b], writes=[hn.b])
            S.op("dve", lambda e, hn=hn, ht=ht: e.tensor_tensor(hn.ap, hn.ap, ht.ap, ALU.add), reads=[hn.b, ht.b], writes=[hn.b])
            S.dma("sp", [(ssms_out[b].rearrange("(j q) n -> q j n", q=128), hn.v("p (j n) -> p j n", j=8))], reads=[hn.b])
            S.op("dve", lambda e, hn=hn, ht=ht: e.tensor_tensor(ht.v("p (g r n) -> p g r n", g=2, r=4), hn.v("p (g r n) -> p g r n", g=2, r=4),
                                                               bank(3, 256, 256).rearrange("p (g n) -> p g n", g=2).unsqueeze(2).to_broadcast([128, 2, 4, 128]), ALU.mult),
                 reads=[hn.b, PB[3]], writes=[ht.b])
            S.op("dve", lambda e, ht=ht, b=b: e.tensor_reduce(yT3[:, :, b], ht.v("p (j n) -> p j n", j=8), AX.X, ALU.add), reads=[ht.b], writes=[yT.b])
        S.op("pe", [tp(ps[0:16, 2 * 512 + j * 128:2 * 512 + (j + 1) * 128], yT3[:, j, :], identf) for j in range(8)], reads=[yT.b, cF.b], writes=[PB[2], PB[3]])
        ys = A.alloc("y_s", 1024)
        dkb = A.alloc("dkb", 1024)
        S.dma("sp", [(dkb.ap[0:16, :], dskrow.to_broadcast([16, 1024]))], writes=[dkb.b])
        S.op("dve", lambda e: e.tensor_tensor(ys.ap[0:16], xa.ap[0:16, 0:1024], dkb.ap[0:16], ALU.mult), reads=[xa.b, dkb.b], writes=[ys.b])
        S.op("dve", lambda e: e.tensor_tensor(ys.ap[0:16], ys.ap[0:16], ps[0:16, 1024:2048], ALU.add), reads=[ys.b, PB[2], PB[3]], writes=[ys.b])
        zt = A.alloc("z_s", 1024)
        silu_to(zt.ap[0:16], P[0:16, 1536:2560], 1024, [pj.b], [zt.b], np_=16, tmp=tz)
        S.op("dve", lambda e: e.tensor_tensor(ys.ap[0:16], ys.ap[0:16], zt.ap[0:16], ALU.mult), reads=[ys.b, zt.b], writes=[ys.b])
        rstd2 = rms_rstd(ys.ap[0:16], 1024, [ys.b])
        S.op("dve", lambda e: e.scalar_tensor_tensor(cs16.ap[0:16, 1024:2048], ys.ap[0:16], rstd2[0:16], gb.ap[0:16, 1024:2048], ALU.mult, ALU.mult),
             reads=[ys.b, sm.b, gb.b], writes=[cs16.b])
        transpose_to(cs16, catsT3, 0, np_=16)
        S.barrier()
        A.release(m)
    m_gm = A.mark()
    gm1 = A.alloc("gm1", D)
    sh1 = A.alloc("sh1", D)
    load_mod_bc(gm1, 1)
    load_mod_bc(sh1, 0)
    mS = A.mark()
    u2T_d = dscr("u2T_d", [16, 128, NOWN], BF16)
    CATD = Buf("cat_d"); X1D = Buf("x1_d"); U2TD = Buf("u2T_d"); HTD = Buf("hT_d")
    w_dt = A.alloc("w_dt", 16 * 16, BF16)
    w_dt3 = w_dt.v("p (k n) -> p k n", k=16)
    S.dma("pool", [(w_dt3, w_in_v[:, :, 4096:4112])], writes=[w_dt.b])
    A_mrow = [A.alloc("mrow_t", 512)]
    A_xp = [A.alloc("xp%d" % i, 515) for i in range(2)]
    A_acc = [A.alloc("acc%d" % i, 512) for i in range(2)]
    A_st = [A.alloc("silt", 512)]
    A_pt = [A.alloc("ptmp", 512)]
    A_xst = [A.alloc("xs_tok", 1024)]
    A_bt = [A.alloc("Btok", 256, BF16)]
    A_xd = [A.alloc("Xd", 1024, BF16)]

    def conv_tiles(uT3, uTb, T, col_tok0, dests, cts, wfn, load_mask=True):
        for ct in cts:
            bi = 4 + ct % 4
            o = bank(bi, T)
            wb = wfn(ct, 0)[1]
            S.op("pe", [mm(o, wfn(ct, kc)[0], uT3[:, kc, 0:T], kc == 0, kc == 15) for kc in range(16)],
                 reads=[wb, uTb], writes=[PB[bi]])
            xp = A_xp[ct % 2]
            S.op("act", lambda e, xp=xp, ct=ct: e.activation(xp.ap[:, 0:3], halo3[:, ct, :], AF.Copy), reads=[halo.b], writes=[xp.b])
            S.op("act", lambda e, xp=xp, o=o: e.activation(xp.ap[:, 3:3 + T], o, AF.Copy), reads=[PB[bi]], writes=[xp.b])
            S.op("act", lambda e, xp=xp, ct=ct: e.activation(halo3[:, ct, :], xp.ap[:, T:T + 3], AF.Copy), reads=[xp.b], writes=[halo.b])
            acc = A_acc[ct % 2]
            S.op("pool", lambda e, xp=xp, acc=acc, ct=ct: e.tensor_scalar(acc.ap[:, 0:T], xp.ap[:, 3:3 + T], cw.ap[:, ct * 4 + 3:ct * 4 + 4], cbi.ap[:, ct:ct + 1], ALU.mult, ALU.add),
                 reads=[xp.b, cw.b, cbi.b], writes=[acc.b])
            for i in (2, 1, 0):
                S.op("pool", lambda e, xp=xp, ct=ct, i=i: e.tensor_scalar(A_pt[0].ap[:, 0:T], xp.ap[:, i:i + T], cw.ap[:, ct * 4 + i:ct * 4 + i + 1], None, ALU.mult),
                     reads=[xp.b, cw.b], writes=[A_pt[0].b])
                S.op("pool", lambda e, acc=acc: e.tensor_tensor(acc.ap[:, 0:T], acc.ap[:, 0:T], A_pt[0].ap[:, 0:T], ALU.add),
                     reads=[acc.b, A_pt[0].b], writes=[acc.b])
            dst, db = dests(ct)
            silu_to(dst, acc.ap[:, 0:T], T, [acc.b], [db], tmp=A_st[0])

    def dt_block(uT3, uTb, c0, blk_ext, own=False):
        smc[0] = 0
        o = bank(0, 16)
        S.op("pe", [mm(o, uT3[:, kc, c0:c0 + 128], w_dt3[:, kc, :], kc == 0, kc == 15) for kc in range(16)],
             reads=[uTb, w_dt.b], writes=[PB[0]])
        x0 = small(16); ax = small(16); dt = small(16); dA = small(16)
        S.op("dve", lambda e: e.tensor_tensor(x0, o, dtb_bc, ALU.add), reads=[PB[0], hv.b], writes=[sm.b])
        S.op("act", lambda e: e.activation(ax, x0, AF.Abs), reads=[sm.b], writes=[sm.b])
        S.op("act", lambda e: e.activation(ax, ax, AF.Exp, scale=-1.0), reads=[sm.b], writes=[sm.b])
        S.op("dve", lambda e: e.tensor_scalar(ax, ax, 1.0, None, ALU.add), reads=[sm.b], writes=[sm.b])
        S.op("act", lambda e: e.activation(ax, ax, AF.Ln), reads=[sm.b], writes=[sm.b])
        S.op("dve", lambda e: e.scalar_tensor_tensor(dt, x0, 0.0, ax, ALU.max, ALU.add), reads=[sm.b], writes=[sm.b])
        S.op("dve", lambda e: e.tensor_scalar(dt, dt, mt.ap[:, blk_ext:blk_ext + 1], None, ALU.mult), reads=[sm.b, mt.b], writes=[sm.b])
        S.op("dve", lambda e: e.tensor_tensor(dA, dt, a_bc.ap, ALU.mult), reads=[sm.b, a_bc.b], writes=[sm.b])
        o2 = bank(0, 32, 32)
        S.op("pe", [mm(o2[:, 0:16], Umat, dA, True, True), mm(o2[:, 16:32], onesf, dA, True, True)], reads=[cF.b, sm.b], writes=[PB[0]])
        at = small(32)
        S.op("act", lambda e: e.activation(at, o2, AF.Copy), reads=[PB[0]], writes=[sm.b])
        acum, tot = at[:, 0:16], at[:, 16:32]
        de = small(16); cd = small(16); w1 = small(16)
        S.op("dve", lambda e: e.tensor_tensor(de, tot, acum, ALU.subtract), reads=[sm.b], writes=[sm.b])
        S.op("act", lambda e: e.activation(de, de, AF.Exp), reads=[sm.b], writes=[sm.b])
        S.op("act", lambda e: e.activation(cd, tot, AF.Exp), reads=[sm.b], writes=[sm.b])
        S.op("dve", lambda e: e.tensor_tensor(w1, dt, de, ALU.mult), reads=[sm.b], writes=[sm.b])
        r = dict(dt=dt, dA=dA, acum=acum, tot=tot, cd=cd, w1=w1)
        if own:
            ea = small(16)
            S.op("act", lambda e: e.activation(ea, acum, AF.Exp), reads=[sm.b], writes=[sm.b])
            r["ea"] = ea
        return r

    def state_part1(xsT3, xsTb, BT3, BTb, c0):
        xt_ = A_xst[0]
        S.op("pe", [tp(bank(2 + i // 4, 128, (i % 4) * 128), xsT3[:, i, c0:c0 + 128], identf) for i in range(8)],
             reads=[xsTb, cF.b], writes=[PB[2], PB[3]])
        S.op("act", lambda e: e.activation(xt_.ap[:, 0:512], bank(2), AF.Copy), reads=[PB[2]], writes=[xt_.b])
        S.op("act", lambda e: e.activation(xt_.ap[:, 512:1024], bank(3), AF.Copy), reads=[PB[3]], writes=[xt_.b])
        S.op("pe", [tp(bank(0, 128, 128 + g * 128), BT3[:, g, c0:c0 + 128], identf) for g in range(2)], reads=[BTb, cF.b], writes=[PB[0]])
        Bt = A_bt[0]
        S.op("act", lambda e: e.activation(Bt.ap, bank(0, 256, 128), AF.Copy), reads=[PB[0]], writes=[Bt.b])
        return xt_, Bt

    def state_part2(xt_, Bt, sc_, keep_hb):
        Xd = A_xd[0]
        S.op("dve", lambda e: e.tensor_tensor(Xd.v("p (h d) -> p h d", h=16), xt_.v("p (h d) -> p h d", h=16),
                                              sc_["w1"].unsqueeze(2).to_broadcast([128, 16, 64]), ALU.mult),
             reads=[xt_.b, sm.b], writes=[Xd.b])
        S.op("pe", [mm(bank(6 + g), Bt.ap[:, g * 128:(g + 1) * 128], Xd.ap[:, g * 512:(g + 1) * 512], True, True) for g in range(2)],
             reads=[Bt.b, Xd.b], writes=[PB[6], PB[7]])
        S.op("dve", lambda e: e.tensor_tensor(hst.v("p (h d) -> p h d", h=16), hst.v("p (h d) -> p h d", h=16),
                                              sc_["cd"].unsqueeze(2).to_broadcast([128, 16, 64]), ALU.mult),
             reads=[hst.b, sm.b], writes=[hst.b])
        S.op("dve", lambda e: e.tensor_tensor(hst.ap[:, 0:512], hst.ap[:, 0:512], bank(6), ALU.add), reads=[hst.b, PB[6]], writes=[hst.b])
        S.op("dve", lambda e: e.tensor_tensor(hst.ap[:, 512:1024], hst.ap[:, 512:1024], bank(7), ALU.add), reads=[hst.b, PB[7]], writes=[hst.b])
        if keep_hb:
            S.op("act", lambda e: e.activation(hb.ap, hst.ap, AF.Copy), reads=[hst.b], writes=[hb.b])

    def load_x_norm(tok0, nblk, uT3, masked=False):
        for b in range(nblk):
            xt = xb[b % 2]
            S.dma("sp", [(xt.ap, xext[tok0 + b * 128:tok0 + (b + 1) * 128, :])], writes=[xt.b])
            blk = tok0 // 128 + b
            norm_mod_T(xt, gm1, sh1, uT3, b * 128, mask=(mt.ap[:, blk:blk + 1] if masked else None))

    mA = A.mark()
    wx = A.alloc("wx", 16 * 1536, BF16)
    wx3 = wx.v("p (k n) -> p k n", k=16)
    S.dma("pool", [(wx3[:, 0:8, :], w_in_v[:, 0:8, 2560:4096]), (wx3[:, 8:16, :], w_in_v[:, 8:16, 2560:4096])], writes=[wx.b])
    uTa = A.alloc("uTa", 16 * 512, BF16)
    uTa3 = reg3(uTa, 16)
    xsTa = A.alloc("xsTa", 8 * 512)
    xsTa3 = xsTa.v("p (k n) -> p k n", k=8)
    BTa = A.alloc("BTa", 2 * 512)
    BTa3 = BTa.v("p (k n) -> p k n", k=2)

    CTd = A.alloc("CTdummy", 2 * 512)
    CTd3 = CTd.v("p (k n) -> p k n", k=2)

    def destsA(ct):
        if ct < 8:
            return xsTa3[:, ct, 0:512], xsTa.b
        if ct < 10:
            return BTa3[:, ct - 8, 0:512], BTa.b
        return CTd3[:, ct - 10, 0:512], CTd.b

    for g in range(NPRE_G - n_pre_groups, NPRE_G):
        load_x_norm(g * 512, 4, uTa3, masked=True)
        conv_tiles(uTa3, uTa.b, 512, g * 512, destsA, range(12 if g == NPRE_G - 1 else 10), lambda ct, kc: (wx3[:, kc, ct * 128:(ct + 1) * 128], wx.b))
        for b in range(4):
            sc_ = dt_block(uTa3, uTa.b, b * 128, g * 4 + b)
            xt_, Bt = state_part1(xsTa3, xsTa.b, BTa3, BTa.b, b * 128)
            state_part2(xt_, Bt, sc_, keep_hb=(g == NPRE_G - 1 and b == 3))
    S.barrier()
    A.release(mA)

    OWN0 = NEXT - NOWN
    def phase_B1b():
        m = A.mark()
        slots = [A.alloc("ws%d" % i, 16 * 512, BF16) for i in range(2)]
        s3 = [t.v("p (k n) -> p k n", k=16) for t in slots]
        uT = A.alloc("uTb", 16 * 256, BF16); uT3 = reg3(uT, 16)
        xsT = A.alloc("xsTb", 8 * 256); xsT3 = xsT.v("p (k n) -> p k n", k=8)
        BT = A.alloc("BTb", 2 * 256); BT3 = BT.v("p (k n) -> p k n", k=2)
        BTh = A.alloc("BTh", 2 * 256, BF16); BTh3 = BTh.v("p (k n) -> p k n", k=2)
        CTh = A.alloc("CTh", 2 * 256, BF16); CTh3 = CTh.v("p (k n) -> p k n", k=2)
        zs = A.alloc("zs", 2 * 1024); zs3 = zs.v("p (b n) -> p b n", b=2)
        gs = A.alloc("gssm", 1024)
        S.dma("sp", [(gs.ap, gssm.to_broadcast([128, 1024]))], writes=[gs.b])
        Rt = A.alloc("Rt", 2048); R3 = Rt.v("p (h l) -> p h l", h=16)
        Et = A.alloc("Et", 2048, BF16)
        MTt = A.alloc("MTt", 2048, BF16); MT3 = MTt.v("p (h l) -> p h l", h=16)
        CBm = A.alloc("CBm", 256)
        Xb = A.alloc("Xb", 1024, BF16)
        yt = A.alloc("yt", 1024)
        y2 = A.alloc("y2", 1024)
        sso = A.alloc("sso", 1024, BF16)
        si = [0]

        def next_slot(src_cols, ncols=512):
            i = si[0] % 2
            si[0] += 1
            S.dma("pool", [(s3[i][:, :, 0:ncols], w_in_v[:, :, src_cols:src_cols + ncols])], writes=[slots[i].b])
            return s3[i], slots[i].b

        def dests(ct):
            if ct < 8:
                return xsT3[:, ct, 0:256], xsT.b
            if ct < 10:
                return BT3[:, ct - 8, 0:256], BT.b
            return CTh3[:, ct - 10, 0:256], CTh.b

        for og in range(NOWN // 256):
            tok0 = OWN0 + og * 256
            load_x_norm(tok0, 2, uT3)
            for pc in range(3):
                w3, wb = next_slot(2560 + pc * 512)
                conv_tiles(uT3, uT.b, 256, tok0, dests, range(pc * 4, pc * 4 + 4),
                           lambda ct, kc, w3=w3, wb=wb, pc=pc: (w3[:, kc, (ct - pc * 4) * 128:(ct - pc * 4 + 1) * 128], wb), load_mask=(pc == 0))
            S.op("act", lambda e: e.activation(BTh.ap, BT.ap, AF.Copy), reads=[BT.b], writes=[BTh.b])
            for pc in range(2):
                w3, wb = next_slot(1536 + pc * 512)
                for b in range(2):
                    bi = 4 + (pc * 2 + b) % 4
                    S.op("pe", [mm(bank(bi), uT3[:, kc, b * 128:(b + 1) * 128], w3[:, kc, :], kc == 0, kc == 15) for kc in range(16)],
                         reads=[uT.b, wb], writes=[PB[bi]])
                    silu_to(zs3[:, b, pc * 512:(pc + 1) * 512], bank(bi), 512, [PB[bi]], [zs.b], tmp=A_st[0])
            for b in range(2):
                c0 = b * 128
                sc_ = dt_block(uT3, uT.b, c0, (tok0 // 128) + b, own=True)
                xt_, Bt = state_part1(xsT3, xsT.b, BT3, BT.b, c0)
                S.op("dve", lambda e: e.tensor_tensor(R3, Umat.unsqueeze(1).to_broadcast([128, 16, 128]),
                                                      sc_["dA"].unsqueeze(2).to_broadcast([128, 16, 128]), ALU.mult),
                     reads=[cF.b, sm.b], writes=[Rt.b])
                for q in range(4):
                    S.op("pe", mm(bank(4 + q), UTmat, Rt.ap[:, q * 512:(q + 1) * 512], True, True), reads=[cF.b, Rt.b], writes=[PB[4 + q]])
                    S.op("act", lambda e, q=q: e.activation(Et.ap[:, q * 512:(q + 1) * 512], bank(4 + q), AF.Exp), reads=[PB[4 + q]], writes=[Et.b])
                S.op("pe", [mm(bank(0, 128, 256 + g * 128), BTh3[:, g, c0:c0 + 128], CTh3[:, g, c0:c0 + 128], True, True) for g in range(2)],
                     reads=[BTh.b, CTh.b], writes=[PB[0]])
                S.op("dve", lambda e: e.tensor_tensor(CBm.v("p (g l) -> p g l", g=2), bank(0, 256, 256).rearrange("p (g l) -> p g l", g=2),
                                                      caus01.unsqueeze(1).to_broadcast([128, 2, 128]), ALU.mult),
                     reads=[PB[0], cF.b], writes=[CBm.b])
                S.op("dve", lambda e: e.tensor_tensor(MTt.v("p (g r l) -> p g r l", g=2, r=8), Et.v("p (g r l) -> p g r l", g=2, r=8),
                                                      CBm.v("p (g l) -> p g l", g=2).unsqueeze(2).to_broadcast([128, 2, 8, 128]), ALU.mult),
                     reads=[Et.b, CBm.b], writes=[MTt.b])
                S.op("dve", lambda e: e.tensor_tensor(Xb.v("p (h d) -> p h d", h=16), xt_.v("p (h d) -> p h d", h=16),
                                                      sc_["dt"].unsqueeze(2).to_broadcast([128, 16, 64]), ALU.mult),
                     reads=[xt_.b, sm.b], writes=[Xb.b])
                S.op("pe", [mm(bank(2 + h // 8, 64, (h % 8) * 64), MT3[:, h, :], Xb.ap[:, h * 64:(h + 1) * 64], True, True) for h in range(16)],
                     reads=[MTt.b, Xb.b], writes=[PB[2], PB[3]])
                S.op("pe", [mm(bank(4 + g), CTh3[:, g, c0:c0 + 128], hb.ap[:, g * 512:(g + 1) * 512], True, True) for g in range(2)],
                     reads=[CTh.b, hb.b], writes=[PB[4], PB[5]])
                for g in range(2):
                    S.op("dve", lambda e, g=g: e.tensor_tensor(yt.ap[:, g * 512:(g + 1) * 512].rearrange("p (h d) -> p h d", h=8),
                                                               bank(4 + g).rearrange("p (h d) -> p h d", h=8),
                                                               sc_["ea"][:, g * 8:(g + 1) * 8].unsqueeze(2).to_broadcast([128, 8, 64]), ALU.mult),
                         reads=[PB[4 + g], sm.b], writes=[yt.b])
                    S.op("dve", lambda e, g=g: e.tensor_tensor(yt.ap[:, g * 512:(g + 1) * 512], yt.ap[:, g * 512:(g + 1) * 512], bank(2 + g), ALU.add),
                         reads=[PB[2 + g], yt.b], writes=[yt.b])
                S.op("dve", lambda e: e.tensor_tensor(y2.v("p (h d) -> p h d", h=16), xt_.v("p (h d) -> p h d", h=16),
                                                      dsk_bc.unsqueeze(2).to_broadcast([128, 16, 64]), ALU.mult),
                     reads=[xt_.b, hv.b], writes=[y2.b])
                S.op("dve", lambda e: e.tensor_tensor(yt.ap, yt.ap, y2.ap, ALU.add), reads=[yt.b, y2.b], writes=[yt.b])
                S.op("dve", lambda e, b=b: e.tensor_tensor(yt.ap, yt.ap, zs3[:, b, :], ALU.mult), reads=[yt.b, zs.b], writes=[yt.b])
                rstd = rms_rstd(yt.ap, 1024, [yt.b])
                S.op("dve", lambda e, rstd=rstd: e.scalar_tensor_tensor(sso.ap, yt.ap, rstd, gs.ap, ALU.mult, ALU.mult), reads=[yt.b, sm.b, gs.b], writes=[sso.b])
                tk = og * 256 + c0
                S.dma("sp", [(cat_d[tk:tk + 128, 1024:2048], sso.ap)], reads=[sso.b], writes=[CATD])
                state_part2(xt_, Bt, sc_, keep_hb=True)
        so = yt
        S.op("pe", [tp(bank(2 + i // 4, 128, (i % 4) * 128), hst.ap[:, i * 128:(i + 1) * 128], identf) for i in range(8)],
             reads=[hst.b, cF.b], writes=[PB[2], PB[3]])
        S.op("act", lambda e: e.activation(so.ap[:, 0:512], bank(2), AF.Copy), reads=[PB[2]], writes=[so.b])
        S.op("act", lambda e: e.activation(so.ap[:, 512:1024], bank(3), AF.Copy), reads=[PB[3]], writes=[so.b])
        S.dma("sp", [(ssm_out.rearrange("(i p) n -> p i n", p=128), so.v("p (i n) -> p i n", i=8))], reads=[so.b])
        co = Rt
        S.op("pe", [tp(ps[0:3, 4 * 512 + ct * 128:4 * 512 + (ct + 1) * 128], halo3[:, ct, :], identf) for ct in range(12)],
             reads=[halo.b, cF.b], writes=[PB[4], PB[5], PB[6]])
        S.op("act", lambda e: e.activation(co.ap[0:3, 0:1536], ps[0:3, 4 * 512:4 * 512 + 1536], AF.Copy), reads=[PB[4], PB[5], PB[6]], writes=[co.b])
        S.dma("sp", [(conv_out, co.ap[0:3, 0:1536])], reads=[co.b])
        S.barrier()
        A.release(m)

    phase_B1b()
    A.release(mS)

    def phase_B1a():
        m = A.mark()
        slots = [A.alloc("wq%d" % i, 16 * 512, BF16) for i in range(2)]
        s3 = [t.v("p (k n) -> p k n", k=16) for t in slots]
        uT = A.alloc("uTq", 16 * 512, BF16); uT3 = reg3(uT, 16)
        uTh = A.alloc("uTh", 16 * 128, BF16); uTh3 = reg3(uTh, 16)
        qT = A.alloc("qT", 8 * 512, BF16); qT3 = qT.v("p (k n) -> p k n", k=8)
        kT = A.alloc("kT", 2 * 4 * 640, BF16); kT4 = kT.v("p (e k n) -> p e k n", e=2, k=4)
        Vt = A.alloc("Vt", 5 * 256, BF16); V3 = Vt.v("p (s n) -> p s n", s=5)
        ab1t = A.alloc("ab", 4096)
        S.dma("sp", [(ab1t.ap, abias0)], writes=[ab1t.b])
        ga = A.alloc("gatt", 1024)
        S.dma("sp", [(ga.ap, gatt.to_broadcast([128, 1024]))], writes=[ga.b])
        sk8 = A.alloc("sk8", 16)
        S.op("dve", lambda e: e.tensor_scalar(sk8.ap, sink_bc, 8.0, None, ALU.mult), reads=[hv.b], writes=[sk8.b])
        tS = A.alloc("tS", 1024)
        Pt = A.alloc("Pt", 1024, BF16)
        PT = A.alloc("PT", 1024, BF16)
        at_ = A.alloc("att", 1024)
        ao = A.alloc("atto", 1024, BF16)
        kvo = A.alloc("kvo", 512)
        si = [0]

        def kv_load():
            j = 0
            S.dma("pool", [(s3[j], w_in_v[:, :, 1024:1536])], writes=[slots[j].b])
            return (s3[j], slots[j].b)

        def k_dup(wv, eo):
            i = 1
            d4 = slots[i].v("p (k v n) -> p k v n", k=16, v=4)
            src4 = wv[0][:, :, 0:256].rearrange("p k (v n) -> p k v n", v=4)
            S.op("dve", lambda e: e.memset(slots[i].ap, 0.0), writes=[slots[i].b])
            for kc in range(16):
                if kc % 2 == 0:
                    S.op("act", lambda e, kc=kc: e.activation(d4[:, kc, :, eo * 64:eo * 64 + 64], src4[:, kc, :, :], AF.Copy), reads=[wv[1]], writes=[slots[i].b])
                else:
                    S.op("dve", lambda e, kc=kc: e.tensor_copy(d4[:, kc, :, eo * 64:eo * 64 + 64], src4[:, kc, :, :]), reads=[wv[1]], writes=[slots[i].b])
            return (s3[i], slots[i].b)

        def k_proj(wk, eo, u3, ub_, T, col0):
            for kv in range(4):
                bi = 4 + kv
                S.op("pe", [mm(bank(bi, T), wk[0][:, kc, kv * 128:(kv + 1) * 128], u3[:, kc, 0:T], kc == 0, kc == 15) for kc in range(16)],
                     reads=[wk[1], ub_], writes=[PB[bi]])
                S.op("dve" if kv % 2 else "act",
                     (lambda e, kv=kv, bi=bi: e.tensor_copy(kT4[:, eo, kv, col0:col0 + T], bank(bi, T))) if kv % 2 else
                     (lambda e, kv=kv, bi=bi: e.activation(kT4[:, eo, kv, col0:col0 + T], bank(bi, T), AF.Copy)), reads=[PB[bi]], writes=[kT.b])

        vcnt = [0]

        def v_proj(wv, u3, ub_, c0, slot, keep32=None):
            bi = 2 + vcnt[0] % 2
            vcnt[0] += 1
            S.op("pe", [mm(bank(bi, 256), u3[:, kc, c0:c0 + 128], wv[0][:, kc, 256:512], kc == 0, kc == 15) for kc in range(16)],
                 reads=[wv[1], ub_], writes=[PB[bi]])
            S.op("dve", lambda e: e.tensor_copy(V3[:, slot, :], bank(bi, 256)), reads=[PB[bi]], writes=[Vt.b])
            if keep32 is not None:
                S.op("dve", lambda e: e.tensor_copy(keep32, bank(bi, 256)), reads=[PB[bi]], writes=[kvo.b])

        import os as _os
        NSTEP = int(_os.environ.get("B1A_STEPS", "99"))
        for og in range(4):
            tok0 = OWN0 + og * 512
            if NSTEP < 2: break
            if og == 0:
                xt = xb[0]
                S.dma("sp", [(xt.ap, xext[OWN0 - 128:OWN0, :])], writes=[xt.b])
                norm_mod_T(xt, gm1, sh1, uTh3, 0)
            if NSTEP < 3: break
            load_x_norm(tok0, 4, uT3)
            if NSTEP < 4: break
            wv = kv_load()
            for eo in range(2):
                wk = k_dup(wv, eo)
                if og == 0:
                    k_proj(wk, eo, uTh3, uTh.b, 128, 0)
                k_proj(wk, eo, uT3, uT.b, 512, 128)
            if og == 0:
                v_proj(wv, uTh3, uTh.b, 0, 0)
            if NSTEP < 7: break
            for b in range(int(_os.environ.get("B1A_VN", "4"))):
                v_proj(wv, uT3, uT.b, b * 128, 1 + b, keep32=(kvo.ap[:, 256:512] if (og == 3 and b == 3) else None))
            if og == 3:
                S.op("pe", [mm(bank(3, 256), uT3[:, kc, 384:512], wv[0][:, kc, 0:256], kc == 0, kc == 15) for kc in range(16)],
                     reads=[wv[1], uT.b], writes=[PB[3]])
                S.op("dve", lambda e: e.tensor_copy(kvo.ap[:, 0:256], bank(3, 256)), reads=[PB[3]], writes=[kvo.b])
                S.dma("sp", [(k_out, kvo.ap[:, 0:256]), (v_out, kvo.ap[:, 256:512])], reads=[kvo.b])
            if NSTEP < 8: break
            for pc in range(2):
                i = 1 - pc
                S.dma("pool", [(s3[i], w_in_v[:, :, pc * 512:(pc + 1) * 512])], writes=[slots[i].b])
                for t4 in range(4):
                    bi = 4 + t4
                    S.op("pe", [mm(bank(bi), s3[i][:, kc, t4 * 128:(t4 + 1) * 128], uT3[:, kc, :], kc == 0, kc == 15) for kc in range(16)],
                         reads=[slots[i].b, uT.b], writes=[PB[bi]])
                    S.op("act" if t4 % 2 == 0 else "dve",
                         (lambda e, t4=t4, bi=bi, pc=pc: e.activation(qT3[:, pc * 4 + t4, :], bank(bi), AF.Copy)) if t4 % 2 == 0 else
                         (lambda e, t4=t4, bi=bi, pc=pc: e.tensor_copy(qT3[:, pc * 4 + t4, :], bank(bi))),
                         reads=[PB[bi]], writes=[qT.b])
            ATT = int(_os.environ.get("ATT_STEPS", "99"))
            for b in range(4):
                if stop_after == "B1a_proj":
                    break
                if ATT < 99 and (og > 0 or b > 0):
                    break
                smc[0] = 0
                abt = ab1t
                if og == 0 and b == 1:
                    S.dma("sp", [(ab1t.ap, abias1)], writes=[ab1t.b])
                c0 = b * 128
                mx = small(16); ngm = small(16); rs = small(16)
                for kvg in range(4):
                    b0 = 4 + 2 * (kvg % 2)
                    mms = []
                    for j in range(4):
                        h = kvg * 4 + j
                        mms.append(mm(bank(b0 + j // 2, 256, (j % 2) * 256), qT3[:, h // 2, c0:c0 + 128],
                                      kT4[:, h % 2, kvg, c0:c0 + 256], True, True))
                    S.op("pe", mms, reads=[qT.b, kT.b], writes=[PB[b0], PB[b0 + 1]])
                    if ATT < 1: continue
                    for hh in range(2):
                        S.op("dve", lambda e, hh=hh, kvg=kvg, b0=b0: e.tensor_tensor(tS.ap[:, hh * 512:(hh + 1) * 512], bank(b0 + hh),
                                                                                    abt.ap[:, (kvg * 4 + hh * 2) * 256:(kvg * 4 + hh * 2 + 2) * 256], ALU.add),
                             reads=[PB[b0 + hh], abt.b], writes=[tS.b])
                    if ATT < 2: continue
                    S.op("dve", lambda e, kvg=kvg: e.tensor_reduce(mx[:, kvg * 4:kvg * 4 + 4], tS.v("p (h k) -> p h k", h=4), AX.X, ALU.max),
                         reads=[tS.b], writes=[sm.b])
                    S.op("dve", lambda e, kvg=kvg: e.tensor_tensor(mx[:, kvg * 4:kvg * 4 + 4], mx[:, kvg * 4:kvg * 4 + 4], sk8.ap[:, kvg * 4:kvg * 4 + 4], ALU.max),
                         reads=[sm.b, sk8.b], writes=[sm.b])
                    S.op("dve", lambda e, kvg=kvg: e.tensor_scalar(ngm[:, kvg * 4:kvg * 4 + 4], mx[:, kvg * 4:kvg * 4 + 4], -0.125, None, ALU.mult),
                         reads=[sm.b], writes=[sm.b])
                    if ATT < 3: continue
                    for j in range(4):
                        h = kvg * 4 + j
                        S.op("act", lambda e, j=j, h=h: e.activation(Pt.ap[:, j * 256:(j + 1) * 256], tS.ap[:, j * 256:(j + 1) * 256], AF.Exp,
                                                                    bias=ngm[:, h:h + 1], scale=0.125, accum_out=rs[:, h:h + 1]),
                             reads=[tS.b, sm.b], writes=[Pt.b, sm.b])
                    if ATT < 4: continue
                    pv = bank16(1).rearrange("p (k m) -> p k m", m=128)
                    S.op("pe", [tp(pv[:, k, :], Pt.ap[:, k * 128:(k + 1) * 128], identb) for k in range(8)], reads=[Pt.b, cB.b], writes=[PB[1]])
                    S.op("act", lambda e: e.activation(PT.ap, bank16(1), AF.Copy), reads=[PB[1]], writes=[PT.b])
                    if ATT < 5: continue
                    PT3 = PT.v("p (k m) -> p k m", k=8)
                    mms = []
                    for j in range(4):
                        h = kvg * 4 + j
                        for half in range(2):
                            mms.append(mm(bank(2 + h // 8, 64, (h % 8) * 64), PT3[:, j * 2 + half, :], V3[:, b + half, kvg * 64:(kvg + 1) * 64], half == 0, half == 1))
                    S.op("pe", mms, reads=[PT.b, Vt.b], writes=[PB[2], PB[3]])
                if ATT < 6: continue
                dn = small(16)
                S.op("dve", lambda e: e.scalar_tensor_tensor(dn, sk8.ap, 0.125, ngm, ALU.mult, ALU.add), reads=[sk8.b, sm.b], writes=[sm.b])
                S.op("act", lambda e: e.activation(dn, dn, AF.Exp), reads=[sm.b], writes=[sm.b])
                S.op("dve", lambda e: e.tensor_tensor(dn, dn, rs, ALU.add), reads=[sm.b], writes=[sm.b])
                S.op("dve", lambda e: e.reciprocal(dn, dn), reads=[sm.b], writes=[sm.b])
                for g in range(2):
                    S.op("dve", lambda e, g=g: e.tensor_tensor(at_.ap[:, g * 512:(g + 1) * 512].rearrange("p (h d) -> p h d", h=8),
                                                               bank(2 + g).rearrange("p (h d) -> p h d", h=8),
                                                               dn[:, g * 8:(g + 1) * 8].unsqueeze(2).to_broadcast([128, 8, 64]), ALU.mult),
                         reads=[PB[2 + g], sm.b], writes=[at_.b])
                rstd = rms_rstd(at_.ap, 1024, [at_.b])
                S.op("dve", lambda e, rstd=rstd: e.scalar_tensor_tensor(ao.ap, at_.ap, rstd, ga.ap, ALU.mult, ALU.mult), reads=[at_.b, sm.b, ga.b], writes=[ao.b])
                tk = og * 512 + c0
                S.dma("sp", [(cat_d[tk:tk + 128, 0:1024], ao.ap)], reads=[ao.b], writes=[CATD])
            for eo in range(2):
                S.op("act", lambda e, eo=eo: e.activation(kT4[:, eo, :, 0:128], kT4[:, eo, :, 512:640], AF.Copy), reads=[kT.b], writes=[kT.b])
            S.op("act", lambda e: e.activation(V3[:, 0, :], V3[:, 4, :], AF.Copy), reads=[Vt.b], writes=[Vt.b])
        S.barrier()
        A.release(m)

    if stop_after != "B1b":
        phase_B1a()
    if do_samples and stop_after is None:
        phase_SM1()
    A.release(m_gm)

    def phase_B2():
        m = A.mark()
        wo = A.alloc("wo", 16 * 2048, BF16); wo3 = wo.v("p (k n) -> p k n", k=16)
        w_out_v = w_out.rearrange("(kc p) n -> p kc n", p=128)
        for pcs in range(4):
            S.dma("pool", [(wo3[:, :, pcs * 512:(pcs + 1) * 512], w_out_v[:, :, pcs * 512:(pcs + 1) * 512])], writes=[wo.b])
        ggt1 = A.alloc("ggt1", D); gm2 = A.alloc("gm2", D); sh2 = A.alloc("sh2", D)
        load_mod_bc(ggt1, 2); load_mod_bc(gm2, 4); load_mod_bc(sh2, 3)
        cb_ = [A.alloc("catb%d" % i, D, BF16) for i in range(2)]
        cT = A.alloc("catT", 16 * 128, BF16); cT3 = reg3(cT, 16)
        u2 = A.alloc("u2T", 16 * 128, BF16); u23 = reg3(u2, 16)
        def b2_body(cT3v, cTbuf, np_, xt, u2dst3, x1_store, u2_store):
            ss4 = small(4)
            for q in range(4):
                o = ps[0:np_, (4 + q) * 512:(5 + q) * 512]
                S.op("pe", [mm(o, cT3v[:, kc, 0:np_], wo3[:, kc, q * 512:(q + 1) * 512], kc == 0, kc == 15) for kc in range(16)],
                     reads=[cTbuf, wo.b], writes=[PB[4 + q]])
                S.op("act", lambda e, q=q, o=o: e.activation(junk.ap[0:np_, 0:512], o, AF.Square, scale=float(D) ** -0.5, accum_out=ss4[0:np_, q:q + 1]),
                     reads=[PB[4 + q]], writes=[junk.b, sm.b])
            rstd = small(1)
            S.op("dve", lambda e: e.tensor_reduce(rstd[0:np_], ss4[0:np_], AX.X, ALU.add), reads=[sm.b], writes=[sm.b])
            S.op("dve", lambda e: e.tensor_scalar(rstd[0:np_], rstd[0:np_], EPS, None, ALU.add), reads=[sm.b], writes=[sm.b])
            S.op("act", lambda e: e.activation(rstd[0:np_], rstd[0:np_], AF.Ln), reads=[sm.b], writes=[sm.b])
            S.op("act", lambda e: e.activation(rstd[0:np_], rstd[0:np_], AF.Exp, scale=-0.5), reads=[sm.b], writes=[sm.b])
            for q in range(4):
                o = ps[0:np_, (4 + q) * 512:(5 + q) * 512]
                S.op("dve", lambda e, q=q, o=o: e.scalar_tensor_tensor(tmp32.ap[0:np_, q * 512:(q + 1) * 512], o, rstd[0:np_], ggt1.ap[0:np_, q * 512:(q + 1) * 512], ALU.mult, ALU.mult),
                     reads=[PB[4 + q], sm.b, ggt1.b], writes=[tmp32.b])
            S.op("dve", lambda e: e.tensor_tensor(xt.ap[0:np_], xt.ap[0:np_], tmp32.ap[0:np_], ALU.add), reads=[xt.b, tmp32.b], writes=[xt.b])
            x1_store(xt)
            norm_mod_T(xt, gm2, sh2, u2dst3, 0, np_=np_)
            u2_store()

        for blk in range(16):
            smc[0] = 0
            cbt = cb_[blk % 2]
            S.dma("sp", [(cbt.ap, cat_d[blk * 128:(blk + 1) * 128, :])], reads=[CATD], writes=[cbt.b])
            xt = xb[blk % 2]
            S.dma("sp", [(xt.ap, xext[OWN0 + blk * 128:OWN0 + (blk + 1) * 128, :])], writes=[xt.b])
            transpose_to(cbt, cT3, 0)
            b2_body(cT3, cT.b, 128, xt,  u23,
                    lambda xt, blk=blk: S.dma("sp", [(x1_d[blk * 128:(blk + 1) * 128, :], xt.ap)], reads=[xt.b], writes=[X1D]),
                    lambda blk=blk: S.dma("sp", [(u2T_d.rearrange("k p t -> p k t")[:, :, blk * 128:(blk + 1) * 128], u23)], reads=[u2.b], writes=[U2TD]))
        if do_samples:
            smc[0] = 0
            load_mod_rows(ggt1, 2); load_mod_rows(gm2, 4); load_mod_rows(sh2, 3)
            S.dma("sp", [(x1s.ap[0:16, :], xs_d)], writes=[x1s.b])
            b2_body(catsT3, catsT.b, 16, x1s, u2sT3, lambda xt: None, lambda: None)
        S.barrier()
        A.release(m)

    if stop_after not in ("B1b", "B1a", "B1a_proj"):
        phase_B2()

    def phase_B3():
        m = A.mark()
        uT = A.alloc("u2Tall", 16 * NOWN, BF16); uT3 = uT.v("p (k n) -> p k n", k=16)
        S.dma("sp", [(uT3[:, kq * 4:(kq + 1) * 4, :], u2T_d.rearrange("k p t -> p k t")[:, kq * 4:(kq + 1) * 4, :]) for kq in range(4)], reads=[U2TD], writes=[uT.b])
        gsl = [A.alloc("wg%d" % i, 16 * 256, BF16) for i in range(2)]
        usl = [A.alloc("wu%d" % i, 16 * 256, BF16) for i in range(2)]
        hst_ = [A.alloc("hstg%d" % i, NOWN, BF16) for i in range(2)]
        et = [A.alloc("et%d" % i, 512) for i in range(2)]
        hsk = A.alloc("hs_tok", DFF, BF16)
        wg_v = w_gate.rearrange("(kc p) n -> p kc n", p=128)
        wu_v = w_up.rearrange("(kc p) n -> p kc n", p=128)
        for pj in range(NJ // 2):
            g3 = gsl[pj % 2].v("p (k n) -> p k n", k=16)
            u3 = usl[pj % 2].v("p (k n) -> p k n", k=16)
            S.dma("pool", [(g3, wg_v[:, :, pj * 256:(pj + 1) * 256])], writes=[gsl[pj % 2].b])
            S.dma("pool", [(u3, wu_v[:, :, pj * 256:(pj + 1) * 256])], writes=[usl[pj % 2].b])
            for jj in range(2):
                j = pj * 2 + jj
                hs = hst_[j % 2]
                for tg in range(4):
                    bg, bu = 4 + (tg % 2) * 2, 5 + (tg % 2) * 2
                    S.op("pe", [mm(bank(bg), g3[:, kc, jj * 128:(jj + 1) * 128], uT3[:, kc, tg * 512:(tg + 1) * 512], kc == 0, kc == 15) for kc in range(16)],
                         reads=[gsl[pj % 2].b, uT.b], writes=[PB[bg]])
                    S.op("pe", [mm(bank(bu), u3[:, kc, jj * 128:(jj + 1) * 128], uT3[:, kc, tg * 512:(tg + 1) * 512], kc == 0, kc == 15) for kc in range(16)],
                         reads=[usl[pj % 2].b, uT.b], writes=[PB[bu]])
                    e_ = et[tg % 2]
                    S.op("act", lambda e, e_=e_, bg=bg: e.activation(e_.ap, bank(bg), AF.Exp, scale=-1.0), reads=[PB[bg]], writes=[e_.b])
                    S.op("dve", lambda e, e_=e_: e.tensor_scalar(e_.ap, e_.ap, 1.0, None, ALU.add), reads=[e_.b], writes=[e_.b])
                    S.op("dve", lambda e, e_=e_: e.reciprocal(e_.ap, e_.ap), reads=[e_.b], writes=[e_.b])
                    S.op("dve", lambda e, e_=e_, bg=bg: e.tensor_tensor(e_.ap, e_.ap, bank(bg), ALU.mult), reads=[e_.b, PB[bg]], writes=[e_.b])
                    S.op("dve", lambda e, e_=e_, bu=bu, hs=hs, tg=tg: e.tensor_tensor(hs.ap[:, tg * 512:(tg + 1) * 512], e_.ap, bank(bu), ALU.mult),
                         reads=[e_.b, PB[bu]], writes=[hs.b])
                S.dma("sp", [(hT_d[j], hs.ap)], reads=[hs.b], writes=[HTD])
            if do_samples:
                og_ = ps[0:16, 2 * 512:2 * 512 + 256]
                ou_ = ps[0:16, 3 * 512:3 * 512 + 256]
                S.op("pe", [mm(og_, u2sT3[:, kc, 0:16], g3[:, kc, :], kc == 0, kc == 15) for kc in range(16)], reads=[u2sT.b, gsl[pj % 2].b], writes=[PB[2]])
                S.op("pe", [mm(ou_, u2sT3[:, kc, 0:16], u3[:, kc, :], kc == 0, kc == 15) for kc in range(16)], reads=[u2sT.b, usl[pj % 2].b], writes=[PB[3]])
                e_ = et[0]
                S.op("act", lambda e, e_=e_: e.activation(e_.ap[0:16, 0:256], og_, AF.Exp, scale=-1.0), reads=[PB[2]], writes=[e_.b])
                S.op("dve", lambda e, e_=e_: e.tensor_scalar(e_.ap[0:16, 0:256], e_.ap[0:16, 0:256], 1.0, None, ALU.add), reads=[e_.b], writes=[e_.b])
                S.op("dve", lambda e, e_=e_: e.reciprocal(e_.ap[0:16, 0:256], e_.ap[0:16, 0:256]), reads=[e_.b], writes=[e_.b])
                S.op("dve", lambda e, e_=e_: e.tensor_tensor(e_.ap[0:16, 0:256], e_.ap[0:16, 0:256], og_, ALU.mult), reads=[e_.b, PB[2]], writes=[e_.b])
                S.op("dve", lambda e, e_=e_, pj=pj: e.tensor_tensor(hsk.ap[0:16, pj * 256:(pj + 1) * 256], e_.ap[0:16, 0:256], ou_, ALU.mult), reads=[e_.b, PB[3]], writes=[hsk.b])
        if do_samples:
            for j0 in range(0, NJ, 8):
                nk = min(8, NJ - j0)
                pv = bank16(1).rearrange("p (k m) -> p k m", m=128)
                S.op("pe", [tp(pv[:, k, 0:16], hsk.ap[0:16, (j0 + k) * 128:(j0 + k + 1) * 128], identb[0:16, 0:16]) for k in range(nk)],
                     reads=[hsk.b, cB.b], writes=[PB[1]])
                S.op("act", lambda e, j0=j0, nk=nk, pv=pv: e.activation(hsT3[:, j0:j0 + nk, :], pv[:, 0:nk, 0:16], AF.Copy), reads=[PB[1]], writes=[hsT.b])
        S.barrier()
        A.release(m)

    def phase_B4():
        m = A.mark()
        ggt2 = A.alloc("ggt2", D)
        load_mod_bc(ggt2, 5)
        hT = A.alloc("hTg", NJ * 512, BF16); hT3 = hT.v("p (j t) -> p j t", j=NJ)
        wsl = [A.alloc("wd%d" % i, NJ * 256, BF16) for i in range(2)]
        ft = A.alloc("ft", 4 * D); f3 = ft.v("p (b n) -> p b n", b=4)
        wd_v = w_down.rearrange("(j p) n -> p j n", p=128)
        hT_v = hT_d.rearrange("j p t -> p j t")
        for tg in range(4):
            S.dma("sp", [(hT3[:, jq * 11:(jq + 1) * 11, :], hT_v[:, jq * 11:(jq + 1) * 11, tg * 512:(tg + 1) * 512]) for jq in range(4)], reads=[HTD], writes=[hT.b])
            for pc in range(8):
                w3 = wsl[pc % 2].v("p (j n) -> p j n", j=NJ)
                S.dma("pool", [(w3[:, jq * 11:(jq + 1) * 11, :], wd_v[:, jq * 11:(jq + 1) * 11, pc * 256:(pc + 1) * 256]) for jq in range(4)], writes=[wsl[pc % 2].b])
                for tb in range(4):
                    bi = 4 + tb
                    S.op("pe", [mm(bank(bi, 256), hT3[:, j, tb * 128:(tb + 1) * 128], w3[:, j, :], j == 0, j == NJ - 1) for j in range(NJ)],
                         reads=[hT.b, wsl[pc % 2].b], writes=[PB[bi]])
                    S.op("act", lambda e, tb=tb, bi=bi, pc=pc: e.activation(f3[:, tb, pc * 256:(pc + 1) * 256], bank(bi, 256), AF.Copy), reads=[PB[bi]], writes=[ft.b])
            for tb in range(4):
                smc[0] = 0
                blk = tg * 4 + tb
                xt = xb[tb % 2]
                S.dma("sp", [(xt.ap, x1_d[blk * 128:(blk + 1) * 128, :])], reads=[X1D], writes=[xt.b])
                rstd = rms_rstd(f3[:, tb, :], D, [ft.b])
                S.op("dve", lambda e, tb=tb, rstd=rstd: e.scalar_tensor_tensor(tmp32.ap, f3[:, tb, :], rstd, ggt2.ap, ALU.mult, ALU.mult), reads=[ft.b, sm.b, ggt2.b], writes=[tmp32.b])
                S.op("dve", lambda e, xt=xt: e.tensor_tensor(xt.ap, xt.ap, tmp32.ap, ALU.add), reads=[xt.b, tmp32.b], writes=[xt.b])
                S.dma("sp", [(y_out[blk * 128:(blk + 1) * 128, :], xt.ap)], reads=[xt.b])
        if do_samples:
            smc[0] = 0
            load_mod_rows(ggt2, 5)
            for pc in range(8):
                w3 = wsl[pc % 2].v("p (j n) -> p j n", j=NJ)
                S.dma("pool", [(w3[:, jq * 11:(jq + 1) * 11, :], wd_v[:, jq * 11:(jq + 1) * 11, pc * 256:(pc + 1) * 256]) for jq in range(4)], writes=[wsl[pc % 2].b])
                o = ps[0:16, (4 + pc % 4) * 512:(4 + pc % 4) * 512 + 256]
                S.op("pe", [mm(o, hsT3[:, j, :], w3[:, j, :], j == 0, j == NJ - 1) for j in range(NJ)], reads=[hsT.b, wsl[pc % 2].b], writes=[PB[4 + pc % 4]])
                S.op("act", lambda e, o=o, pc=pc: e.activation(f3[0:16, 0, pc * 256:(pc + 1) * 256], o, AF.Copy), reads=[PB[4 + pc % 4]], writes=[ft.b])
            rstd = rms_rstd(f3[0:16, 0, :], D, [ft.b])
            S.op("dve", lambda e: e.scalar_tensor_tensor(tmp32.ap[0:16], f3[0:16, 0, :], rstd[0:16], ggt2.ap[0:16], ALU.mult, ALU.mult), reads=[ft.b, sm.b, ggt2.b], writes=[tmp32.b])
            S.op("dve", lambda e: e.tensor_tensor(x1s.ap[0:16], x1s.ap[0:16], tmp32.ap[0:16], ALU.add), reads=[x1s.b, tmp32.b], writes=[x1s.b])
            S.dma("sp", [(ys_out, x1s.ap[0:16, :])], reads=[x1s.b])
        S.barrier()
        A.release(m)

    if stop_after is None:
        phase_B3()
        phase_B4()
    S.final_wait("sp")
    S.emit()
    st.close()
    print("[build] arena peak words", A.peak, "instr", {e: S.count[e] for e in S.count})
    return nc


def _consts():
    i = np.arange(128)
    ident = np.eye(128, dtype=np.float32)
    U = (i[:, None] <= i[None, :]).astype(np.float32)
    UT = (i[:, None] > i[None, :]).astype(np.float32)
    ones = np.ones((128, 128), np.float32)
    caus = (i[None, :] >= i[:, None]).astype(np.float32)
    cf32 = np.concatenate([ident, U, UT, ones, caus], axis=1)
    cb16 = np.concatenate([ident, ones], axis=1).astype(ml_dtypes.bfloat16)
    return cf32, cb16


def _abias(first_core):
    slopes = (2.0 ** (-8.0 * np.arange(1, 17) / 16)).astype(np.float32)
    a = np.arange(128)[:, None]
    j = np.arange(256)[None, :]
    dist = a + 128 - j
    valid = (dist >= 0) & (dist < 128)
    out = []
    for first in (True, False):
        v = valid & ((j >= 128) if (first and first_core) else True)
        b = np.where(v[None], -slopes[:, None, None] * dist[None].astype(np.float32), -30000.0) * 8.0
        out.append(np.ascontiguousarray(np.transpose(b, (1, 0, 2)).reshape(128, 16 * 256)).astype(np.float32))
    return out


def make_in_maps(inp):
    cf32, cb16 = _consts()
    xp = np.asarray(inp["x_prompt"])[0]
    maps = []
    wada = np.concatenate([np.asarray(inp["w_ada"])[0], np.asarray(inp["b_ada"])[0][None, :]], axis=0)
    gvec = np.stack([np.asarray(inp[k])[0] for k in ("g_pre_mix", "g_post_mix", "g_pre_ffn", "g_post_ffn")], 0)
    cwv = np.asarray(inp["conv_w"])[0]
    convw = np.ascontiguousarray(cwv.reshape(4, 12, 128).transpose(2, 1, 0).reshape(128, 48))
    convb = np.ascontiguousarray(np.asarray(inp["conv_b"])[0].reshape(12, 128).T)
    hvec = np.concatenate([np.asarray(inp[k])[0] for k in ("dt_bias", "a_log", "d_skip", "attn_sinks")])[None, :]
    slopes = (2.0 ** (-8.0 * np.arange(1, 17) / 16)).astype(np.float32)
    sbias = (-slopes[:, None] * (127 - np.arange(128))[None, :].astype(np.float32) * 8.0).astype(np.float32)
    selkv = (np.arange(16)[:, None] // 4 == np.arange(4)[None, :]).astype(np.float32)
    for c in range(8):
        nreal = 2048 * (c + 1)
        xext = np.zeros((NEXT, D), np.float32)
        xext[NEXT - nreal:] = xp[:nreal]
        m = np.zeros((NEXT,), np.float32)
        m[NEXT - nreal:] = 1.0
        ab0, ab1 = _abias(c == 0)
        maps.append({
            "xext": xext, "mrow": m[None, :], "mtok": np.ascontiguousarray(m.reshape(NEXT // 128, 128).T),
            "cmat": np.concatenate([np.asarray(inp["c_sample"])[16 * c:16 * c + 16], np.asarray(inp["c_prompt"])], 0),
            "wada": wada, "gvec": gvec, "w_in": np.asarray(inp["w_in"])[0], "w_out": np.asarray(inp["w_out"])[0],
            "w_gate": np.asarray(inp["w_gate"])[0], "w_up": np.asarray(inp["w_up"])[0], "w_down": np.asarray(inp["w_down"])[0],
            "convw": convw, "convb": convb, "hvec": hvec.astype(np.float32),
            "gatt": np.asarray(inp["g_attn_out"]), "gssm": np.asarray(inp["g_ssm_out"]),
            "cf32": cf32, "cb16": cb16, "abias0": ab0, "abias1": ab1,
            "xs_d": np.ascontiguousarray(np.asarray(inp["x_sample"])[16 * c:16 * c + 16, 0, :]),
            "ck_d": np.ascontiguousarray(np.asarray(inp["cache_k"])[0, 16 * c:16 * c + 16].reshape(16, 128, 256)),
            "cv_d": np.ascontiguousarray(np.asarray(inp["cache_v"])[0, 16 * c:16 * c + 16].reshape(16, 128, 256)),
            "sconv_d": np.ascontiguousarray(np.asarray(inp["state_conv"])[0, 16 * c:16 * c + 16].reshape(16, 4608)),
            "sssm_d": np.ascontiguousarray(np.asarray(inp["state_ssm"])[0, 16 * c:16 * c + 16].reshape(16, 1024, 128)),
            "convw_raw": np.ascontiguousarray(cwv.reshape(1, 6144)), "convb_raw": np.asarray(inp["conv_b"]).reshape(1, 1536),
            "sinkcol": np.asarray(inp["attn_sinks"]).reshape(16, 1), "sbias": sbias, "selkv": selkv,
            "dskrow": np.repeat(np.asarray(inp["d_skip"])[0], 64)[None, :].astype(np.float32),
        })
    return maps


_NC_CACHE = {}


def kernel(**inputs):
    if "nc" not in _NC_CACHE:
        _NC_CACHE["nc"] = build()
    nc = _NC_CACHE["nc"]
    maps = make_in_maps(inputs)
    res = run_bass_kernel_spmd(nc, maps, core_ids=list(range(8)))
    r = res.results
    f = np.float32
    y_prompt = np.concatenate([np.asarray(r[c]["y_out"]) for c in range(8)], 0)[None].astype(f)
    k_prompt = np.asarray(r[7]["k_out"]).reshape(1, 1, 128, 4, 64).astype(f)
    v_prompt = np.asarray(r[7]["v_out"]).reshape(1, 1, 128, 4, 64).astype(f)
    conv_prompt = np.asarray(r[7]["conv_out"]).reshape(1, 1, 3, 1536).astype(f)
    ssm_prompt = np.asarray(r[7]["ssm_out"]).reshape(1, 1, 16, 64, 128).astype(f)
    cat = lambda name: np.concatenate([np.asarray(r[c][name]) for c in range(8)], 0).astype(f)
    y_sample = cat("ys_out").reshape(128, 1, 2048)
    k_sample = cat("ks_out").reshape(1, 128, 128, 4, 64)
    v_sample = cat("vs_out").reshape(1, 128, 128, 4, 64)
    conv_sample = cat("convs_out").reshape(1, 128, 3, 1536)
    ssm_sample = cat("ssms_out").reshape(1, 128, 16, 64, 128)
    return (y_prompt, y_sample, k_prompt, v_prompt, conv_prompt, ssm_prompt,
            k_sample, v_sample, conv_sample, ssm_sample)
```

```python
import contextlib
import numpy as np
import ml_dtypes
import concourse.bass as bass
import concourse.mybir as mybir
from concourse.bass_utils import run_bass_kernel_spmd

F32 = mybir.dt.float32
BF16 = mybir.dt.bfloat16
ALU = mybir.AluOpType
AF = mybir.ActivationFunctionType
AX = mybir.AxisListType

ENGS = ("pe", "act", "dve", "pool", "sp")


class Buf:
    __slots__ = ("name", "last_w", "readers")

    def __init__(self, name):
        self.name = name
        self.last_w = None
        self.readers = []


class Sched:
    def __init__(self, nc, n_dma_sp=40, n_dma_pool=12, self_wait=True):
        self.nc = nc
        self.q = {e: [] for e in ENGS}
        self.count = {e: 0 for e in ENGS}
        self.seen = {e: {} for e in ENGS}
        self.self_wait = self_wait
        self.ndma = {"sp": n_dma_sp, "pool": n_dma_pool, "act": 8}
        self.dma_next = {"sp": 0, "pool": 0, "act": 0}
        self.dma_cnt = {}
        self.sems = {}

    def _deps(self, eng, reads, writes):
        deps = {}

        def add(tok):
            if tok is None:
                return
            k, v = tok
            if deps.get(k, 0) < v:
                deps[k] = v
        for b in reads:
            add(b.last_w)
        for b in writes:
            add(b.last_w)
            for r in b.readers:
                add(r)
        out = []
        for k, v in deps.items():
            if k == eng and (eng == "pe" or not self.self_wait):
                continue
            if self.seen[eng].get(k, 0) >= v:
                continue
            self.seen[eng][k] = v
            out.append((k, v))
        return out

    def _commit(self, tok, reads, writes):
        for b in writes:
            b.last_w = tok
            b.readers = []
        for b in reads:
            if b not in writes:
                b.readers.append(tok)
                if len(b.readers) > 64:
                    best = {}
                    for k, v in b.readers:
                        if best.get(k, 0) < v:
                            best[k] = v
                    b.readers = list(best.items())

    def op(self, eng, fns, reads=(), writes=()):
        if callable(fns):
            fns = [fns]
        waits = self._deps(eng, reads, writes)
        self.count[eng] += 1
        tok = (eng, self.count[eng])
        self.q[eng].append(("op", waits, fns, tok))
        self._commit(tok, reads, writes)
        return tok

    def dma(self, eng, pairs, reads=(), writes=(), **kw):
        i = self.dma_next[eng]
        self.dma_next[eng] = (i + 1) % self.ndma[eng]
        key = "d_%s_%d" % (eng, i)
        prev = self.dma_cnt.get(key, 0)
        waits = self._deps(eng, reads, writes)
        if prev and self.seen[eng].get(key, 0) < prev:
            self.seen[eng][key] = prev
            waits.append((key, prev))
        val = prev + 16 * len(pairs)
        self.dma_cnt[key] = val
        tok = (key, val)
        self.q[eng].append(("dma", waits, pairs, tok, kw))
        self._commit(tok, reads, writes)
        return tok

    def barrier(self):
        targets = [(e, self.count[e]) for e in ENGS if self.count[e] > 0]
        targets += [(k, v) for k, v in self.dma_cnt.items()]
        for e in ENGS:
            waits = []
            for k, v in targets:
                if k == e:
                    continue
                if self.seen[e].get(k, 0) >= v:
                    continue
                self.seen[e][k] = v
                waits.append((k, v))
            if waits:
                self.q[e].append(("wait", waits))

    def final_wait(self, eng="sp"):
        waits = [(k, v) for k, v in self.dma_cnt.items()]
        waits += [(e, self.count[e]) for e in ENGS if self.count[e] > 0 and e != eng]
        self.q[eng].append(("wait", waits))

    def emit(self):
        nc = self.nc
        keys = [e for e in ENGS if self.count[e] > 0] + sorted(self.dma_cnt.keys())
        with contextlib.ExitStack() as st:
            for k in keys:
                self.sems[k] = st.enter_context(nc.semaphore("s_" + k))
            block = st.enter_context(nc.Block())
            sems = self.sems

            def run(engobj, items):
                for it in items:
                    for (k, v) in it[1]:
                        engobj.wait_ge(sems[k], v)
                    if it[0] == "op":
                        _, _, fns, tok = it
                        ins = None
                        for f in fns:
                            ins = f(engobj)
                        ins.then_inc(sems[tok[0]], 1)
                    elif it[0] == "dma":
                        _, _, pairs, tok, kw = it
                        for (o, i) in pairs:
                            engobj.dma_start(out=o, in_=i, **kw).then_inc(sems[tok[0]], 16)

            if self.q["pe"]:
                @block.tensor
                def _(e):
                    run(e, self.q["pe"])
            if self.q["act"]:
                @block.scalar
                def _(e):
                    run(e, self.q["act"])
            if self.q["dve"]:
                @block.vector
                def _(e):
                    run(e, self.q["dve"])
            if self.q["pool"]:
                @block.gpsimd
                def _(e):
                    run(e, self.q["pool"])
            if self.q["sp"]:
                @block.sync
                def _(e):
                    run(e, self.q["sp"])


class Tile:
    def __init__(self, ap, name):
        self.ap = ap
        self.b = Buf(name)

    def v(self, pat, **kw):
        return self.ap.rearrange(pat, **kw)


class Arena:
    def __init__(self, base, nwords):
        self.base = base
        self.n = nwords
        self.off = 0
        self.peak = 0

    def alloc(self, name, nelem, dt=F32):
        words = nelem if dt == F32 else (nelem + 1) // 2
        wal = (words + 7) // 8 * 8
        assert self.off + wal <= self.n, ("SBUF arena overflow", name, self.off, wal, self.n)
        ap = self.base[:, self.off:self.off + words]
        if dt != F32:
            ap = ap.bitcast(dt)
        self.off += wal
        self.peak = max(self.peak, self.off)
        return Tile(ap, name)

    def mark(self):
        return self.off

    def release(self, m):
        self.off = m


def mm(out, lhsT, rhs, start, stop):
    return lambda e: e.matmul(out, lhsT, rhs, start=start, stop=stop)


def tp(out, in_, ident):
    return lambda e: e.transpose(out, in_, ident)
D = 2048
KC = 16
NEXT = 16384
NOWN = 2048
EPS = 1e-6
NPRE_G = (NEXT - NOWN) // 512
INW = 4112
DFF = 5632
NJ = DFF // 128


def build(n_pre_groups=NPRE_G, stop_after=None, do_samples=True):
    nc = bass.Bass("TRN2", target_bir_lowering=False)

    def din(name, shape, dt=F32):
        return nc.dram_tensor(name, list(shape), dt, kind="ExternalInput").ap()

    def dout(name, shape, dt=F32):
        return nc.dram_tensor(name, list(shape), dt, kind="ExternalOutput").ap()

    def dscr(name, shape, dt=F32):
        return nc.dram_tensor(name, list(shape), dt, kind="Internal").ap()

    xext = din("xext", [NEXT, D])
    mrow = din("mrow", [1, NEXT])
    mtok = din("mtok", [128, NEXT // 128])
    cmat = din("cmat", [17, D])
    wada = din("wada", [D + 1, 6 * D])
    gvec = din("gvec", [4, D])
    w_in = din("w_in", [D, INW])
    w_out = din("w_out", [D, D])
    w_gate = din("w_gate", [D, DFF])
    w_up = din("w_up", [D, DFF])
    w_down = din("w_down", [DFF, D])
    convw = din("convw", [128, 48])
    convb = din("convb", [128, 12])
    hvec = din("hvec", [1, 64])
    gatt = din("gatt", [1, 1024])
    gssm = din("gssm", [1, 1024])
    cf32 = din("cf32", [128, 5 * 128])
    cb16 = din("cb16", [128, 2 * 128], BF16)
    abias0 = din("abias0", [128, 16 * 256])
    abias1 = din("abias1", [128, 16 * 256])
    y_out = dout("y_out", [NOWN, D])
    k_out = dout("k_out", [128, 256])
    v_out = dout("v_out", [128, 256])
    conv_out = dout("conv_out", [3, 1536])
    ssm_out = dout("ssm_out", [1024, 128])
    mod_d = dscr("mod_d", [17, 6 * D])
    cat_d = dscr("cat_d", [NOWN, D], BF16)
    x1_d = dscr("x1_d", [NOWN, D])
    hT_d = dscr("hT_d", [NJ, 128, NOWN], BF16)

    st = contextlib.ExitStack()
    NW = 47616
    big = st.enter_context(nc.sbuf_tensor("big", [128, NW], F32))
    ps = st.enter_context(nc.psum_tensor("ps", [128, 4096], F32))
    A = Arena(big, NW)
    S = Sched(nc)
    PB = [Buf("psb%d" % i) for i in range(8)]

    def bank(i, n=512, off=0):
        return ps[:, i * 512 + off:i * 512 + off + n]

    def bank16(i):
        return ps[:, i * 512:(i + 1) * 512].bitcast(BF16)

    MODD = Buf("mod_d")
    w_in_v = w_in.rearrange("(kc p) n -> p kc n", p=128)

    cF = A.alloc("cF", 5 * 128)
    cB = A.alloc("cB", 2 * 128, BF16)
    S.dma("sp", [(cF.ap, cf32)], writes=[cF.b])
    S.dma("sp", [(cB.ap, cb16)], writes=[cB.b])
    identf = cF.ap[:, 0:128]
    Umat = cF.ap[:, 128:256]
    UTmat = cF.ap[:, 256:384]
    onesf = cF.ap[:, 384:512]
    caus01 = cF.ap[:, 512:640]
    identb = cB.ap[:, 0:128]
    onesb = cB.ap[:, 128:256]
    hv = A.alloc("hv", 64)
    S.dma("sp", [(hv.ap, hvec.to_broadcast([128, 64]))], writes=[hv.b])
    dtb_bc = hv.ap[:, 0:16]
    dsk_bc = hv.ap[:, 32:48]
    sink_bc = hv.ap[:, 48:64]
    a_bc = A.alloc("a_bc", 16)
    S.op("act", lambda e: e.activation(a_bc.ap, hv.ap[:, 16:32], AF.Exp), reads=[hv.b], writes=[a_bc.b])
    S.op("dve", lambda e: e.tensor_scalar(a_bc.ap, a_bc.ap, -1.0, None, ALU.mult), reads=[a_bc.b], writes=[a_bc.b])
    cw = A.alloc("cw", 48)
    cbi = A.alloc("cbi", 12)
    S.dma("sp", [(cw.ap, convw)], writes=[cw.b])
    S.dma("sp", [(cbi.ap, convb)], writes=[cbi.b])
    mt = A.alloc("mt", NEXT // 128)
    S.dma("sp", [(mt.ap, mtok)], writes=[mt.b])

    def phase0():
        m0 = A.mark()
        cm = A.alloc("cm", D)
        S.dma("sp", [(cm.ap[0:17, :], cmat)], writes=[cm.b])
        ee = A.alloc("ee", D)
        S.op("act", lambda e: e.activation(ee.ap[0:17], cm.ap[0:17], AF.Exp, scale=-1.0), reads=[cm.b], writes=[ee.b])
        S.op("dve", lambda e: e.tensor_scalar(ee.ap[0:17], ee.ap[0:17], 1.0, None, ALU.add), reads=[ee.b], writes=[ee.b])
        S.op("dve", lambda e: e.reciprocal(ee.ap[0:17], ee.ap[0:17]), reads=[ee.b], writes=[ee.b])
        sc = A.alloc("sc", D, BF16)
        S.op("dve", lambda e: e.tensor_tensor(sc.ap[0:17], cm.ap[0:17], ee.ap[0:17], ALU.mult), reads=[cm.b, ee.b], writes=[sc.b])
        cT = A.alloc("cT", 16 * 32, BF16)
        cT3 = cT.v("p (k m) -> p k m", k=16)
        pb = bank16(0).rearrange("p (k m) -> p k m", m=32)
        S.op("pe", [tp(pb[:, kc, 0:17], sc.ap[0:17, kc * 128:(kc + 1) * 128], identb[0:17, 0:17]) for kc in range(16)],
             reads=[sc.b, cB.b], writes=[PB[0]])
        S.op("act", lambda e: e.activation(cT3[:, :, 0:17], pb[:, 0:16, 0:17], AF.Copy), reads=[PB[0]], writes=[cT.b])
        gv = A.alloc("gv", 4 * D)
        S.dma("sp", [(gv.ap[0:17, g * D:(g + 1) * D], gvec[g:g + 1, :].to_broadcast([17, D])) for g in range(4)], writes=[gv.b])
        mod = A.alloc("mod", 6 * D)
        slots = [A.alloc("wa%d" % i, 17 * 512, BF16) for i in range(2)]
        wada_v = wada[0:D, :].rearrange("(kc p) n -> p kc n", p=128)
        for pc in range(24):
            sl = slots[pc % 2]
            s3 = sl.v("p (k n) -> p k n", k=17)
            S.dma("pool", [(s3[:, 0:16, :], wada_v[:, :, pc * 512:(pc + 1) * 512]),
                           (s3[0:1, 16, :], wada[D:D + 1, pc * 512:(pc + 1) * 512])], writes=[sl.b])
            bi = 1 + pc % 2
            o = ps[0:17, bi * 512:(bi + 1) * 512]
            S.op("pe", [mm(o, cT3[:, kc, 0:17], s3[:, kc, :], kc == 0, False) for kc in range(16)]
                 + [mm(o, onesb[0:1, 0:17], s3[0:1, 16, :], False, True)],
                 reads=[cT.b, sl.b, cB.b], writes=[PB[bi]])
            ch, co = pc // 4, (pc % 4) * 512
            dst = mod.ap[0:17, pc * 512:(pc + 1) * 512]
            if ch in (0, 3):
                S.op("act", lambda e, dst=dst, o=o: e.activation(dst, o, AF.Copy), reads=[PB[bi]], writes=[mod.b])
            elif ch in (1, 4):
                g = gv.ap[0:17, (0 if ch == 1 else 2) * D + co:(0 if ch == 1 else 2) * D + co + 512]
                S.op("dve", lambda e, dst=dst, o=o, g=g: e.scalar_tensor_tensor(dst, o, 1.0, g, ALU.add, ALU.mult),
                     reads=[PB[bi], gv.b], writes=[mod.b])
            else:
                g = gv.ap[0:17, (1 if ch == 2 else 3) * D + co:(1 if ch == 2 else 3) * D + co + 512]
                S.op("dve", lambda e, dst=dst, o=o, g=g: e.tensor_tensor(dst, o, g, ALU.mult),
                     reads=[PB[bi], gv.b], writes=[mod.b])
        S.dma("sp", [(mod_d, mod.ap[0:17, :])], reads=[mod.b], writes=[MODD])
        S.barrier()
        A.release(m0)

    phase0()

    def load_mod_bc(tile, ch):
        S.dma("sp", [(tile.ap, mod_d[16:17, ch * D:(ch + 1) * D].to_broadcast([128, D]))], reads=[MODD], writes=[tile.b])

    halo = A.alloc("halo", 36)
    halo3 = halo.v("p (c i) -> p c i", i=3)
    S.op("dve", lambda e: e.memset(halo.ap, 0.0), writes=[halo.b])
    hst = A.alloc("hst", 1024)
    S.op("dve", lambda e: e.memset(hst.ap, 0.0), writes=[hst.b])
    hb = A.alloc("hb", 1024, BF16)
    S.op("dve", lambda e: e.memset(hb.ap, 0.0), writes=[hb.b])
    xb = [A.alloc("xb%d" % i, D) for i in range(2)]
    junk = A.alloc("junk", D, BF16)
    ub = A.alloc("ub", D, BF16)
    tmp32 = A.alloc("tmp32", D)
    sm = A.alloc("sm", 512)
    smc = [0]

    def small(n):
        if smc[0] + n > 512:
            smc[0] = 0
        a = sm.ap[:, smc[0]:smc[0] + n]
        smc[0] += n
        return a

    def rms_rstd(src_ap, n, rd, wr_extra=()):
        ssn = small(1)
        pp = src_ap.shape[0]
        S.op("act", lambda e: e.activation(junk.ap[0:pp, 0:n], src_ap, AF.Square, scale=float(n) ** -0.5, accum_out=ssn[0:pp]),
             reads=list(rd), writes=[junk.b, sm.b])
        S.op("dve", lambda e: e.tensor_scalar(ssn[0:pp], ssn[0:pp], EPS, None, ALU.add), reads=[sm.b], writes=[sm.b])
        S.op("act", lambda e: e.activation(ssn[0:pp], ssn[0:pp], AF.Ln), reads=[sm.b], writes=[sm.b])
        S.op("act", lambda e: e.activation(ssn[0:pp], ssn[0:pp], AF.Exp, scale=-0.5), reads=[sm.b], writes=[sm.b])
        return ssn

    def norm_mod_T(x_tile, gm, sh, dstT3, col0, np_=128, mask=None):
        rstd = rms_rstd(x_tile.ap[0:np_], D, [x_tile.b])
        S.op("dve", lambda e: e.scalar_tensor_tensor(tmp32.ap[0:np_], x_tile.ap[0:np_], rstd[0:np_], gm.ap[0:np_], ALU.mult, ALU.mult),
             reads=[x_tile.b, sm.b, gm.b], writes=[tmp32.b])
        if mask is None:
            S.op("dve", lambda e: e.tensor_tensor(ub.ap[0:np_], tmp32.ap[0:np_], sh.ap[0:np_], ALU.add), reads=[tmp32.b, sh.b], writes=[ub.b])
        else:
            S.op("dve", lambda e: e.scalar_tensor_tensor(ub.ap[0:np_], sh.ap[0:np_], mask, tmp32.ap[0:np_], ALU.mult, ALU.add),
                 reads=[tmp32.b, sh.b, mt.b], writes=[ub.b])
        transpose_to(ub, dstT3, col0, np_)

    def transpose_to(src_bf, dstT3, col0, np_=128, nk=16, kofs=0):
        for half in range(nk // 8):
            bi = 1
            pv = bank16(bi).rearrange("p (k m) -> p k m", m=128)
            S.op("pe", [tp(pv[:, k, 0:np_], src_bf.ap[0:np_, (half * 8 + k) * 128:(half * 8 + k + 1) * 128], identb[0:np_, 0:np_]) for k in range(8)],
                 reads=[src_bf.b, cB.b], writes=[PB[bi]])
            eng = "act" if half % 2 == 0 else "dve"
            dst = dstT3[:, kofs + half * 8:kofs + half * 8 + 8, col0:col0 + np_]
            if eng == "act":
                S.op("act", lambda e, dst=dst, pv=pv: e.activation(dst, pv[:, 0:8, 0:np_], AF.Copy), reads=[PB[bi]], writes=[dstT3_buf[id(dstT3)]])
            else:
                S.op("dve", lambda e, dst=dst, pv=pv: e.tensor_copy(dst, pv[:, 0:8, 0:np_]), reads=[PB[bi]], writes=[dstT3_buf[id(dstT3)]])

    dstT3_buf = {}

    def reg3(tile, k):
        v3 = tile.v("p (k n) -> p k n", k=k)
        dstT3_buf[id(v3)] = tile.b
        return v3

    def silu_to(dst, src, n, rd, wr, np_=128, tmp=None):
        t = tmp if tmp is not None else tmp32
        S.op("act", lambda e: e.activation(t.ap[0:np_, 0:n], src, AF.Exp, scale=-1.0), reads=list(rd), writes=[t.b])
        S.op("dve", lambda e: e.tensor_scalar(t.ap[0:np_, 0:n], t.ap[0:np_, 0:n], 1.0, None, ALU.add), reads=[t.b], writes=[t.b])
        S.op("dve", lambda e: e.reciprocal(t.ap[0:np_, 0:n], t.ap[0:np_, 0:n]), reads=[t.b], writes=[t.b])
        S.op("dve", lambda e: e.tensor_tensor(dst, src, t.ap[0:np_, 0:n], ALU.mult), reads=list(rd) + [t.b], writes=list(wr))

    xs_d = din("xs_d", [16, D])
    ck_d = din("ck_d", [16, 128, 256])
    cv_d = din("cv_d", [16, 128, 256])
    sconv_d = din("sconv_d", [16, 3 * 1536])
    sssm_d = din("sssm_d", [16, 1024, 128])
    convw_raw = din("convw_raw", [1, 4 * 1536])
    convb_raw = din("convb_raw", [1, 1536])
    sinkcol = din("sinkcol", [16, 1])
    sbias = din("sbias", [16, 128])
    selkv = din("selkv", [16, 4])
    dskrow = din("dskrow", [1, 1024])
    ys_out = dout("ys_out", [16, D])
    ks_out = dout("ks_out", [16, 128, 256])
    vs_out = dout("vs_out", [16, 128, 256])
    convs_out = dout("convs_out", [16, 3 * 1536])
    ssms_out = dout("ssms_out", [16, 1024, 128])
    att_d = dscr("att_d", [16, 1024])
    KSO = Buf("ks_out"); VSO = Buf("vs_out"); ATTD = Buf("att_d")
    catsT = A.alloc("catsT", 16 * 16, BF16); catsT3 = reg3(catsT, 16)
    u2sT = A.alloc("u2sT", 16 * 16, BF16); u2sT3 = reg3(u2sT, 16)
    hsT = A.alloc("hsT", NJ * 16, BF16); hsT3 = hsT.v("p (j b) -> p j b", j=NJ)
    x1s = A.alloc("x1s", D)

    def load_mod_rows(tile, ch):
        S.dma("sp", [(tile.ap[0:16, :], mod_d[0:16, ch * D:(ch + 1) * D])], reads=[MODD], writes=[tile.b])

    def phase_SM1():
        m = A.mark()
        smc[0] = 0
        load_mod_rows(gm1, 1)
        load_mod_rows(sh1, 0)
        xt = xb[0]
        S.dma("sp", [(xt.ap[0:16, :], xs_d)], writes=[xt.b])
        usT = A.alloc("usT", 16 * 16, BF16); usT3 = reg3(usT, 16)
        norm_mod_T(xt, gm1, sh1, usT3, 0, np_=16)
        pj = A.alloc("proj_s", INW)
        sel = A.alloc("sel", 16 * 128); sel3 = sel.v("p (b m) -> p b m", b=16)
        cs = A.alloc("cat_s", D)
        cs16 = A.alloc("cat_s16", D, BF16)
        gb = A.alloc("g_bc", 2048)
        xa = A.alloc("xbc_s", 1536)
        S.op("dve", lambda e: e.tensor_copy(sel3[0:16], identf[0:16, 0:16].unsqueeze(2).to_broadcast([16, 16, 128])), reads=[cF.b], writes=[sel.b])
        m1 = A.mark()
        wsl = [A.alloc("wss%d" % i, 16 * 512, BF16) for i in range(2)]
        w3s = [t.v("p (k n) -> p k n", k=16) for t in wsl]
        for pc in range(9):
            n = 512 if pc < 8 else 16
            i = pc % 2
            S.dma("pool", [(w3s[i][:, :, 0:n], w_in_v[:, :, pc * 512:pc * 512 + n])], writes=[wsl[i].b])
            bi = 4 + pc % 4
            o = ps[0:16, bi * 512:bi * 512 + n]
            S.op("pe", [mm(o, usT3[:, kc, 0:16], w3s[i][:, kc, 0:n], kc == 0, kc == 15) for kc in range(16)], reads=[usT.b, wsl[i].b], writes=[PB[bi]])
            S.op("act", lambda e, o=o, pc=pc, n=n: e.activation(pj.ap[0:16, pc * 512:pc * 512 + n], o, AF.Copy), reads=[PB[bi]], writes=[pj.b])
        P = pj.ap
        S.barrier()
        A.release(m1)
        S.dma("sp", [(ks_out[:, 0:127, :], ck_d[:, 1:128, :]), (ks_out[:, 127, :], P[0:16, 1024:1280])], reads=[pj.b], writes=[KSO])
        S.dma("sp", [(vs_out[:, 0:127, :], cv_d[:, 1:128, :]), (vs_out[:, 127, :], P[0:16, 1280:1536])], reads=[pj.b], writes=[VSO])
        S.dma("sp", [(convs_out[:, 0:3072], sconv_d[:, 1536:4608]), (convs_out[:, 3072:4608], P[0:16, 2560:4096])], reads=[pj.b])
        Ka = A.alloc("Ka", 16 * 256); Ka3 = Ka.v("p (b n) -> p b n", b=16)
        S.dma("sp", [(Ka3, ks_out.rearrange("b s n -> s b n"))], reads=[KSO], writes=[Ka.b])
        Vh = A.alloc("Vh", 16 * 256, BF16); Vh3 = Vh.v("p (b n) -> p b n", b=16)
        S.dma("pool", [(Vh3, vs_out.rearrange("b s n -> s b n"))], reads=[VSO], writes=[Vh.b])
        cst = A.alloc("scst", 128 + 8)
        S.dma("sp", [(cst.ap[0:16, 0:128], sbias), (cst.ap[0:16, 128:129], sinkcol), (cst.ap[0:16, 129:133], selkv)], writes=[cst.b])
        sk8c = cst.ap[0:16, 133:134]
        S.op("dve", lambda e: e.tensor_scalar(sk8c, cst.ap[0:16, 128:129], 8.0, None, ALU.mult), reads=[cst.b], writes=[cst.b])
        prod = A.alloc("prod", 1024)
        STt = A.alloc("STt", 16 * 16); ST3 = STt.v("p (b h) -> p b h", b=16)
        for b in range(16):
            for hf in range(2):
                S.op("pe", mm(bank(2 + hf), sel3[0:16, b, :], P[0:16, hf * 512:(hf + 1) * 512], True, True), reads=[sel.b, pj.b], writes=[PB[2 + hf]])
                S.op("dve", lambda e, b=b, hf=hf: e.tensor_tensor(prod.ap[:, hf * 512:(hf + 1) * 512].rearrange("p (k g d) -> p k g d", k=2, g=4),
                                                                 bank(2 + hf).rearrange("p (k g d) -> p k g d", k=2, g=4),
                                                                 Ka3[:, b, hf * 128:(hf + 1) * 128].rearrange("p (k d) -> p k d", k=2).unsqueeze(2).to_broadcast([128, 2, 4, 64]), ALU.mult),
                     reads=[PB[2 + hf], Ka.b], writes=[prod.b])
            S.op("dve", lambda e, b=b: e.tensor_reduce(ST3[:, b, :], prod.v("p (h d) -> p h d", h=16), AX.X, ALU.add), reads=[prod.b], writes=[STt.b])
        for b in range(16):
            S.op("pe", tp(ps[0:16, 4 * 512 + b * 128:4 * 512 + (b + 1) * 128], ST3[:, b, :], identf), reads=[STt.b, cF.b], writes=[PB[4 + b // 4]])
        tsm = A.alloc("tsm", 2048); t3 = tsm.v("p (b s) -> p b s", b=16)
        S.op("dve", lambda e: e.tensor_tensor(t3[0:16], ps[0:16, 2048:4096].rearrange("p (b s) -> p b s", b=16),
                                              cst.ap[0:16, 0:128].unsqueeze(1).to_broadcast([16, 16, 128]), ALU.add),
             reads=[PB[4], PB[5], PB[6], PB[7], cst.b], writes=[tsm.b])
        mxs = small(16); ngs = small(16); rss = small(16); dns = small(16)
        S.op("dve", lambda e: e.tensor_reduce(mxs[0:16], t3[0:16], AX.X, ALU.max), reads=[tsm.b], writes=[sm.b])
        S.op("dve", lambda e: e.tensor_scalar(mxs[0:16], mxs[0:16], sk8c, None, ALU.max), reads=[sm.b, cst.b], writes=[sm.b])
        S.op("dve", lambda e: e.tensor_scalar(ngs[0:16], mxs[0:16], -0.125, None, ALU.mult), reads=[sm.b], writes=[sm.b])
        S.op("dve", lambda e: e.scalar_tensor_tensor(t3[0:16], t3[0:16], 0.125, ngs[0:16].unsqueeze(2).to_broadcast([16, 16, 128]), ALU.mult, ALU.add),
             reads=[tsm.b, sm.b], writes=[tsm.b])
        S.op("act", lambda e: e.activation(tsm.ap[0:16], tsm.ap[0:16], AF.Exp), reads=[tsm.b], writes=[tsm.b])
        S.op("dve", lambda e: e.tensor_reduce(rss[0:16], t3[0:16], AX.X, ALU.add), reads=[tsm.b], writes=[sm.b])
        S.op("dve", lambda e: e.tensor_scalar(dns[0:16], ngs[0:16], cst.ap[0:16, 128:129], None, ALU.add), reads=[sm.b, cst.b], writes=[sm.b])
        S.op("act", lambda e: e.activation(dns[0:16], dns[0:16], AF.Exp), reads=[sm.b], writes=[sm.b])
        S.op("dve", lambda e: e.tensor_tensor(dns[0:16], dns[0:16], rss[0:16], ALU.add), reads=[sm.b], writes=[sm.b])
        S.op("dve", lambda e: e.reciprocal(dns[0:16], dns[0:16]), reads=[sm.b], writes=[sm.b])
        Pb = A.alloc("Pb", 2048, BF16); Pb3 = Pb.v("p (b s) -> p b s", b=16)
        S.op("dve", lambda e: e.tensor_tensor(Pb3[0:16], t3[0:16], dns[0:16].unsqueeze(2).to_broadcast([16, 16, 128]), ALU.mult), reads=[tsm.b, sm.b], writes=[Pb.b])
        pvb = bank16(1).rearrange("p (b h) -> p b h", h=16)
        S.op("pe", [tp(pvb[:, b, :], Pb3[0:16, b, :], identb[0:16, 0:16]) for b in range(16)], reads=[Pb.b, cB.b], writes=[PB[1]])
        PTs = A.alloc("PTs", 256, BF16); PTs3 = PTs.v("p (b h) -> p b h", b=16)
        S.op("act", lambda e: e.activation(PTs3, pvb[:, 0:16, :], AF.Copy), reads=[PB[1]], writes=[PTs.b])
        ah = A.alloc("ah", 16 * 64); ah3 = ah.v("p (b d) -> p b d", b=16)
        t4 = A.alloc("t4s", 8 * 256)
        for half in range(2):
            for bb in range(8):
                b = half * 8 + bb
                S.op("pe", mm(ps[0:16, 4 * 512 + bb * 256:4 * 512 + (bb + 1) * 256], PTs3[:, b, :], Vh3[:, b, :], True, True), reads=[PTs.b, Vh.b], writes=[PB[4 + bb // 2]])
            S.op("dve", lambda e: e.tensor_tensor(t4.ap[0:16].rearrange("p (b k d) -> p b k d", b=8, k=4),
                                                  ps[0:16, 2048:4096].rearrange("p (b k d) -> p b k d", b=8, k=4),
                                                  cst.ap[0:16, 129:133].unsqueeze(1).unsqueeze(3).to_broadcast([16, 8, 4, 64]), ALU.mult),
                 reads=[PB[4], PB[5], PB[6], PB[7], cst.b], writes=[t4.b])
            S.op("dve", lambda e, half=half: e.tensor_reduce(ah3[0:16, half * 8:(half + 1) * 8, :], t4.ap[0:16].rearrange("p (b k d) -> p b d k", b=8, k=4), AX.X, ALU.add),
                 reads=[t4.b], writes=[ah.b])
        S.dma("sp", [(att_d.rearrange("b (h d) -> h b d", h=16), ah3[0:16])], reads=[ah.b], writes=[ATTD])
        S.dma("sp", [(cs.ap[0:16, 0:1024], att_d)], reads=[ATTD], writes=[cs.b])
        S.barrier()
        A.release(m1)
        S.dma("sp", [(gb.ap[0:16, 0:1024], gatt.to_broadcast([16, 1024])), (gb.ap[0:16, 1024:2048], gssm.to_broadcast([16, 1024]))], writes=[gb.b])
        rstd = rms_rstd(cs.ap[0:16, 0:1024], 1024, [cs.b])
        S.op("dve", lambda e: e.scalar_tensor_tensor(cs16.ap[0:16, 0:1024], cs.ap[0:16, 0:1024], rstd[0:16], gb.ap[0:16, 0:1024], ALU.mult, ALU.mult),
             reads=[cs.b, sm.b, gb.b], writes=[cs16.b])
        cwb = A.alloc("cwb", 5 * 1536)
        S.dma("sp", [(cwb.ap[0:16, 0:6144], convw_raw.to_broadcast([16, 6144])), (cwb.ap[0:16, 6144:7680], convb_raw.to_broadcast([16, 1536]))], writes=[cwb.b])
        sc_ = A.alloc("sconv", 3 * 1536)
        S.dma("sp", [(sc_.ap[0:16, :], sconv_d)], writes=[sc_.b])
        xc = A.alloc("xc", 1536); xc2 = A.alloc("xc2", 1536)
        S.op("dve", lambda e: e.tensor_tensor(xc.ap[0:16], P[0:16, 2560:4096], cwb.ap[0:16, 3 * 1536:4 * 1536], ALU.mult), reads=[pj.b, cwb.b], writes=[xc.b])
        S.op("dve", lambda e: e.tensor_tensor(xc.ap[0:16], xc.ap[0:16], cwb.ap[0:16, 6144:7680], ALU.add), reads=[xc.b, cwb.b], writes=[xc.b])
        for i in range(3):
            S.op("dve", lambda e, i=i: e.tensor_tensor(xc2.ap[0:16], sc_.ap[0:16, i * 1536:(i + 1) * 1536], cwb.ap[0:16, i * 1536:(i + 1) * 1536], ALU.mult), reads=[sc_.b, cwb.b], writes=[xc2.b])
            S.op("dve", lambda e: e.tensor_tensor(xc.ap[0:16], xc.ap[0:16], xc2.ap[0:16], ALU.add), reads=[xc.b, xc2.b], writes=[xc.b])
        silu_to(xa.ap[0:16], xc.ap[0:16], 1536, [xc.b], [xa.b], np_=16, tmp=xc2)
        S.barrier()
        A.release(m1)
        tz = A.alloc("tmpz", 1536)
        x0 = small(16); ax = small(16); dts = small(16); dAs = small(16)
        S.op("dve", lambda e: e.tensor_tensor(x0[0:16], P[0:16, 4096:4112], dtb_bc[0:16], ALU.add), reads=[pj.b, hv.b], writes=[sm.b])
        S.op("act", lambda e: e.activation(ax[0:16], x0[0:16], AF.Abs), reads=[sm.b], writes=[sm.b])
        S.op("act", lambda e: e.activation(ax[0:16], ax[0:16], AF.Exp, scale=-1.0), reads=[sm.b], writes=[sm.b])
        S.op("dve", lambda e: e.tensor_scalar(ax[0:16], ax[0:16], 1.0, None, ALU.add), reads=[sm.b], writes=[sm.b])
        S.op("act", lambda e: e.activation(ax[0:16], ax[0:16], AF.Ln), reads=[sm.b], writes=[sm.b])
        S.op("dve", lambda e: e.scalar_tensor_tensor(dts[0:16], x0[0:16], 0.0, ax[0:16], ALU.max, ALU.add), reads=[sm.b], writes=[sm.b])
        S.op("dve", lambda e: e.tensor_tensor(dAs[0:16], dts[0:16], a_bc.ap[0:16], ALU.mult), reads=[sm.b, a_bc.b], writes=[sm.b])
        S.op("act", lambda e: e.activation(dAs[0:16], dAs[0:16], AF.Exp), reads=[sm.b], writes=[sm.b])
        XE = A.alloc("XE", 2048)
        S.op("dve", lambda e: e.tensor_tensor(XE.ap[0:16, 0:1024].rearrange("p (h d) -> p h d", h=16), xa.ap[0:16, 0:1024].rearrange("p (h d) -> p h d", h=16),
                                              dts[0:16].unsqueeze(2).to_broadcast([16, 16, 64]), ALU.mult), reads=[xa.b, sm.b], writes=[XE.b])
        S.op("dve", lambda e: e.tensor_copy(XE.ap[0:16, 1024:2048].rearrange("p (h d) -> p h d", h=16), dAs[0:16].unsqueeze(2).to_broadcast([16, 16, 64])),
             reads=[sm.b], writes=[XE.b])
        XT = A.alloc("XT", 16 * 16); XT3 = XT.v("p (j b) -> p j b", j=16)
        S.op("pe", [tp(bank(0, 16, j * 16), XE.ap[0:16, j * 128:(j + 1) * 128], identf[0:16, 0:16]) for j in range(16)], reads=[XE.b, cF.b], writes=[PB[0]])
        S.op("act", lambda e: e.activation(XT.ap, bank(0, 256), AF.Copy), reads=[PB[0]], writes=[XT.b])
        yT = A.alloc("yT", 8 * 16); yT3 = yT.v("p (j b) -> p j b", j=8)
        h0 = [A.alloc("h0_%d" % i, 1024) for i in range(2)]
        h1 = [A.alloc("h1_%d" % i, 1024) for i in range(2)]
        for b in range(16):
            ht = h0[b % 2]; hn = h1[b % 2]
            S.dma("sp", [(ht.v("p (j n) -> p j n", j=8), sssm_d[b].rearrange("(j q) n -> q j n", q=128))], writes=[ht.b])
            S.op("pe", mm(bank(3), sel3[0:16, b, :], xa.ap[0:16, 1024:1536], True, True), reads=[sel.b, xa.b], writes=[PB[3]])
            S.op("dve", lambda e, ht=ht, b=b: e.tensor_tensor(ht.v("p (j n) -> p j n", j=8), ht.v("p (j n) -> p j n", j=8),
                                                             XT3[:, 8:16, b].unsqueeze(2).to_broadcast([128, 8, 128]), ALU.mult), reads=[ht.b, XT.b], writes=[ht.b])
            S.op("dve", lambda e, hn=hn, b=b: e.tensor_tensor(hn.v("p (g r n) -> p g r n", g=2, r=4),
                                                             bank(3, 256).rearrange("p (g n) -> p g n", g=2).unsqueeze(2).to_broadcast([128, 2, 4, 128]),
                                                             XT3[:, 0:8, b].rearrange("p (g r) -> p g r", g=2).unsqueeze(3).to_broadcast([128, 2, 4, 128]), ALU.mult),
                 reads=[PB[3], XT.b], writes=[hn.b])
            S.op("dve", lambda e, hn=hn, ht=ht: e.tensor_tensor(hn.ap, hn.ap, ht.ap, ALU.add), reads=[hn.b, ht.b], writes=[hn.b])
            S.dma("sp", [(ssms_out[b].rearrange("(j q) n -> q j n", q=128), hn.v("p (j n) -> p j n", j=8))], reads=[hn.b])
            S.op("dve", lambda e, hn=hn, ht=ht: e.tensor_tensor(ht.v("p (g r n) -> p g r n", g=2, r=4), hn.v("p (g r n) -> p g r n", g=2, r=4),
                                                               bank(3, 256, 256).rearrange("p (g n) -> p g n", g=2).unsqueeze(2).to_broadcast([128, 2, 4, 128]), ALU.mult),
                 reads=[hn.b, PB[3]], writes=[ht.b])
            S.op("dve", lambda e, ht=ht, b=b: e.tensor_reduce(yT3[:, :, b], ht.v("p (j n) -> p j n", j=8), AX.X, ALU.add), reads=[ht.b], writes=[yT.b])
        S.op("pe", [tp(ps[0:16, 2 * 512 + j * 128:2 * 512 + (j + 1) * 128], yT3[:, j, :], identf) for j in range(8)], reads=[yT.b, cF.b], writes=[PB[2], PB[3]])
        ys = A.alloc("y_s", 1024)
        dkb = A.alloc("dkb", 1024)
        S.dma("sp", [(dkb.ap[0:16, :], dskrow.to_broadcast([16, 1024]))], writes=[dkb.b])
        S.op("dve", lambda e: e.tensor_tensor(ys.ap[0:16], xa.ap[0:16, 0:1024], dkb.ap[0:16], ALU.mult), reads=[xa.b, dkb.b], writes=[ys.b])
        S.op("dve", lambda e: e.tensor_tensor(ys.ap[0:16], ys.ap[0:16], ps[0:16, 1024:2048], ALU.add), reads=[ys.b, PB[2], PB[3]], writes=[ys.b])
        zt = A.alloc("z_s", 1024)
        silu_to(zt.ap[0:16], P[0:16, 1536:2560], 1024, [pj.b], [zt.b], np_=16, tmp=tz)
        S.op("dve", lambda e: e.tensor_tensor(ys.ap[0:16], ys.ap[0:16], zt.ap[0:16], ALU.mult), reads=[ys.b, zt.b], writes=[ys.b])
        rstd2 = rms_rstd(ys.ap[0:16], 1024, [ys.b])
        S.op("dve", lambda e: e.scalar_tensor_tensor(cs16.ap[0:16, 1024:2048], ys.ap[0:16], rstd2[0:16], gb.ap[0:16, 1024:2048], ALU.mult, ALU.mult),
             reads=[ys.b, sm.b, gb.b], writes=[cs16.b])
        transpose_to(cs16, catsT3, 0, np_=16)
        S.barrier()
        A.release(m)
    m_gm = A.mark()
    gm1 = A.alloc("gm1", D)
    sh1 = A.alloc("sh1", D)
    load_mod_bc(gm1, 1)
    load_mod_bc(sh1, 0)
    mS = A.mark()
    u2T_d = dscr("u2T_d", [16, 128, NOWN], BF16)
    CATD = Buf("cat_d"); X1D = Buf("x1_d"); U2TD = Buf("u2T_d"); HTD = Buf("hT_d")
    w_dt = A.alloc("w_dt", 16 * 16, BF16)
    w_dt3 = w_dt.v("p (k n) -> p k n", k=16)
    S.dma("pool", [(w_dt3, w_in_v[:, :, 4096:4112])], writes=[w_dt.b])
    A_mrow = [A.alloc("mrow_t", 512)]
    A_xp = [A.alloc("xp%d" % i, 515) for i in range(2)]
    A_acc = [A.alloc("acc%d" % i, 512) for i in range(2)]
    A_st = [A.alloc("silt", 512)]
    A_pt = [A.alloc("ptmp", 512)]
    A_xst = [A.alloc("xs_tok", 1024)]
    A_bt = [A.alloc("Btok", 256, BF16)]
    A_xd = [A.alloc("Xd", 1024, BF16)]

    def conv_tiles(uT3, uTb, T, col_tok0, dests, cts, wfn, load_mask=True):
        for ct in cts:
            bi = 4 + ct % 4
            o = bank(bi, T)
            wb = wfn(ct, 0)[1]
            S.op("pe", [mm(o, wfn(ct, kc)[0], uT3[:, kc, 0:T], kc == 0, kc == 15) for kc in range(16)],
                 reads=[wb, uTb], writes=[PB[bi]])
            xp = A_xp[ct % 2]
            S.op("act", lambda e, xp=xp, ct=ct: e.activation(xp.ap[:, 0:3], halo3[:, ct, :], AF.Copy), reads=[halo.b], writes=[xp.b])
            S.op("act", lambda e, xp=xp, o=o: e.activation(xp.ap[:, 3:3 + T], o, AF.Copy), reads=[PB[bi]], writes=[xp.b])
            S.op("act", lambda e, xp=xp, ct=ct: e.activation(halo3[:, ct, :], xp.ap[:, T:T + 3], AF.Copy), reads=[xp.b], writes=[halo.b])
            acc = A_acc[ct % 2]
            S.op("act", lambda e, xp=xp, acc=acc, ct=ct: e.activation(acc.ap[:, 0:T], xp.ap[:, 3:3 + T], AF.Identity, bias=cbi.ap[:, ct:ct + 1], scale=cw.ap[:, ct * 4 + 3:ct * 4 + 4]),
                 reads=[xp.b, cw.b, cbi.b], writes=[acc.b])
            for i in (2, 1, 0):
                S.op("dve", lambda e, xp=xp, acc=acc, ct=ct, i=i: e.scalar_tensor_tensor(acc.ap[:, 0:T], xp.ap[:, i:i + T], cw.ap[:, ct * 4 + i:ct * 4 + i + 1], acc.ap[:, 0:T], ALU.mult, ALU.add),
                     reads=[xp.b, cw.b, acc.b], writes=[acc.b])
            dst, db = dests(ct)
            silu_to(dst, acc.ap[:, 0:T], T, [acc.b], [db], tmp=A_st[0])

    def dt_block(uT3, uTb, c0, blk_ext, own=False):
        smc[0] = 0
        o = bank(0, 16)
        S.op("pe", [mm(o, uT3[:, kc, c0:c0 + 128], w_dt3[:, kc, :], kc == 0, kc == 15) for kc in range(16)],
             reads=[uTb, w_dt.b], writes=[PB[0]])
        x0 = small(16); ax = small(16); dt = small(16); dA = small(16)
        S.op("dve", lambda e: e.tensor_tensor(x0, o, dtb_bc, ALU.add), reads=[PB[0], hv.b], writes=[sm.b])
        S.op("act", lambda e: e.activation(ax, x0, AF.Abs), reads=[sm.b], writes=[sm.b])
        S.op("act", lambda e: e.activation(ax, ax, AF.Exp, scale=-1.0), reads=[sm.b], writes=[sm.b])
        S.op("dve", lambda e: e.tensor_scalar(ax, ax, 1.0, None, ALU.add), reads=[sm.b], writes=[sm.b])
        S.op("act", lambda e: e.activation(ax, ax, AF.Ln), reads=[sm.b], writes=[sm.b])
        S.op("dve", lambda e: e.scalar_tensor_tensor(dt, x0, 0.0, ax, ALU.max, ALU.add), reads=[sm.b], writes=[sm.b])
        S.op("dve", lambda e: e.tensor_scalar(dt, dt, mt.ap[:, blk_ext:blk_ext + 1], None, ALU.mult), reads=[sm.b, mt.b], writes=[sm.b])
        S.op("dve", lambda e: e.tensor_tensor(dA, dt, a_bc.ap, ALU.mult), reads=[sm.b, a_bc.b], writes=[sm.b])
        o2 = bank(0, 32, 32)
        S.op("pe", [mm(o2[:, 0:16], Umat, dA, True, True), mm(o2[:, 16:32], onesf, dA, True, True)], reads=[cF.b, sm.b], writes=[PB[0]])
        at = small(32)
        S.op("act", lambda e: e.activation(at, o2, AF.Copy), reads=[PB[0]], writes=[sm.b])
        acum, tot = at[:, 0:16], at[:, 16:32]
        de = small(16); cd = small(16); w1 = small(16)
        S.op("dve", lambda e: e.tensor_tensor(de, tot, acum, ALU.subtract), reads=[sm.b], writes=[sm.b])
        S.op("act", lambda e: e.activation(de, de, AF.Exp), reads=[sm.b], writes=[sm.b])
        S.op("act", lambda e: e.activation(cd, tot, AF.Exp), reads=[sm.b], writes=[sm.b])
        S.op("dve", lambda e: e.tensor_tensor(w1, dt, de, ALU.mult), reads=[sm.b], writes=[sm.b])
        r = dict(dt=dt, dA=dA, acum=acum, tot=tot, cd=cd, w1=w1)
        if own:
            ea = small(16)
            S.op("act", lambda e: e.activation(ea, acum, AF.Exp), reads=[sm.b], writes=[sm.b])
            r["ea"] = ea
        return r

    def state_part1(xsT3, xsTb, BT3, BTb, c0):
        xt_ = A_xst[0]
        S.op("pe", [tp(bank(2 + i // 4, 128, (i % 4) * 128), xsT3[:, i, c0:c0 + 128], identf) for i in range(8)],
             reads=[xsTb, cF.b], writes=[PB[2], PB[3]])
        S.op("act", lambda e: e.activation(xt_.ap[:, 0:512], bank(2), AF.Copy), reads=[PB[2]], writes=[xt_.b])
        S.op("act", lambda e: e.activation(xt_.ap[:, 512:1024], bank(3), AF.Copy), reads=[PB[3]], writes=[xt_.b])
        S.op("pe", [tp(bank(0, 128, 128 + g * 128), BT3[:, g, c0:c0 + 128], identf) for g in range(2)], reads=[BTb, cF.b], writes=[PB[0]])
        Bt = A_bt[0]
        S.op("act", lambda e: e.activation(Bt.ap, bank(0, 256, 128), AF.Copy), reads=[PB[0]], writes=[Bt.b])
        return xt_, Bt

    def state_part2(xt_, Bt, sc_, keep_hb):
        Xd = A_xd[0]
        S.op("dve", lambda e: e.tensor_tensor(Xd.v("p (h d) -> p h d", h=16), xt_.v("p (h d) -> p h d", h=16),
                                              sc_["w1"].unsqueeze(2).to_broadcast([128, 16, 64]), ALU.mult),
             reads=[xt_.b, sm.b], writes=[Xd.b])
        S.op("pe", [mm(bank(6 + g), Bt.ap[:, g * 128:(g + 1) * 128], Xd.ap[:, g * 512:(g + 1) * 512], True, True) for g in range(2)],
             reads=[Bt.b, Xd.b], writes=[PB[6], PB[7]])
        S.op("dve", lambda e: e.tensor_tensor(hst.v("p (h d) -> p h d", h=16), hst.v("p (h d) -> p h d", h=16),
                                              sc_["cd"].unsqueeze(2).to_broadcast([128, 16, 64]), ALU.mult),
             reads=[hst.b, sm.b], writes=[hst.b])
        S.op("dve", lambda e: e.tensor_tensor(hst.ap[:, 0:512], hst.ap[:, 0:512], bank(6), ALU.add), reads=[hst.b, PB[6]], writes=[hst.b])
        S.op("dve", lambda e: e.tensor_tensor(hst.ap[:, 512:1024], hst.ap[:, 512:1024], bank(7), ALU.add), reads=[hst.b, PB[7]], writes=[hst.b])
        if keep_hb:
            S.op("act", lambda e: e.activation(hb.ap, hst.ap, AF.Copy), reads=[hst.b], writes=[hb.b])

    def load_x_norm(tok0, nblk, uT3, masked=False):
        for b in range(nblk):
            xt = xb[b % 2]
            S.dma("sp", [(xt.ap, xext[tok0 + b * 128:tok0 + (b + 1) * 128, :])], writes=[xt.b])
            blk = tok0 // 128 + b
            norm_mod_T(xt, gm1, sh1, uT3, b * 128, mask=(mt.ap[:, blk:blk + 1] if masked else None))

    mA = A.mark()
    wx = A.alloc("wx", 16 * 1536, BF16)
    wx3 = wx.v("p (k n) -> p k n", k=16)
    S.dma("pool", [(wx3[:, 0:8, :], w_in_v[:, 0:8, 2560:4096]), (wx3[:, 8:16, :], w_in_v[:, 8:16, 2560:4096])], writes=[wx.b])
    uTa = A.alloc("uTa", 16 * 512, BF16)
    uTa3 = reg3(uTa, 16)
    xsTa = A.alloc("xsTa", 8 * 512)
    xsTa3 = xsTa.v("p (k n) -> p k n", k=8)
    BTa = A.alloc("BTa", 2 * 512)
    BTa3 = BTa.v("p (k n) -> p k n", k=2)

    CTd = A.alloc("CTdummy", 2 * 512)
    CTd3 = CTd.v("p (k n) -> p k n", k=2)

    def destsA(ct):
        if ct < 8:
            return xsTa3[:, ct, 0:512], xsTa.b
        if ct < 10:
            return BTa3[:, ct - 8, 0:512], BTa.b
        return CTd3[:, ct - 10, 0:512], CTd.b

    for g in range(NPRE_G - n_pre_groups, NPRE_G):
        load_x_norm(g * 512, 4, uTa3, masked=True)
        conv_tiles(uTa3, uTa.b, 512, g * 512, destsA, range(12 if g == NPRE_G - 1 else 10), lambda ct, kc: (wx3[:, kc, ct * 128:(ct + 1) * 128], wx.b))
        for b in range(4):
            sc_ = dt_block(uTa3, uTa.b, b * 128, g * 4 + b)
            xt_, Bt = state_part1(xsTa3, xsTa.b, BTa3, BTa.b, b * 128)
            state_part2(xt_, Bt, sc_, keep_hb=(g == NPRE_G - 1 and b == 3))
    S.barrier()
    A.release(mA)

    OWN0 = NEXT - NOWN
    def phase_B1b():
        m = A.mark()
        slots = [A.alloc("ws%d" % i, 16 * 512, BF16) for i in range(2)]
        s3 = [t.v("p (k n) -> p k n", k=16) for t in slots]
        uT = A.alloc("uTb", 16 * 256, BF16); uT3 = reg3(uT, 16)
        xsT = A.alloc("xsTb", 8 * 256); xsT3 = xsT.v("p (k n) -> p k n", k=8)
        BT = A.alloc("BTb", 2 * 256); BT3 = BT.v("p (k n) -> p k n", k=2)
        BTh = A.alloc("BTh", 2 * 256, BF16); BTh3 = BTh.v("p (k n) -> p k n", k=2)
        CTh = A.alloc("CTh", 2 * 256, BF16); CTh3 = CTh.v("p (k n) -> p k n", k=2)
        zs = A.alloc("zs", 2 * 1024); zs3 = zs.v("p (b n) -> p b n", b=2)
        gs = A.alloc("gssm", 1024)
        S.dma("sp", [(gs.ap, gssm.to_broadcast([128, 1024]))], writes=[gs.b])
        Rt = A.alloc("Rt", 2048); R3 = Rt.v("p (h l) -> p h l", h=16)
        Et = A.alloc("Et", 2048, BF16)
        MTt = A.alloc("MTt", 2048, BF16); MT3 = MTt.v("p (h l) -> p h l", h=16)
        CBm = A.alloc("CBm", 256)
        Xb = A.alloc("Xb", 1024, BF16)
        yt = A.alloc("yt", 1024)
        y2 = A.alloc("y2", 1024)
        sso = A.alloc("sso", 1024, BF16)
        si = [0]

        def next_slot(src_cols, ncols=512):
            i = si[0] % 2
            si[0] += 1
            S.dma("pool", [(s3[i][:, :, 0:ncols], w_in_v[:, :, src_cols:src_cols + ncols])], writes=[slots[i].b])
            return s3[i], slots[i].b

        def dests(ct):
            if ct < 8:
                return xsT3[:, ct, 0:256], xsT.b
            if ct < 10:
                return BT3[:, ct - 8, 0:256], BT.b
            return CTh3[:, ct - 10, 0:256], CTh.b

        for og in range(NOWN // 256):
            tok0 = OWN0 + og * 256
            load_x_norm(tok0, 2, uT3)
            for pc in range(3):
                w3, wb = next_slot(2560 + pc * 512)
                conv_tiles(uT3, uT.b, 256, tok0, dests, range(pc * 4, pc * 4 + 4),
                           lambda ct, kc, w3=w3, wb=wb, pc=pc: (w3[:, kc, (ct - pc * 4) * 128:(ct - pc * 4 + 1) * 128], wb), load_mask=(pc == 0))
            S.op("act", lambda e: e.activation(BTh.ap, BT.ap, AF.Copy), reads=[BT.b], writes=[BTh.b])
            for pc in range(2):
                w3, wb = next_slot(1536 + pc * 512)
                for b in range(2):
                    bi = 4 + (pc * 2 + b) % 4
                    S.op("pe", [mm(bank(bi), uT3[:, kc, b * 128:(b + 1) * 128], w3[:, kc, :], kc == 0, kc == 15) for kc in range(16)],
                         reads=[uT.b, wb], writes=[PB[bi]])
                    silu_to(zs3[:, b, pc * 512:(pc + 1) * 512], bank(bi), 512, [PB[bi]], [zs.b], tmp=A_st[0])
            for b in range(2):
                c0 = b * 128
                sc_ = dt_block(uT3, uT.b, c0, (tok0 // 128) + b, own=True)
                xt_, Bt = state_part1(xsT3, xsT.b, BT3, BT.b, c0)
                S.op("dve", lambda e: e.tensor_tensor(R3, Umat.unsqueeze(1).to_broadcast([128, 16, 128]),
                                                      sc_["dA"].unsqueeze(2).to_broadcast([128, 16, 128]), ALU.mult),
                     reads=[cF.b, sm.b], writes=[Rt.b])
                for q in range(4):
                    S.op("pe", mm(bank(4 + q), UTmat, Rt.ap[:, q * 512:(q + 1) * 512], True, True), reads=[cF.b, Rt.b], writes=[PB[4 + q]])
                    S.op("act", lambda e, q=q: e.activation(Et.ap[:, q * 512:(q + 1) * 512], bank(4 + q), AF.Exp), reads=[PB[4 + q]], writes=[Et.b])
                S.op("pe", [mm(bank(0, 128, 256 + g * 128), BTh3[:, g, c0:c0 + 128], CTh3[:, g, c0:c0 + 128], True, True) for g in range(2)],
                     reads=[BTh.b, CTh.b], writes=[PB[0]])
                S.op("dve", lambda e: e.tensor_tensor(CBm.v("p (g l) -> p g l", g=2), bank(0, 256, 256).rearrange("p (g l) -> p g l", g=2),
                                                      caus01.unsqueeze(1).to_broadcast([128, 2, 128]), ALU.mult),
                     reads=[PB[0], cF.b], writes=[CBm.b])
                S.op("dve", lambda e: e.tensor_tensor(MTt.v("p (g r l) -> p g r l", g=2, r=8), Et.v("p (g r l) -> p g r l", g=2, r=8),
                                                      CBm.v("p (g l) -> p g l", g=2).unsqueeze(2).to_broadcast([128, 2, 8, 128]), ALU.mult),
                     reads=[Et.b, CBm.b], writes=[MTt.b])
                S.op("dve", lambda e: e.tensor_tensor(Xb.v("p (h d) -> p h d", h=16), xt_.v("p (h d) -> p h d", h=16),
                                                      sc_["dt"].unsqueeze(2).to_broadcast([128, 16, 64]), ALU.mult),
                     reads=[xt_.b, sm.b], writes=[Xb.b])
                S.op("pe", [mm(bank(2 + h // 8, 64, (h % 8) * 64), MT3[:, h, :], Xb.ap[:, h * 64:(h + 1) * 64], True, True) for h in range(16)],
                     reads=[MTt.b, Xb.b], writes=[PB[2], PB[3]])
                S.op("pe", [mm(bank(4 + g), CTh3[:, g, c0:c0 + 128], hb.ap[:, g * 512:(g + 1) * 512], True, True) for g in range(2)],
                     reads=[CTh.b, hb.b], writes=[PB[4], PB[5]])
                for g in range(2):
                    S.op("dve", lambda e, g=g: e.tensor_tensor(yt.ap[:, g * 512:(g + 1) * 512].rearrange("p (h d) -> p h d", h=8),
                                                               bank(4 + g).rearrange("p (h d) -> p h d", h=8),
                                                               sc_["ea"][:, g * 8:(g + 1) * 8].unsqueeze(2).to_broadcast([128, 8, 64]), ALU.mult),
                         reads=[PB[4 + g], sm.b], writes=[yt.b])
                    S.op("dve", lambda e, g=g: e.tensor_tensor(yt.ap[:, g * 512:(g + 1) * 512], yt.ap[:, g * 512:(g + 1) * 512], bank(2 + g), ALU.add),
                         reads=[PB[2 + g], yt.b], writes=[yt.b])
                S.op("dve", lambda e: e.tensor_tensor(y2.v("p (h d) -> p h d", h=16), xt_.v("p (h d) -> p h d", h=16),
                                                      dsk_bc.unsqueeze(2).to_broadcast([128, 16, 64]), ALU.mult),
                     reads=[xt_.b, hv.b], writes=[y2.b])
                S.op("dve", lambda e: e.tensor_tensor(yt.ap, yt.ap, y2.ap, ALU.add), reads=[yt.b, y2.b], writes=[yt.b])
                S.op("dve", lambda e, b=b: e.tensor_tensor(yt.ap, yt.ap, zs3[:, b, :], ALU.mult), reads=[yt.b, zs.b], writes=[yt.b])
                rstd = rms_rstd(yt.ap, 1024, [yt.b])
                S.op("dve", lambda e, rstd=rstd: e.scalar_tensor_tensor(sso.ap, yt.ap, rstd, gs.ap, ALU.mult, ALU.mult), reads=[yt.b, sm.b, gs.b], writes=[sso.b])
                tk = og * 256 + c0
                S.dma("sp", [(cat_d[tk:tk + 128, 1024:2048], sso.ap)], reads=[sso.b], writes=[CATD])
                state_part2(xt_, Bt, sc_, keep_hb=True)
        so = yt
        S.op("pe", [tp(bank(2 + i // 4, 128, (i % 4) * 128), hst.ap[:, i * 128:(i + 1) * 128], identf) for i in range(8)],
             reads=[hst.b, cF.b], writes=[PB[2], PB[3]])
        S.op("act", lambda e: e.activation(so.ap[:, 0:512], bank(2), AF.Copy), reads=[PB[2]], writes=[so.b])
        S.op("act", lambda e: e.activation(so.ap[:, 512:1024], bank(3), AF.Copy), reads=[PB[3]], writes=[so.b])
        S.dma("sp", [(ssm_out.rearrange("(i p) n -> p i n", p=128), so.v("p (i n) -> p i n", i=8))], reads=[so.b])
        co = Rt
        S.op("pe", [tp(ps[0:3, 4 * 512 + ct * 128:4 * 512 + (ct + 1) * 128], halo3[:, ct, :], identf) for ct in range(12)],
             reads=[halo.b, cF.b], writes=[PB[4], PB[5], PB[6]])
        S.op("act", lambda e: e.activation(co.ap[0:3, 0:1536], ps[0:3, 4 * 512:4 * 512 + 1536], AF.Copy), reads=[PB[4], PB[5], PB[6]], writes=[co.b])
        S.dma("sp", [(conv_out, co.ap[0:3, 0:1536])], reads=[co.b])
        S.barrier()
        A.release(m)

    phase_B1b()
    A.release(mS)

    def phase_B1a():
        m = A.mark()
        slots = [A.alloc("wq%d" % i, 16 * 512, BF16) for i in range(2)]
        s3 = [t.v("p (k n) -> p k n", k=16) for t in slots]
        uT = A.alloc("uTq", 16 * 512, BF16); uT3 = reg3(uT, 16)
        uTh = A.alloc("uTh", 16 * 128, BF16); uTh3 = reg3(uTh, 16)
        qT = A.alloc("qT", 8 * 512, BF16); qT3 = qT.v("p (k n) -> p k n", k=8)
        kT = A.alloc("kT", 2 * 4 * 640, BF16); kT4 = kT.v("p (e k n) -> p e k n", e=2, k=4)
        Vt = A.alloc("Vt", 5 * 256, BF16); V3 = Vt.v("p (s n) -> p s n", s=5)
        ab1t = A.alloc("ab", 4096)
        S.dma("sp", [(ab1t.ap, abias0)], writes=[ab1t.b])
        ga = A.alloc("gatt", 1024)
        S.dma("sp", [(ga.ap, gatt.to_broadcast([128, 1024]))], writes=[ga.b])
        sk8 = A.alloc("sk8", 16)
        S.op("dve", lambda e: e.tensor_scalar(sk8.ap, sink_bc, 8.0, None, ALU.mult), reads=[hv.b], writes=[sk8.b])
        tS = A.alloc("tS", 1024)
        Pt = A.alloc("Pt", 1024, BF16)
        PT = A.alloc("PT", 1024, BF16)
        at_ = A.alloc("att", 1024)
        ao = A.alloc("atto", 1024, BF16)
        kvo = A.alloc("kvo", 512)
        si = [0]

        def kv_load():
            j = 0
            S.dma("pool", [(s3[j], w_in_v[:, :, 1024:1536])], writes=[slots[j].b])
            return (s3[j], slots[j].b)

        def k_dup(wv, eo):
            i = 1
            d4 = slots[i].v("p (k v n) -> p k v n", k=16, v=4)
            src4 = wv[0][:, :, 0:256].rearrange("p k (v n) -> p k v n", v=4)
            S.op("dve", lambda e: e.memset(slots[i].ap, 0.0), writes=[slots[i].b])
            for kc in range(16):
                if kc % 2 == 0:
                    S.op("act", lambda e, kc=kc: e.activation(d4[:, kc, :, eo * 64:eo * 64 + 64], src4[:, kc, :, :], AF.Copy), reads=[wv[1]], writes=[slots[i].b])
                else:
                    S.op("dve", lambda e, kc=kc: e.tensor_copy(d4[:, kc, :, eo * 64:eo * 64 + 64], src4[:, kc, :, :]), reads=[wv[1]], writes=[slots[i].b])
            return (s3[i], slots[i].b)

        def k_proj(wk, eo, u3, ub_, T, col0):
            for kv in range(4):
                bi = 4 + kv
                S.op("pe", [mm(bank(bi, T), wk[0][:, kc, kv * 128:(kv + 1) * 128], u3[:, kc, 0:T], kc == 0, kc == 15) for kc in range(16)],
                     reads=[wk[1], ub_], writes=[PB[bi]])
                S.op("dve" if kv % 2 else "act",
                     (lambda e, kv=kv, bi=bi: e.tensor_copy(kT4[:, eo, kv, col0:col0 + T], bank(bi, T))) if kv % 2 else
                     (lambda e, kv=kv, bi=bi: e.activation(kT4[:, eo, kv, col0:col0 + T], bank(bi, T), AF.Copy)), reads=[PB[bi]], writes=[kT.b])

        vcnt = [0]

        def v_proj(wv, u3, ub_, c0, slot, keep32=None):
            bi = 2 + vcnt[0] % 2
            vcnt[0] += 1
            S.op("pe", [mm(bank(bi, 256), u3[:, kc, c0:c0 + 128], wv[0][:, kc, 256:512], kc == 0, kc == 15) for kc in range(16)],
                 reads=[wv[1], ub_], writes=[PB[bi]])
            S.op("dve", lambda e: e.tensor_copy(V3[:, slot, :], bank(bi, 256)), reads=[PB[bi]], writes=[Vt.b])
            if keep32 is not None:
                S.op("dve", lambda e: e.tensor_copy(keep32, bank(bi, 256)), reads=[PB[bi]], writes=[kvo.b])

        import os as _os
        NSTEP = int(_os.environ.get("B1A_STEPS", "99"))
        for og in range(4):
            tok0 = OWN0 + og * 512
            if NSTEP < 2: break
            if og == 0:
                xt = xb[0]
                S.dma("sp", [(xt.ap, xext[OWN0 - 128:OWN0, :])], writes=[xt.b])
                norm_mod_T(xt, gm1, sh1, uTh3, 0)
            if NSTEP < 3: break
            load_x_norm(tok0, 4, uT3)
            if NSTEP < 4: break
            wv = kv_load()
            for eo in range(2):
                wk = k_dup(wv, eo)
                if og == 0:
                    k_proj(wk, eo, uTh3, uTh.b, 128, 0)
                k_proj(wk, eo, uT3, uT.b, 512, 128)
            if og == 0:
                v_proj(wv, uTh3, uTh.b, 0, 0)
            if NSTEP < 7: break
            for b in range(int(_os.environ.get("B1A_VN", "4"))):
                v_proj(wv, uT3, uT.b, b * 128, 1 + b, keep32=(kvo.ap[:, 256:512] if (og == 3 and b == 3) else None))
            if og == 3:
                S.op("pe", [mm(bank(3, 256), uT3[:, kc, 384:512], wv[0][:, kc, 0:256], kc == 0, kc == 15) for kc in range(16)],
                     reads=[wv[1], uT.b], writes=[PB[3]])
                S.op("dve", lambda e: e.tensor_copy(kvo.ap[:, 0:256], bank(3, 256)), reads=[PB[3]], writes=[kvo.b])
                S.dma("sp", [(k_out, kvo.ap[:, 0:256]), (v_out, kvo.ap[:, 256:512])], reads=[kvo.b])
            if NSTEP < 8: break
            for pc in range(2):
                i = 1 - pc
                S.dma("pool", [(s3[i], w_in_v[:, :, pc * 512:(pc + 1) * 512])], writes=[slots[i].b])
                for t4 in range(4):
                    bi = 4 + t4
                    S.op("pe", [mm(bank(bi), s3[i][:, kc, t4 * 128:(t4 + 1) * 128], uT3[:, kc, :], kc == 0, kc == 15) for kc in range(16)],
                         reads=[slots[i].b, uT.b], writes=[PB[bi]])
                    S.op("act" if t4 % 2 == 0 else "dve",
                         (lambda e, t4=t4, bi=bi, pc=pc: e.activation(qT3[:, pc * 4 + t4, :], bank(bi), AF.Copy)) if t4 % 2 == 0 else
                         (lambda e, t4=t4, bi=bi, pc=pc: e.tensor_copy(qT3[:, pc * 4 + t4, :], bank(bi))),
                         reads=[PB[bi]], writes=[qT.b])
            ATT = int(_os.environ.get("ATT_STEPS", "99"))
            for b in range(4):
                if stop_after == "B1a_proj":
                    break
                if ATT < 99 and (og > 0 or b > 0):
                    break
                smc[0] = 0
                abt = ab1t
                if og == 0 and b == 1:
                    S.dma("sp", [(ab1t.ap, abias1)], writes=[ab1t.b])
                c0 = b * 128
                mx = small(16); ngm = small(16); rs = small(16)
                for kvg in range(4):
                    b0 = 4 + 2 * (kvg % 2)
                    mms = []
                    for j in range(4):
                        h = kvg * 4 + j
                        mms.append(mm(bank(b0 + j // 2, 256, (j % 2) * 256), qT3[:, h // 2, c0:c0 + 128],
                                      kT4[:, h % 2, kvg, c0:c0 + 256], True, True))
                    S.op("pe", mms, reads=[qT.b, kT.b], writes=[PB[b0], PB[b0 + 1]])
                    if ATT < 1: continue
                    for hh in range(2):
                        S.op("dve", lambda e, hh=hh, kvg=kvg, b0=b0: e.tensor_tensor(tS.ap[:, hh * 512:(hh + 1) * 512], bank(b0 + hh),
                                                                                    abt.ap[:, (kvg * 4 + hh * 2) * 256:(kvg * 4 + hh * 2 + 2) * 256], ALU.add),
                             reads=[PB[b0 + hh], abt.b], writes=[tS.b])
                    if ATT < 2: continue
                    S.op("dve", lambda e, kvg=kvg: e.tensor_reduce(mx[:, kvg * 4:kvg * 4 + 4], tS.v("p (h k) -> p h k", h=4), AX.X, ALU.max),
                         reads=[tS.b], writes=[sm.b])
                    S.op("dve", lambda e, kvg=kvg: e.tensor_tensor(mx[:, kvg * 4:kvg * 4 + 4], mx[:, kvg * 4:kvg * 4 + 4], sk8.ap[:, kvg * 4:kvg * 4 + 4], ALU.max),
                         reads=[sm.b, sk8.b], writes=[sm.b])
                    S.op("dve", lambda e, kvg=kvg: e.tensor_scalar(ngm[:, kvg * 4:kvg * 4 + 4], mx[:, kvg * 4:kvg * 4 + 4], -0.125, None, ALU.mult),
                         reads=[sm.b], writes=[sm.b])
                    if ATT < 3: continue
                    for j in range(4):
                        h = kvg * 4 + j
                        S.op("act", lambda e, j=j, h=h: e.activation(Pt.ap[:, j * 256:(j + 1) * 256], tS.ap[:, j * 256:(j + 1) * 256], AF.Exp,
                                                                    bias=ngm[:, h:h + 1], scale=0.125, accum_out=rs[:, h:h + 1]),
                             reads=[tS.b, sm.b], writes=[Pt.b, sm.b])
                    if ATT < 4: continue
                    pv = bank16(1).rearrange("p (k m) -> p k m", m=128)
                    S.op("pe", [tp(pv[:, k, :], Pt.ap[:, k * 128:(k + 1) * 128], identb) for k in range(8)], reads=[Pt.b, cB.b], writes=[PB[1]])
                    S.op("act", lambda e: e.activation(PT.ap, bank16(1), AF.Copy), reads=[PB[1]], writes=[PT.b])
                    if ATT < 5: continue
                    PT3 = PT.v("p (k m) -> p k m", k=8)
                    mms = []
                    for j in range(4):
                        h = kvg * 4 + j
                        for half in range(2):
                            mms.append(mm(bank(2 + h // 8, 64, (h % 8) * 64), PT3[:, j * 2 + half, :], V3[:, b + half, kvg * 64:(kvg + 1) * 64], half == 0, half == 1))
                    S.op("pe", mms, reads=[PT.b, Vt.b], writes=[PB[2], PB[3]])
                if ATT < 6: continue
                dn = small(16)
                S.op("dve", lambda e: e.scalar_tensor_tensor(dn, sk8.ap, 0.125, ngm, ALU.mult, ALU.add), reads=[sk8.b, sm.b], writes=[sm.b])
                S.op("act", lambda e: e.activation(dn, dn, AF.Exp), reads=[sm.b], writes=[sm.b])
                S.op("dve", lambda e: e.tensor_tensor(dn, dn, rs, ALU.add), reads=[sm.b], writes=[sm.b])
                S.op("dve", lambda e: e.reciprocal(dn, dn), reads=[sm.b], writes=[sm.b])
                for g in range(2):
                    S.op("dve", lambda e, g=g: e.tensor_tensor(at_.ap[:, g * 512:(g + 1) * 512].rearrange("p (h d) -> p h d", h=8),
                                                               bank(2 + g).rearrange("p (h d) -> p h d", h=8),
                                                               dn[:, g * 8:(g + 1) * 8].unsqueeze(2).to_broadcast([128, 8, 64]), ALU.mult),
                         reads=[PB[2 + g], sm.b], writes=[at_.b])
                rstd = rms_rstd(at_.ap, 1024, [at_.b])
                S.op("dve", lambda e, rstd=rstd: e.scalar_tensor_tensor(ao.ap, at_.ap, rstd, ga.ap, ALU.mult, ALU.mult), reads=[at_.b, sm.b, ga.b], writes=[ao.b])
                tk = og * 512 + c0
                S.dma("sp", [(cat_d[tk:tk + 128, 0:1024], ao.ap)], reads=[ao.b], writes=[CATD])
            for eo in range(2):
                S.op("act", lambda e, eo=eo: e.activation(kT4[:, eo, :, 0:128], kT4[:, eo, :, 512:640], AF.Copy), reads=[kT.b], writes=[kT.b])
            S.op("act", lambda e: e.activation(V3[:, 0, :], V3[:, 4, :], AF.Copy), reads=[Vt.b], writes=[Vt.b])
        S.barrier()
        A.release(m)

    if stop_after != "B1b":
        phase_B1a()
    if do_samples and stop_after is None:
        phase_SM1()
    A.release(m_gm)

    def phase_B2():
        m = A.mark()
        wo = A.alloc("wo", 16 * 2048, BF16); wo3 = wo.v("p (k n) -> p k n", k=16)
        w_out_v = w_out.rearrange("(kc p) n -> p kc n", p=128)
        for pcs in range(4):
            S.dma("pool", [(wo3[:, :, pcs * 512:(pcs + 1) * 512], w_out_v[:, :, pcs * 512:(pcs + 1) * 512])], writes=[wo.b])
        ggt1 = A.alloc("ggt1", D); gm2 = A.alloc("gm2", D); sh2 = A.alloc("sh2", D)
        load_mod_bc(ggt1, 2); load_mod_bc(gm2, 4); load_mod_bc(sh2, 3)
        cb_ = [A.alloc("catb%d" % i, D, BF16) for i in range(2)]
        cT = A.alloc("catT", 16 * 128, BF16); cT3 = reg3(cT, 16)
        u2 = A.alloc("u2T", 16 * 128, BF16); u23 = reg3(u2, 16)
        def b2_body(cT3v, cTbuf, np_, xt, u2dst3, x1_store, u2_store):
            ss4 = small(4)
            for q in range(4):
                o = ps[0:np_, (4 + q) * 512:(5 + q) * 512]
                S.op("pe", [mm(o, cT3v[:, kc, 0:np_], wo3[:, kc, q * 512:(q + 1) * 512], kc == 0, kc == 15) for kc in range(16)],
                     reads=[cTbuf, wo.b], writes=[PB[4 + q]])
                S.op("act", lambda e, q=q, o=o: e.activation(junk.ap[0:np_, 0:512], o, AF.Square, scale=float(D) ** -0.5, accum_out=ss4[0:np_, q:q + 1]),
                     reads=[PB[4 + q]], writes=[junk.b, sm.b])
            rstd = small(1)
            S.op("dve", lambda e: e.tensor_reduce(rstd[0:np_], ss4[0:np_], AX.X, ALU.add), reads=[sm.b], writes=[sm.b])
            S.op("dve", lambda e: e.tensor_scalar(rstd[0:np_], rstd[0:np_], EPS, None, ALU.add), reads=[sm.b], writes=[sm.b])
            S.op("act", lambda e: e.activation(rstd[0:np_], rstd[0:np_], AF.Ln), reads=[sm.b], writes=[sm.b])
            S.op("act", lambda e: e.activation(rstd[0:np_], rstd[0:np_], AF.Exp, scale=-0.5), reads=[sm.b], writes=[sm.b])
            for q in range(4):
                o = ps[0:np_, (4 + q) * 512:(5 + q) * 512]
                S.op("dve", lambda e, q=q, o=o: e.scalar_tensor_tensor(tmp32.ap[0:np_, q * 512:(q + 1) * 512], o, rstd[0:np_], ggt1.ap[0:np_, q * 512:(q + 1) * 512], ALU.mult, ALU.mult),
                     reads=[PB[4 + q], sm.b, ggt1.b], writes=[tmp32.b])
            S.op("dve", lambda e: e.tensor_tensor(xt.ap[0:np_], xt.ap[0:np_], tmp32.ap[0:np_], ALU.add), reads=[xt.b, tmp32.b], writes=[xt.b])
            x1_store(xt)
            norm_mod_T(xt, gm2, sh2, u2dst3, 0, np_=np_)
            u2_store()

        for blk in range(16):
            smc[0] = 0
            cbt = cb_[blk % 2]
            S.dma("sp", [(cbt.ap, cat_d[blk * 128:(blk + 1) * 128, :])], reads=[CATD], writes=[cbt.b])
            xt = xb[blk % 2]
            S.dma("sp", [(xt.ap, xext[OWN0 + blk * 128:OWN0 + (blk + 1) * 128, :])], writes=[xt.b])
            transpose_to(cbt, cT3, 0)
            b2_body(cT3, cT.b, 128, xt,  u23,
                    lambda xt, blk=blk: S.dma("sp", [(x1_d[blk * 128:(blk + 1) * 128, :], xt.ap)], reads=[xt.b], writes=[X1D]),
                    lambda blk=blk: S.dma("sp", [(u2T_d.rearrange("k p t -> p k t")[:, :, blk * 128:(blk + 1) * 128], u23)], reads=[u2.b], writes=[U2TD]))
        if do_samples:
            smc[0] = 0
            load_mod_rows(ggt1, 2); load_mod_rows(gm2, 4); load_mod_rows(sh2, 3)
            S.dma("sp", [(x1s.ap[0:16, :], xs_d)], writes=[x1s.b])
            b2_body(catsT3, catsT.b, 16, x1s, u2sT3, lambda xt: None, lambda: None)
        S.barrier()
        A.release(m)

    if stop_after not in ("B1b", "B1a", "B1a_proj"):
        phase_B2()

    def phase_B3():
        m = A.mark()
        uT = A.alloc("u2Tall", 16 * NOWN, BF16); uT3 = uT.v("p (k n) -> p k n", k=16)
        S.dma("sp", [(uT3[:, kq * 4:(kq + 1) * 4, :], u2T_d.rearrange("k p t -> p k t")[:, kq * 4:(kq + 1) * 4, :]) for kq in range(4)], reads=[U2TD], writes=[uT.b])
        gsl = [A.alloc("wg%d" % i, 16 * 256, BF16) for i in range(2)]
        usl = [A.alloc("wu%d" % i, 16 * 256, BF16) for i in range(2)]
        hst_ = [A.alloc("hstg%d" % i, NOWN, BF16) for i in range(2)]
        et = [A.alloc("et%d" % i, 512) for i in range(2)]
        hsk = A.alloc("hs_tok", DFF, BF16)
        wg_v = w_gate.rearrange("(kc p) n -> p kc n", p=128)
        wu_v = w_up.rearrange("(kc p) n -> p kc n", p=128)
        for pj in range(NJ // 2):
            g3 = gsl[pj % 2].v("p (k n) -> p k n", k=16)
            u3 = usl[pj % 2].v("p (k n) -> p k n", k=16)
            S.dma("pool", [(g3, wg_v[:, :, pj * 256:(pj + 1) * 256])], writes=[gsl[pj % 2].b])
            S.dma("pool", [(u3, wu_v[:, :, pj * 256:(pj + 1) * 256])], writes=[usl[pj % 2].b])
            for jj in range(2):
                j = pj * 2 + jj
                hs = hst_[j % 2]
                for tg in range(4):
                    bg, bu = 4 + (tg % 2) * 2, 5 + (tg % 2) * 2
                    S.op("pe", [mm(bank(bg), g3[:, kc, jj * 128:(jj + 1) * 128], uT3[:, kc, tg * 512:(tg + 1) * 512], kc == 0, kc == 15) for kc in range(16)],
                         reads=[gsl[pj % 2].b, uT.b], writes=[PB[bg]])
                    S.op("pe", [mm(bank(bu), u3[:, kc, jj * 128:(jj + 1) * 128], uT3[:, kc, tg * 512:(tg + 1) * 512], kc == 0, kc == 15) for kc in range(16)],
                         reads=[usl[pj % 2].b, uT.b], writes=[PB[bu]])
                    e_ = et[tg % 2]
                    S.op("act", lambda e, e_=e_, bg=bg: e.activation(e_.ap, bank(bg), AF.Exp, scale=-1.0), reads=[PB[bg]], writes=[e_.b])
                    S.op("dve", lambda e, e_=e_: e.tensor_scalar(e_.ap, e_.ap, 1.0, None, ALU.add), reads=[e_.b], writes=[e_.b])
                    S.op("dve", lambda e, e_=e_: e.reciprocal(e_.ap, e_.ap), reads=[e_.b], writes=[e_.b])
                    S.op("dve", lambda e, e_=e_, bg=bg: e.tensor_tensor(e_.ap, e_.ap, bank(bg), ALU.mult), reads=[e_.b, PB[bg]], writes=[e_.b])
                    S.op("dve", lambda e, e_=e_, bu=bu, hs=hs, tg=tg: e.tensor_tensor(hs.ap[:, tg * 512:(tg + 1) * 512], e_.ap, bank(bu), ALU.mult),
                         reads=[e_.b, PB[bu]], writes=[hs.b])
                S.dma("sp", [(hT_d[j], hs.ap)], reads=[hs.b], writes=[HTD])
            if do_samples:
                og_ = ps[0:16, 2 * 512:2 * 512 + 256]
                ou_ = ps[0:16, 3 * 512:3 * 512 + 256]
                S.op("pe", [mm(og_, u2sT3[:, kc, 0:16], g3[:, kc, :], kc == 0, kc == 15) for kc in range(16)], reads=[u2sT.b, gsl[pj % 2].b], writes=[PB[2]])
                S.op("pe", [mm(ou_, u2sT3[:, kc, 0:16], u3[:, kc, :], kc == 0, kc == 15) for kc in range(16)], reads=[u2sT.b, usl[pj % 2].b], writes=[PB[3]])
                e_ = et[0]
                S.op("act", lambda e, e_=e_: e.activation(e_.ap[0:16, 0:256], og_, AF.Exp, scale=-1.0), reads=[PB[2]], writes=[e_.b])
                S.op("dve", lambda e, e_=e_: e.tensor_scalar(e_.ap[0:16, 0:256], e_.ap[0:16, 0:256], 1.0, None, ALU.add), reads=[e_.b], writes=[e_.b])
                S.op("dve", lambda e, e_=e_: e.reciprocal(e_.ap[0:16, 0:256], e_.ap[0:16, 0:256]), reads=[e_.b], writes=[e_.b])
                S.op("dve", lambda e, e_=e_: e.tensor_tensor(e_.ap[0:16, 0:256], e_.ap[0:16, 0:256], og_, ALU.mult), reads=[e_.b, PB[2]], writes=[e_.b])
                S.op("dve", lambda e, e_=e_, pj=pj: e.tensor_tensor(hsk.ap[0:16, pj * 256:(pj + 1) * 256], e_.ap[0:16, 0:256], ou_, ALU.mult), reads=[e_.b, PB[3]], writes=[hsk.b])
        if do_samples:
            for j0 in range(0, NJ, 8):
                nk = min(8, NJ - j0)
                pv = bank16(1).rearrange("p (k m) -> p k m", m=128)
                S.op("pe", [tp(pv[:, k, 0:16], hsk.ap[0:16, (j0 + k) * 128:(j0 + k + 1) * 128], identb[0:16, 0:16]) for k in range(nk)],
                     reads=[hsk.b, cB.b], writes=[PB[1]])
                S.op("act", lambda e, j0=j0, nk=nk, pv=pv: e.activation(hsT3[:, j0:j0 + nk, :], pv[:, 0:nk, 0:16], AF.Copy), reads=[PB[1]], writes=[hsT.b])
        S.barrier()
        A.release(m)

    def phase_B4():
        m = A.mark()
        ggt2 = A.alloc("ggt2", D)
        load_mod_bc(ggt2, 5)
        hT = A.alloc("hTg", NJ * 512, BF16); hT3 = hT.v("p (j t) -> p j t", j=NJ)
        wsl = [A.alloc("wd%d" % i, NJ * 256, BF16) for i in range(2)]
        ft = A.alloc("ft", 4 * D); f3 = ft.v("p (b n) -> p b n", b=4)
        wd_v = w_down.rearrange("(j p) n -> p j n", p=128)
        hT_v = hT_d.rearrange("j p t -> p j t")
        for tg in range(4):
            S.dma("sp", [(hT3[:, jq * 11:(jq + 1) * 11, :], hT_v[:, jq * 11:(jq + 1) * 11, tg * 512:(tg + 1) * 512]) for jq in range(4)], reads=[HTD], writes=[hT.b])
            for pc in range(8):
                w3 = wsl[pc % 2].v("p (j n) -> p j n", j=NJ)
                S.dma("pool", [(w3[:, jq * 11:(jq + 1) * 11, :], wd_v[:, jq * 11:(jq + 1) * 11, pc * 256:(pc + 1) * 256]) for jq in range(4)], writes=[wsl[pc % 2].b])
                for tb in range(4):
                    bi = 4 + tb
                    S.op("pe", [mm(bank(bi, 256), hT3[:, j, tb * 128:(tb + 1) * 128], w3[:, j, :], j == 0, j == NJ - 1) for j in range(NJ)],
                         reads=[hT.b, wsl[pc % 2].b], writes=[PB[bi]])
                    S.op("act", lambda e, tb=tb, bi=bi, pc=pc: e.activation(f3[:, tb, pc * 256:(pc + 1) * 256], bank(bi, 256), AF.Copy), reads=[PB[bi]], writes=[ft.b])
            for tb in range(4):
                smc[0] = 0
                blk = tg * 4 + tb
                xt = xb[tb % 2]
                S.dma("sp", [(xt.ap, x1_d[blk * 128:(blk + 1) * 128, :])], reads=[X1D], writes=[xt.b])
                rstd = rms_rstd(f3[:, tb, :], D, [ft.b])
                S.op("dve", lambda e, tb=tb, rstd=rstd: e.scalar_tensor_tensor(tmp32.ap, f3[:, tb, :], rstd, ggt2.ap, ALU.mult, ALU.mult), reads=[ft.b, sm.b, ggt2.b], writes=[tmp32.b])
                S.op("dve", lambda e, xt=xt: e.tensor_tensor(xt.ap, xt.ap, tmp32.ap, ALU.add), reads=[xt.b, tmp32.b], writes=[xt.b])
                S.dma("sp", [(y_out[blk * 128:(blk + 1) * 128, :], xt.ap)], reads=[xt.b])
        if do_samples:
            smc[0] = 0
            load_mod_rows(ggt2, 5)
            for pc in range(8):
                w3 = wsl[pc % 2].v("p (j n) -> p j n", j=NJ)
                S.dma("pool", [(w3[:, jq * 11:(jq + 1) * 11, :], wd_v[:, jq * 11:(jq + 1) * 11, pc * 256:(pc + 1) * 256]) for jq in range(4)], writes=[wsl[pc % 2].b])
                o = ps[0:16, (4 + pc % 4) * 512:(4 + pc % 4) * 512 + 256]
                S.op("pe", [mm(o, hsT3[:, j, :], w3[:, j, :], j == 0, j == NJ - 1) for j in range(NJ)], reads=[hsT.b, wsl[pc % 2].b], writes=[PB[4 + pc % 4]])
                S.op("act", lambda e, o=o, pc=pc: e.activation(f3[0:16, 0, pc * 256:(pc + 1) * 256], o, AF.Copy), reads=[PB[4 + pc % 4]], writes=[ft.b])
            rstd = rms_rstd(f3[0:16, 0, :], D, [ft.b])
            S.op("dve", lambda e: e.scalar_tensor_tensor(tmp32.ap[0:16], f3[0:16, 0, :], rstd[0:16], ggt2.ap[0:16], ALU.mult, ALU.mult), reads=[ft.b, sm.b, ggt2.b], writes=[tmp32.b])
            S.op("dve", lambda e: e.tensor_tensor(x1s.ap[0:16], x1s.ap[0:16], tmp32.ap[0:16], ALU.add), reads=[x1s.b, tmp32.b], writes=[x1s.b])
            S.dma("sp", [(ys_out, x1s.ap[0:16, :])], reads=[x1s.b])
        S.barrier()
        A.release(m)

    if stop_after is None:
        phase_B3()
        phase_B4()
    S.final_wait("sp")
    S.emit()
    st.close()
    print("[build] arena peak words", A.peak, "instr", {e: S.count[e] for e in S.count})
    return nc


def _consts():
    i = np.arange(128)
    ident = np.eye(128, dtype=np.float32)
    U = (i[:, None] <= i[None, :]).astype(np.float32)
    UT = (i[:, None] > i[None, :]).astype(np.float32)
    ones = np.ones((128, 128), np.float32)
    caus = (i[None, :] >= i[:, None]).astype(np.float32)
    cf32 = np.concatenate([ident, U, UT, ones, caus], axis=1)
    cb16 = np.concatenate([ident, ones], axis=1).astype(ml_dtypes.bfloat16)
    return cf32, cb16


def _abias(first_core):
    slopes = (2.0 ** (-8.0 * np.arange(1, 17) / 16)).astype(np.float32)
    a = np.arange(128)[:, None]
    j = np.arange(256)[None, :]
    dist = a + 128 - j
    valid = (dist >= 0) & (dist < 128)
    out = []
    for first in (True, False):
        v = valid & ((j >= 128) if (first and first_core) else True)
        b = np.where(v[None], -slopes[:, None, None] * dist[None].astype(np.float32), -30000.0) * 8.0
        out.append(np.ascontiguousarray(np.transpose(b, (1, 0, 2)).reshape(128, 16 * 256)).astype(np.float32))
    return out


def make_in_maps(inp):
    cf32, cb16 = _consts()
    xp = np.asarray(inp["x_prompt"])[0]
    maps = []
    wada = np.concatenate([np.asarray(inp["w_ada"])[0], np.asarray(inp["b_ada"])[0][None, :]], axis=0)
    gvec = np.stack([np.asarray(inp[k])[0] for k in ("g_pre_mix", "g_post_mix", "g_pre_ffn", "g_post_ffn")], 0)
    cwv = np.asarray(inp["conv_w"])[0]
    convw = np.ascontiguousarray(cwv.reshape(4, 12, 128).transpose(2, 1, 0).reshape(128, 48))
    convb = np.ascontiguousarray(np.asarray(inp["conv_b"])[0].reshape(12, 128).T)
    hvec = np.concatenate([np.asarray(inp[k])[0] for k in ("dt_bias", "a_log", "d_skip", "attn_sinks")])[None, :]
    slopes = (2.0 ** (-8.0 * np.arange(1, 17) / 16)).astype(np.float32)
    sbias = (-slopes[:, None] * (127 - np.arange(128))[None, :].astype(np.float32) * 8.0).astype(np.float32)
    selkv = (np.arange(16)[:, None] // 4 == np.arange(4)[None, :]).astype(np.float32)
    for c in range(8):
        nreal = 2048 * (c + 1)
        xext = np.zeros((NEXT, D), np.float32)
        xext[NEXT - nreal:] = xp[:nreal]
        m = np.zeros((NEXT,), np.float32)
        m[NEXT - nreal:] = 1.0
        ab0, ab1 = _abias(c == 0)
        maps.append({
            "xext": xext, "mrow": m[None, :], "mtok": np.ascontiguousarray(m.reshape(NEXT // 128, 128).T),
            "cmat": np.concatenate([np.asarray(inp["c_sample"])[16 * c:16 * c + 16], np.asarray(inp["c_prompt"])], 0),
            "wada": wada, "gvec": gvec, "w_in": np.asarray(inp["w_in"])[0], "w_out": np.asarray(inp["w_out"])[0],
            "w_gate": np.asarray(inp["w_gate"])[0], "w_up": np.asarray(inp["w_up"])[0], "w_down": np.asarray(inp["w_down"])[0],
            "convw": convw, "convb": convb, "hvec": hvec.astype(np.float32),
            "gatt": np.asarray(inp["g_attn_out"]), "gssm": np.asarray(inp["g_ssm_out"]),
            "cf32": cf32, "cb16": cb16, "abias0": ab0, "abias1": ab1,
            "xs_d": np.ascontiguousarray(np.asarray(inp["x_sample"])[16 * c:16 * c + 16, 0, :]),
            "ck_d": np.ascontiguousarray(np.asarray(inp["cache_k"])[0, 16 * c:16 * c + 16].reshape(16, 128, 256)),
            "cv_d": np.ascontiguousarray(np.asarray(inp["cache_v"])[0, 16 * c:16 * c + 16].reshape(16, 128, 256)),
            "sconv_d": np.ascontiguousarray(np.asarray(inp["state_conv"])[0, 16 * c:16 * c + 16].reshape(16, 4608)),
            "sssm_d": np.ascontiguousarray(np.asarray(inp["state_ssm"])[0, 16 * c:16 * c + 16].reshape(16, 1024, 128)),
            "convw_raw": np.ascontiguousarray(cwv.reshape(1, 6144)), "convb_raw": np.asarray(inp["conv_b"]).reshape(1, 1536),
            "sinkcol": np.asarray(inp["attn_sinks"]).reshape(16, 1), "sbias": sbias, "selkv": selkv,
            "dskrow": np.repeat(np.asarray(inp["d_skip"])[0], 64)[None, :].astype(np.float32),
        })
    return maps


_NC_CACHE = {}


def kernel(**inputs):
    if "nc" not in _NC_CACHE:
        _NC_CACHE["nc"] = build()
    nc = _NC_CACHE["nc"]
    maps = make_in_maps(inputs)
    res = run_bass_kernel_spmd(nc, maps, core_ids=list(range(8)))
    r = res.results
    f = np.float32
    y_prompt = np.concatenate([np.asarray(r[c]["y_out"]) for c in range(8)], 0)[None].astype(f)
    k_prompt = np.asarray(r[7]["k_out"]).reshape(1, 1, 128, 4, 64).astype(f)
    v_prompt = np.asarray(r[7]["v_out"]).reshape(1, 1, 128, 4, 64).astype(f)
    conv_prompt = np.asarray(r[7]["conv_out"]).reshape(1, 1, 3, 1536).astype(f)
    ssm_prompt = np.asarray(r[7]["ssm_out"]).reshape(1, 1, 16, 64, 128).astype(f)
    cat = lambda name: np.concatenate([np.asarray(r[c][name]) for c in range(8)], 0).astype(f)
    y_sample = cat("ys_out").reshape(128, 1, 2048)
    k_sample = cat("ks_out").reshape(1, 128, 128, 4, 64)
    v_sample = cat("vs_out").reshape(1, 128, 128, 4, 64)
    conv_sample = cat("convs_out").reshape(1, 128, 3, 1536)
    ssm_sample = cat("ssms_out").reshape(1, 128, 16, 64, 128)
    return (y_prompt, y_sample, k_prompt, v_prompt, conv_prompt, ssm_prompt,
            k_sample, v_sample, conv_sample, ssm_sample)
```

```python
import contextlib
import numpy as np
import ml_dtypes
import concourse.bass as bass
import concourse.mybir as mybir
from concourse.bass_utils import run_bass_kernel_spmd

F32 = mybir.dt.float32
BF16 = mybir.dt.bfloat16
ALU = mybir.AluOpType
AF = mybir.ActivationFunctionType
AX = mybir.AxisListType

ENGS = ("pe", "act", "dve", "pool", "sp")


class Buf:
    __slots__ = ("name", "last_w", "readers")

    def __init__(self, name):
        self.name = name
        self.last_w = None
        self.readers = []


class Sched:
    def __init__(self, nc, n_dma_sp=40, n_dma_pool=12, self_wait=True):
        self.nc = nc
        self.q = {e: [] for e in ENGS}
        self.count = {e: 0 for e in ENGS}
        self.seen = {e: {} for e in ENGS}
        self.self_wait = self_wait
        self.ndma = {"sp": n_dma_sp, "pool": n_dma_pool, "act": 8}
        self.dma_next = {"sp": 0, "pool": 0, "act": 0}
        self.dma_cnt = {}
        self.sems = {}

    def _deps(self, eng, reads, writes):
        deps = {}

        def add(tok):
            if tok is None:
                return
            k, v = tok
            if deps.get(k, 0) < v:
                deps[k] = v
        for b in reads:
            add(b.last_w)
        for b in writes:
            add(b.last_w)
            for r in b.readers:
                add(r)
        out = []
        for k, v in deps.items():
            if k == eng and (eng == "pe" or not self.self_wait):
                continue
            if self.seen[eng].get(k, 0) >= v:
                continue
            self.seen[eng][k] = v
            out.append((k, v))
        return out

    def _commit(self, tok, reads, writes):
        for b in writes:
            b.last_w = tok
            b.readers = []
        for b in reads:
            if b not in writes:
                b.readers.append(tok)
                if len(b.readers) > 64:
                    best = {}
                    for k, v in b.readers:
                        if best.get(k, 0) < v:
                            best[k] = v
                    b.readers = list(best.items())

    def op(self, eng, fns, reads=(), writes=()):
        if callable(fns):
            fns = [fns]
        waits = self._deps(eng, reads, writes)
        self.count[eng] += 1
        tok = (eng, self.count[eng])
        self.q[eng].append(("op", waits, fns, tok))
        self._commit(tok, reads, writes)
        return tok

    def dma(self, eng, pairs, reads=(), writes=(), **kw):
        i = self.dma_next[eng]
        self.dma_next[eng] = (i + 1) % self.ndma[eng]
        key = "d_%s_%d" % (eng, i)
        prev = self.dma_cnt.get(key, 0)
        waits = self._deps(eng, reads, writes)
        if prev and self.seen[eng].get(key, 0) < prev:
            self.seen[eng][key] = prev
            waits.append((key, prev))
        val = prev + 16 * len(pairs)
        self.dma_cnt[key] = val
        tok = (key, val)
        self.q[eng].append(("dma", waits, pairs, tok, kw))
        self._commit(tok, reads, writes)
        return tok

    def barrier(self):
        targets = [(e, self.count[e]) for e in ENGS if self.count[e] > 0]
        targets += [(k, v) for k, v in self.dma_cnt.items()]
        for e in ENGS:
            waits = []
            for k, v in targets:
                if k == e:
                    continue
                if self.seen[e].get(k, 0) >= v:
                    continue
                self.seen[e][k] = v
                waits.append((k, v))
            if waits:
                self.q[e].append(("wait", waits))

    def final_wait(self, eng="sp"):
        waits = [(k, v) for k, v in self.dma_cnt.items()]
        waits += [(e, self.count[e]) for e in ENGS if self.count[e] > 0 and e != eng]
        self.q[eng].append(("wait", waits))

    def emit(self):
        nc = self.nc
        keys = [e for e in ENGS if self.count[e] > 0] + sorted(self.dma_cnt.keys())
        with contextlib.ExitStack() as st:
            for k in keys:
                self.sems[k] = st.enter_context(nc.semaphore("s_" + k))
            block = st.enter_context(nc.Block())
            sems = self.sems

            def run(engobj, items):
                for it in items:
                    for (k, v) in it[1]:
                        engobj.wait_ge(sems[k], v)
                    if it[0] == "op":
                        _, _, fns, tok = it
                        ins = None
                        for f in fns:
                            ins = f(engobj)
                        ins.then_inc(sems[tok[0]], 1)
                    elif it[0] == "dma":
                        _, _, pairs, tok, kw = it
                        for (o, i) in pairs:
                            engobj.dma_start(out=o, in_=i, **kw).then_inc(sems[tok[0]], 16)

            if self.q["pe"]:
                @block.tensor
                def _(e):
                    run(e, self.q["pe"])
            if self.q["act"]:
                @block.scalar
                def _(e):
                    run(e, self.q["act"])
            if self.q["dve"]:
                @block.vector
                def _(e):
                    run(e, self.q["dve"])
            if self.q["pool"]:
                @block.gpsimd
                def _(e):
                    run(e, self.q["pool"])
            if self.q["sp"]:
                @block.sync
                def _(e):
                    run(e, self.q["sp"])


class Tile:
    def __init__(self, ap, name):
        self.ap = ap
        self.b = Buf(name)

    def v(self, pat, **kw):
        return self.ap.rearrange(pat, **kw)


class Arena:
    def __init__(self, base, nwords):
        self.base = base
        self.n = nwords
        self.off = 0
        self.peak = 0

    def alloc(self, name, nelem, dt=F32):
        words = nelem if dt == F32 else (nelem + 1) // 2
        wal = (words + 7) // 8 * 8
        assert self.off + wal <= self.n, ("SBUF arena overflow", name, self.off, wal, self.n)
        ap = self.base[:, self.off:self.off + words]
        if dt != F32:
            ap = ap.bitcast(dt)
        self.off += wal
        self.peak = max(self.peak, self.off)
        return Tile(ap, name)

    def mark(self):
        return self.off

    def release(self, m):
        self.off = m


def mm(out, lhsT, rhs, start, stop):
    return lambda e: e.matmul(out, lhsT, rhs, start=start, stop=stop)


def tp(out, in_, ident):
    return lambda e: e.transpose(out, in_, ident)
D = 2048
KC = 16
NEXT = 16384
NOWN = 2048
EPS = 1e-6
NPRE_G = (NEXT - NOWN) // 512
INW = 4112
DFF = 5632
NJ = DFF // 128


def build(n_pre_groups=NPRE_G, stop_after=None, do_samples=True):
    nc = bass.Bass("TRN2", target_bir_lowering=False)

    def din(name, shape, dt=F32):
        return nc.dram_tensor(name, list(shape), dt, kind="ExternalInput").ap()

    def dout(name, shape, dt=F32):
        return nc.dram_tensor(name, list(shape), dt, kind="ExternalOutput").ap()

    def dscr(name, shape, dt=F32):
        return nc.dram_tensor(name, list(shape), dt, kind="Internal").ap()

    xext = din("xext", [NEXT, D])
    mrow = din("mrow", [1, NEXT])
    mtok = din("mtok", [128, NEXT // 128])
    cmat = din("cmat", [17, D])
    wada = din("wada", [D + 1, 6 * D])
    gvec = din("gvec", [4, D])
    w_in = din("w_in", [D, INW])
    w_out = din("w_out", [D, D])
    w_gate = din("w_gate", [D, DFF])
    w_up = din("w_up", [D, DFF])
    w_down = din("w_down", [DFF, D])
    convw = din("convw", [128, 48])
    convb = din("convb", [128, 12])
    hvec = din("hvec", [1, 64])
    gatt = din("gatt", [1, 1024])
    gssm = din("gssm", [1, 1024])
    cf32 = din("cf32", [128, 5 * 128])
    cb16 = din("cb16", [128, 2 * 128], BF16)
    abias0 = din("abias0", [128, 16 * 256])
    abias1 = din("abias1", [128, 16 * 256])
    y_out = dout("y_out", [NOWN, D])
    k_out = dout("k_out", [128, 256])
    v_out = dout("v_out", [128, 256])
    conv_out = dout("conv_out", [3, 1536])
    ssm_out = dout("ssm_out", [1024, 128])
    mod_d = dscr("mod_d", [17, 6 * D])
    cat_d = dscr("cat_d", [NOWN, D], BF16)
    x1_d = dscr("x1_d", [NOWN, D])
    hT_d = dscr("hT_d", [NJ, 128, NOWN], BF16)

    st = contextlib.ExitStack()
    NW = 47616
    big = st.enter_context(nc.sbuf_tensor("big", [128, NW], F32))
    ps = st.enter_context(nc.psum_tensor("ps", [128, 4096], F32))
    A = Arena(big, NW)
    S = Sched(nc)
    PB = [Buf("psb%d" % i) for i in range(8)]

    def bank(i, n=512, off=0):
        return ps[:, i * 512 + off:i * 512 + off + n]

    def bank16(i):
        return ps[:, i * 512:(i + 1) * 512].bitcast(BF16)

    MODD = Buf("mod_d")
    w_in_v = w_in.rearrange("(kc p) n -> p kc n", p=128)

    cF = A.alloc("cF", 5 * 128)
    cB = A.alloc("cB", 2 * 128, BF16)
    S.dma("sp", [(cF.ap, cf32)], writes=[cF.b])
    S.dma("sp", [(cB.ap, cb16)], writes=[cB.b])
    identf = cF.ap[:, 0:128]
    Umat = cF.ap[:, 128:256]
    UTmat = cF.ap[:, 256:384]
    onesf = cF.ap[:, 384:512]
    caus01 = cF.ap[:, 512:640]
    identb = cB.ap[:, 0:128]
    onesb = cB.ap[:, 128:256]
    hv = A.alloc("hv", 64)
    S.dma("sp", [(hv.ap, hvec.to_broadcast([128, 64]))], writes=[hv.b])
    dtb_bc = hv.ap[:, 0:16]
    dsk_bc = hv.ap[:, 32:48]
    sink_bc = hv.ap[:, 48:64]
    a_bc = A.alloc("a_bc", 16)
    S.op("act", lambda e: e.activation(a_bc.ap, hv.ap[:, 16:32], AF.Exp), reads=[hv.b], writes=[a_bc.b])
    S.op("dve", lambda e: e.tensor_scalar(a_bc.ap, a_bc.ap, -1.0, None, ALU.mult), reads=[a_bc.b], writes=[a_bc.b])
    cw = A.alloc("cw", 48)
    cbi = A.alloc("cbi", 12)
    S.dma("sp", [(cw.ap, convw)], writes=[cw.b])
    S.dma("sp", [(cbi.ap, convb)], writes=[cbi.b])
    mt = A.alloc("mt", NEXT // 128)
    S.dma("sp", [(mt.ap, mtok)], writes=[mt.b])

    def phase0():
        m0 = A.mark()
        cm = A.alloc("cm", D)
        S.dma("sp", [(cm.ap[0:17, :], cmat)], writes=[cm.b])
        ee = A.alloc("ee", D)
        S.op("act", lambda e: e.activation(ee.ap[0:17], cm.ap[0:17], AF.Exp, scale=-1.0), reads=[cm.b], writes=[ee.b])
        S.op("dve", lambda e: e.tensor_scalar(ee.ap[0:17], ee.ap[0:17], 1.0, None, ALU.add), reads=[ee.b], writes=[ee.b])
        S.op("dve", lambda e: e.reciprocal(ee.ap[0:17], ee.ap[0:17]), reads=[ee.b], writes=[ee.b])
        sc = A.alloc("sc", D, BF16)
        S.op("dve", lambda e: e.tensor_tensor(sc.ap[0:17], cm.ap[0:17], ee.ap[0:17], ALU.mult), reads=[cm.b, ee.b], writes=[sc.b])
        cT = A.alloc("cT", 16 * 32, BF16)
        cT3 = cT.v("p (k m) -> p k m", k=16)
        pb = bank16(0).rearrange("p (k m) -> p k m", m=32)
        S.op("pe", [tp(pb[:, kc, 0:17], sc.ap[0:17, kc * 128:(kc + 1) * 128], identb[0:17, 0:17]) for kc in range(16)],
             reads=[sc.b, cB.b], writes=[PB[0]])
        S.op("act", lambda e: e.activation(cT3[:, :, 0:17], pb[:, 0:16, 0:17], AF.Copy), reads=[PB[0]], writes=[cT.b])
        gv = A.alloc("gv", 4 * D)
        S.dma("sp", [(gv.ap[0:17, g * D:(g + 1) * D], gvec[g:g + 1, :].to_broadcast([17, D])) for g in range(4)], writes=[gv.b])
        mod = A.alloc("mod", 6 * D)
        slots = [A.alloc("wa%d" % i, 17 * 512, BF16) for i in range(2)]
        wada_v = wada[0:D, :].rearrange("(kc p) n -> p kc n", p=128)
        for pc in range(24):
            sl = slots[pc % 2]
            s3 = sl.v("p (k n) -> p k n", k=17)
            S.dma("pool", [(s3[:, 0:16, :], wada_v[:, :, pc * 512:(pc + 1) * 512]),
                           (s3[0:1, 16, :], wada[D:D + 1, pc * 512:(pc + 1) * 512])], writes=[sl.b])
            bi = 1 + pc % 2
            o = ps[0:17, bi * 512:(bi + 1) * 512]
            S.op("pe", [mm(o, cT3[:, kc, 0:17], s3[:, kc, :], kc == 0, False) for kc in range(16)]
                 + [mm(o, onesb[0:1, 0:17], s3[0:1, 16, :], False, True)],
                 reads=[cT.b, sl.b, cB.b], writes=[PB[bi]])
            ch, co = pc // 4, (pc % 4) * 512
            dst = mod.ap[0:17, pc * 512:(pc + 1) * 512]
            if ch in (0, 3):
                S.op("act", lambda e, dst=dst, o=o: e.activation(dst, o, AF.Copy), reads=[PB[bi]], writes=[mod.b])
            elif ch in (1, 4):
                g = gv.ap[0:17, (0 if ch == 1 else 2) * D + co:(0 if ch == 1 else 2) * D + co + 512]
                S.op("dve", lambda e, dst=dst, o=o, g=g: e.scalar_tensor_tensor(dst, o, 1.0, g, ALU.add, ALU.mult),
                     reads=[PB[bi], gv.b], writes=[mod.b])
            else:
                g = gv.ap[0:17, (1 if ch == 2 else 3) * D + co:(1 if ch == 2 else 3) * D + co + 512]
                S.op("dve", lambda e, dst=dst, o=o, g=g: e.tensor_tensor(dst, o, g, ALU.mult),
                     reads=[PB[bi], gv.b], writes=[mod.b])
        S.dma("sp", [(mod_d, mod.ap[0:17, :])], reads=[mod.b], writes=[MODD])
        S.barrier()
        A.release(m0)

    phase0()

    def load_mod_bc(tile, ch):
        S.dma("sp", [(tile.ap, mod_d[16:17, ch * D:(ch + 1) * D].to_broadcast([128, D]))], reads=[MODD], writes=[tile.b])

    halo = A.alloc("halo", 36)
    halo3 = halo.v("p (c i) -> p c i", i=3)
    S.op("dve", lambda e: e.memset(halo.ap, 0.0), writes=[halo.b])
    hst = A.alloc("hst", 1024)
    S.op("dve", lambda e: e.memset(hst.ap, 0.0), writes=[hst.b])
    hb = A.alloc("hb", 1024, BF16)
    S.op("dve", lambda e: e.memset(hb.ap, 0.0), writes=[hb.b])
    xb = [A.alloc("xb%d" % i, D) for i in range(2)]
    junk = A.alloc("junk", D, BF16)
    ub = A.alloc("ub", D, BF16)
    tmp32 = A.alloc("tmp32", D)
    sm = A.alloc("sm", 768)
    smc = [0]

    def small(n):
        if smc[0] + n > 768:
            smc[0] = 0
        a = sm.ap[:, smc[0]:smc[0] + n]
        smc[0] += n
        return a

    def rms_rstd(src_ap, n, rd, wr_extra=()):
        ssn = small(1)
        pp = src_ap.shape[0]
        S.op("act", lambda e: e.activation(junk.ap[0:pp, 0:n], src_ap, AF.Square, scale=float(n) ** -0.5, accum_out=ssn[0:pp]),
             reads=list(rd), writes=[junk.b, sm.b])
        S.op("dve", lambda e: e.tensor_scalar(ssn[0:pp], ssn[0:pp], EPS, None, ALU.add), reads=[sm.b], writes=[sm.b])
        S.op("act", lambda e: e.activation(ssn[0:pp], ssn[0:pp], AF.Ln), reads=[sm.b], writes=[sm.b])
        S.op("act", lambda e: e.activation(ssn[0:pp], ssn[0:pp], AF.Exp, scale=-0.5), reads=[sm.b], writes=[sm.b])
        return ssn

    def norm_mod_T(x_tile, gm, sh, dstT3, col0, np_=128, mask=None):
        rstd = rms_rstd(x_tile.ap[0:np_], D, [x_tile.b])
        S.op("dve", lambda e: e.scalar_tensor_tensor(tmp32.ap[0:np_], x_tile.ap[0:np_], rstd[0:np_], gm.ap[0:np_], ALU.mult, ALU.mult),
             reads=[x_tile.b, sm.b, gm.b], writes=[tmp32.b])
        if mask is None:
            S.op("dve", lambda e: e.tensor_tensor(ub.ap[0:np_], tmp32.ap[0:np_], sh.ap[0:np_], ALU.add), reads=[tmp32.b, sh.b], writes=[ub.b])
        else:
            S.op("dve", lambda e: e.scalar_tensor_tensor(ub.ap[0:np_], sh.ap[0:np_], mask, tmp32.ap[0:np_], ALU.mult, ALU.add),
                 reads=[tmp32.b, sh.b, mt.b], writes=[ub.b])
        transpose_to(ub, dstT3, col0, np_)

    def transpose_to(src_bf, dstT3, col0, np_=128, nk=16, kofs=0):
        for half in range(nk // 8):
            bi = 1
            pv = bank16(bi).rearrange("p (k m) -> p k m", m=128)
            S.op("pe", [tp(pv[:, k, 0:np_], src_bf.ap[0:np_, (half * 8 + k) * 128:(half * 8 + k + 1) * 128], identb[0:np_, 0:np_]) for k in range(8)],
                 reads=[src_bf.b, cB.b], writes=[PB[bi]])
            eng = "act" if half % 2 == 0 else "dve"
            dst = dstT3[:, kofs + half * 8:kofs + half * 8 + 8, col0:col0 + np_]
            if eng == "act":
                S.op("act", lambda e, dst=dst, pv=pv: e.activation(dst, pv[:, 0:8, 0:np_], AF.Copy), reads=[PB[bi]], writes=[dstT3_buf[id(dstT3)]])
            else:
                S.op("dve", lambda e, dst=dst, pv=pv: e.tensor_copy(dst, pv[:, 0:8, 0:np_]), reads=[PB[bi]], writes=[dstT3_buf[id(dstT3)]])

    dstT3_buf = {}

    def reg3(tile, k):
        v3 = tile.v("p (k n) -> p k n", k=k)
        dstT3_buf[id(v3)] = tile.b
        return v3

    def silu_to(dst, src, n, rd, wr, np_=128, tmp=None):
        t = tmp if tmp is not None else tmp32
        S.op("act", lambda e: e.activation(t.ap[0:np_, 0:n], src, AF.Exp, scale=-1.0), reads=list(rd), writes=[t.b])
        S.op("dve", lambda e: e.tensor_scalar(t.ap[0:np_, 0:n], t.ap[0:np_, 0:n], 1.0, None, ALU.add), reads=[t.b], writes=[t.b])
        S.op("dve", lambda e: e.reciprocal(t.ap[0:np_, 0:n], t.ap[0:np_, 0:n]), reads=[t.b], writes=[t.b])
        S.op("dve", lambda e: e.tensor_tensor(dst, src, t.ap[0:np_, 0:n], ALU.mult), reads=list(rd) + [t.b], writes=list(wr))

    xs_d = din("xs_d", [16, D])
    ck_d = din("ck_d", [16, 128, 256])
    cv_d = din("cv_d", [16, 128, 256])
    sconv_d = din("sconv_d", [16, 3 * 1536])
    sssm_d = din("sssm_d", [16, 1024, 128])
    convw_raw = din("convw_raw", [1, 4 * 1536])
    convb_raw = din("convb_raw", [1, 1536])
    sinkcol = din("sinkcol", [16, 1])
    sbias = din("sbias", [16, 128])
    selkv = din("selkv", [16, 4])
    dskrow = din("dskrow", [1, 1024])
    ys_out = dout("ys_out", [16, D])
    ks_out = dout("ks_out", [16, 128, 256])
    vs_out = dout("vs_out", [16, 128, 256])
    convs_out = dout("convs_out", [16, 3 * 1536])
    ssms_out = dout("ssms_out", [16, 1024, 128])
    att_d = dscr("att_d", [16, 1024])
    KSO = Buf("ks_out"); VSO = Buf("vs_out"); ATTD = Buf("att_d")
    catsT = A.alloc("catsT", 16 * 16, BF16); catsT3 = reg3(catsT, 16)
    u2sT = A.alloc("u2sT", 16 * 16, BF16); u2sT3 = reg3(u2sT, 16)
    hsT = A.alloc("hsT", NJ * 16, BF16); hsT3 = hsT.v("p (j b) -> p j b", j=NJ)
    x1s = A.alloc("x1s", D)

    def load_mod_rows(tile, ch):
        S.dma("sp", [(tile.ap[0:16, :], mod_d[0:16, ch * D:(ch + 1) * D])], reads=[MODD], writes=[tile.b])

    def phase_SM1():
        m = A.mark()
        smc[0] = 0
        load_mod_rows(gm1, 1)
        load_mod_rows(sh1, 0)
        xt = xb[0]
        S.dma("sp", [(xt.ap[0:16, :], xs_d)], writes=[xt.b])
        usT = A.alloc("usT", 16 * 16, BF16); usT3 = reg3(usT, 16)
        norm_mod_T(xt, gm1, sh1, usT3, 0, np_=16)
        pj = A.alloc("proj_s", INW)
        sel = A.alloc("sel", 16 * 128); sel3 = sel.v("p (b m) -> p b m", b=16)
        cs = A.alloc("cat_s", D)
        cs16 = A.alloc("cat_s16", D, BF16)
        gb = A.alloc("g_bc", 2048)
        xa = A.alloc("xbc_s", 1536)
        S.op("dve", lambda e: e.tensor_copy(sel3[0:16], identf[0:16, 0:16].unsqueeze(2).to_broadcast([16, 16, 128])), reads=[cF.b], writes=[sel.b])
        m1 = A.mark()
        wsl = [A.alloc("wss%d" % i, 16 * 512, BF16) for i in range(2)]
        w3s = [t.v("p (k n) -> p k n", k=16) for t in wsl]
        for pc in range(9):
            n = 512 if pc < 8 else 16
            i = pc % 2
            S.dma("pool", [(w3s[i][:, :, 0:n], w_in_v[:, :, pc * 512:pc * 512 + n])], writes=[wsl[i].b])
            bi = 4 + pc % 4
            o = ps[0:16, bi * 512:bi * 512 + n]
            S.op("pe", [mm(o, usT3[:, kc, 0:16], w3s[i][:, kc, 0:n], kc == 0, kc == 15) for kc in range(16)], reads=[usT.b, wsl[i].b], writes=[PB[bi]])
            S.op("act", lambda e, o=o, pc=pc, n=n: e.activation(pj.ap[0:16, pc * 512:pc * 512 + n], o, AF.Copy), reads=[PB[bi]], writes=[pj.b])
        P = pj.ap
        S.barrier()
        A.release(m1)
        S.dma("sp", [(ks_out[:, 0:127, :], ck_d[:, 1:128, :]), (ks_out[:, 127, :], P[0:16, 1024:1280])], reads=[pj.b], writes=[KSO])
        S.dma("sp", [(vs_out[:, 0:127, :], cv_d[:, 1:128, :]), (vs_out[:, 127, :], P[0:16, 1280:1536])], reads=[pj.b], writes=[VSO])
        S.dma("sp", [(convs_out[:, 0:3072], sconv_d[:, 1536:4608]), (convs_out[:, 3072:4608], P[0:16, 2560:4096])], reads=[pj.b])
        Ka = A.alloc("Ka", 16 * 256); Ka3 = Ka.v("p (b n) -> p b n", b=16)
        S.dma("sp", [(Ka3, ks_out.rearrange("b s n -> s b n"))], reads=[KSO], writes=[Ka.b])
        Vh = A.alloc("Vh", 16 * 256, BF16); Vh3 = Vh.v("p (b n) -> p b n", b=16)
        S.dma("pool", [(Vh3, vs_out.rearrange("b s n -> s b n"))], reads=[VSO], writes=[Vh.b])
        cst = A.alloc("scst", 128 + 8)
        S.dma("sp", [(cst.ap[0:16, 0:128], sbias), (cst.ap[0:16, 128:129], sinkcol), (cst.ap[0:16, 129:133], selkv)], writes=[cst.b])
        sk8c = cst.ap[0:16, 133:134]
        S.op("dve", lambda e: e.tensor_scalar(sk8c, cst.ap[0:16, 128:129], 8.0, None, ALU.mult), reads=[cst.b], writes=[cst.b])
        prod = A.alloc("prod", 1024)
        STt = A.alloc("STt", 16 * 16); ST3 = STt.v("p (b h) -> p b h", b=16)
        for b in range(16):
            for hf in range(2):
                S.op("pe", mm(bank(2 + hf), sel3[0:16, b, :], P[0:16, hf * 512:(hf + 1) * 512], True, True), reads=[sel.b, pj.b], writes=[PB[2 + hf]])
                S.op("dve", lambda e, b=b, hf=hf: e.tensor_tensor(prod.ap[:, hf * 512:(hf + 1) * 512].rearrange("p (k g d) -> p k g d", k=2, g=4),
                                                                 bank(2 + hf).rearrange("p (k g d) -> p k g d", k=2, g=4),
                                                                 Ka3[:, b, hf * 128:(hf + 1) * 128].rearrange("p (k d) -> p k d", k=2).unsqueeze(2).to_broadcast([128, 2, 4, 64]), ALU.mult),
                     reads=[PB[2 + hf], Ka.b], writes=[prod.b])
            S.op("dve", lambda e, b=b: e.tensor_reduce(ST3[:, b, :], prod.v("p (h d) -> p h d", h=16), AX.X, ALU.add), reads=[prod.b], writes=[STt.b])
        for b in range(16):
            S.op("pe", tp(ps[0:16, 4 * 512 + b * 128:4 * 512 + (b + 1) * 128], ST3[:, b, :], identf), reads=[STt.b, cF.b], writes=[PB[4 + b // 4]])
        tsm = A.alloc("tsm", 2048); t3 = tsm.v("p (b s) -> p b s", b=16)
        S.op("dve", lambda e: e.tensor_tensor(t3[0:16], ps[0:16, 2048:4096].rearrange("p (b s) -> p b s", b=16),
                                              cst.ap[0:16, 0:128].unsqueeze(1).to_broadcast([16, 16, 128]), ALU.add),
             reads=[PB[4], PB[5], PB[6], PB[7], cst.b], writes=[tsm.b])
        mxs = small(16); ngs = small(16); rss = small(16); dns = small(16)
        S.op("dve", lambda e: e.tensor_reduce(mxs[0:16], t3[0:16], AX.X, ALU.max), reads=[tsm.b], writes=[sm.b])
        S.op("dve", lambda e: e.tensor_scalar(mxs[0:16], mxs[0:16], sk8c, None, ALU.max), reads=[sm.b, cst.b], writes=[sm.b])
        S.op("dve", lambda e: e.tensor_scalar(ngs[0:16], mxs[0:16], -0.125, None, ALU.mult), reads=[sm.b], writes=[sm.b])
        S.op("dve", lambda e: e.scalar_tensor_tensor(t3[0:16], t3[0:16], 0.125, ngs[0:16].unsqueeze(2).to_broadcast([16, 16, 128]), ALU.mult, ALU.add),
             reads=[tsm.b, sm.b], writes=[tsm.b])
        S.op("act", lambda e: e.activation(tsm.ap[0:16], tsm.ap[0:16], AF.Exp), reads=[tsm.b], writes=[tsm.b])
        S.op("dve", lambda e: e.tensor_reduce(rss[0:16], t3[0:16], AX.X, ALU.add), reads=[tsm.b], writes=[sm.b])
        S.op("dve", lambda e: e.tensor_scalar(dns[0:16], ngs[0:16], cst.ap[0:16, 128:129], None, ALU.add), reads=[sm.b, cst.b], writes=[sm.b])
        S.op("act", lambda e: e.activation(dns[0:16], dns[0:16], AF.Exp), reads=[sm.b], writes=[sm.b])
        S.op("dve", lambda e: e.tensor_tensor(dns[0:16], dns[0:16], rss[0:16], ALU.add), reads=[sm.b], writes=[sm.b])
        S.op("dve", lambda e: e.reciprocal(dns[0:16], dns[0:16]), reads=[sm.b], writes=[sm.b])
        Pb = A.alloc("Pb", 2048, BF16); Pb3 = Pb.v("p (b s) -> p b s", b=16)
        S.op("dve", lambda e: e.tensor_tensor(Pb3[0:16], t3[0:16], dns[0:16].unsqueeze(2).to_broadcast([16, 16, 128]), ALU.mult), reads=[tsm.b, sm.b], writes=[Pb.b])
        pvb = bank16(1).rearrange("p (b h) -> p b h", h=16)
        S.op("pe", [tp(pvb[:, b, :], Pb3[0:16, b, :], identb[0:16, 0:16]) for b in range(16)], reads=[Pb.b, cB.b], writes=[PB[1]])
        PTs = A.alloc("PTs", 256, BF16); PTs3 = PTs.v("p (b h) -> p b h", b=16)
        S.op("act", lambda e: e.activation(PTs3, pvb[:, 0:16, :], AF.Copy), reads=[PB[1]], writes=[PTs.b])
        ah = A.alloc("ah", 16 * 64); ah3 = ah.v("p (b d) -> p b d", b=16)
        t4 = A.alloc("t4s", 8 * 256)
        for half in range(2):
            for bb in range(8):
                b = half * 8 + bb
                S.op("pe", mm(ps[0:16, 4 * 512 + bb * 256:4 * 512 + (bb + 1) * 256], PTs3[:, b, :], Vh3[:, b, :], True, True), reads=[PTs.b, Vh.b], writes=[PB[4 + bb // 2]])
            S.op("dve", lambda e: e.tensor_tensor(t4.ap[0:16].rearrange("p (b k d) -> p b k d", b=8, k=4),
                                                  ps[0:16, 2048:4096].rearrange("p (b k d) -> p b k d", b=8, k=4),
                                                  cst.ap[0:16, 129:133].unsqueeze(1).unsqueeze(3).to_broadcast([16, 8, 4, 64]), ALU.mult),
                 reads=[PB[4], PB[5], PB[6], PB[7], cst.b], writes=[t4.b])
            S.op("dve", lambda e, half=half: e.tensor_reduce(ah3[0:16, half * 8:(half + 1) * 8, :], t4.ap[0:16].rearrange("p (b k d) -> p b d k", b=8, k=4), AX.X, ALU.add),
                 reads=[t4.b], writes=[ah.b])
        S.dma("sp", [(att_d.rearrange("b (h d) -> h b d", h=16), ah3[0:16])], reads=[ah.b], writes=[ATTD])
        S.dma("sp", [(cs.ap[0:16, 0:1024], att_d)], reads=[ATTD], writes=[cs.b])
        S.barrier()
        A.release(m1)
        S.dma("sp", [(gb.ap[0:16, 0:1024], gatt.to_broadcast([16, 1024])), (gb.ap[0:16, 1024:2048], gssm.to_broadcast([16, 1024]))], writes=[gb.b])
        rstd = rms_rstd(cs.ap[0:16, 0:1024], 1024, [cs.b])
        S.op("dve", lambda e: e.scalar_tensor_tensor(cs16.ap[0:16, 0:1024], cs.ap[0:16, 0:1024], rstd[0:16], gb.ap[0:16, 0:1024], ALU.mult, ALU.mult),
             reads=[cs.b, sm.b, gb.b], writes=[cs16.b])
        cwb = A.alloc("cwb", 5 * 1536)
        S.dma("sp", [(cwb.ap[0:16, 0:6144], convw_raw.to_broadcast([16, 6144])), (cwb.ap[0:16, 6144:7680], convb_raw.to_broadcast([16, 1536]))], writes=[cwb.b])
        sc_ = A.alloc("sconv", 3 * 1536)
        S.dma("sp", [(sc_.ap[0:16, :], sconv_d)], writes=[sc_.b])
        xc = A.alloc("xc", 1536); xc2 = A.alloc("xc2", 1536)
        S.op("dve", lambda e: e.tensor_tensor(xc.ap[0:16], P[0:16, 2560:4096], cwb.ap[0:16, 3 * 1536:4 * 1536], ALU.mult), reads=[pj.b, cwb.b], writes=[xc.b])
        S.op("dve", lambda e: e.tensor_tensor(xc.ap[0:16], xc.ap[0:16], cwb.ap[0:16, 6144:7680], ALU.add), reads=[xc.b, cwb.b], writes=[xc.b])
        for i in range(3):
            S.op("dve", lambda e, i=i: e.tensor_tensor(xc2.ap[0:16], sc_.ap[0:16, i * 1536:(i + 1) * 1536], cwb.ap[0:16, i * 1536:(i + 1) * 1536], ALU.mult), reads=[sc_.b, cwb.b], writes=[xc2.b])
            S.op("dve", lambda e: e.tensor_tensor(xc.ap[0:16], xc.ap[0:16], xc2.ap[0:16], ALU.add), reads=[xc.b, xc2.b], writes=[xc.b])
        silu_to(xa.ap[0:16], xc.ap[0:16], 1536, [xc.b], [xa.b], np_=16, tmp=xc2)
        S.barrier()
        A.release(m1)
        tz = A.alloc("tmpz", 1536)
        x0 = small(16); ax = small(16); dts = small(16); dAs = small(16)
        S.op("dve", lambda e: e.tensor_tensor(x0[0:16], P[0:16, 4096:4112], dtb_bc[0:16], ALU.add), reads=[pj.b, hv.b], writes=[sm.b])
        S.op("act", lambda e: e.activation(ax[0:16], x0[0:16], AF.Abs), reads=[sm.b], writes=[sm.b])
        S.op("act", lambda e: e.activation(ax[0:16], ax[0:16], AF.Exp, scale=-1.0), reads=[sm.b], writes=[sm.b])
        S.op("dve", lambda e: e.tensor_scalar(ax[0:16], ax[0:16], 1.0, None, ALU.add), reads=[sm.b], writes=[sm.b])
        S.op("act", lambda e: e.activation(ax[0:16], ax[0:16], AF.Ln), reads=[sm.b], writes=[sm.b])
        S.op("dve", lambda e: e.scalar_tensor_tensor(dts[0:16], x0[0:16], 0.0, ax[0:16], ALU.max, ALU.add), reads=[sm.b], writes=[sm.b])
        S.op("dve", lambda e: e.tensor_tensor(dAs[0:16], dts[0:16], a_bc.ap[0:16], ALU.mult), reads=[sm.b, a_bc.b], writes=[sm.b])
        S.op("act", lambda e: e.activation(dAs[0:16], dAs[0:16], AF.Exp), reads=[sm.b], writes=[sm.b])
        XE = A.alloc("XE", 2048)
        S.op("dve", lambda e: e.tensor_tensor(XE.ap[0:16, 0:1024].rearrange("p (h d) -> p h d", h=16), xa.ap[0:16, 0:1024].rearrange("p (h d) -> p h d", h=16),
                                              dts[0:16].unsqueeze(2).to_broadcast([16, 16, 64]), ALU.mult), reads=[xa.b, sm.b], writes=[XE.b])
        S.op("dve", lambda e: e.tensor_copy(XE.ap[0:16, 1024:2048].rearrange("p (h d) -> p h d", h=16), dAs[0:16].unsqueeze(2).to_broadcast([16, 16, 64])),
             reads=[sm.b], writes=[XE.b])
        XT = A.alloc("XT", 16 * 16); XT3 = XT.v("p (j b) -> p j b", j=16)
        S.op("pe", [tp(bank(0, 16, j * 16), XE.ap[0:16, j * 128:(j + 1) * 128], identf[0:16, 0:16]) for j in range(16)], reads=[XE.b, cF.b], writes=[PB[0]])
        S.op("act", lambda e: e.activation(XT.ap, bank(0, 256), AF.Copy), reads=[PB[0]], writes=[XT.b])
        yT = A.alloc("yT", 8 * 16); yT3 = yT.v("p (j b) -> p j b", j=8)
        h0 = [A.alloc("h0_%d" % i, 1024) for i in range(2)]
        h1 = [A.alloc("h1_%d" % i, 1024) for i in range(2)]
        for b in range(16):
            ht = h0[b % 2]; hn = h1[b % 2]
            S.dma("sp", [(ht.v("p (j n) -> p j n", j=8), sssm_d[b].rearrange("(j q) n -> q j n", q=128))], writes=[ht.b])
            S.op("pe", mm(bank(3), sel3[0:16, b, :], xa.ap[0:16, 1024:1536], True, True), reads=[sel.b, xa.b], writes=[PB[3]])
            S.op("dve", lambda e, ht=ht, b=b: e.tensor_tensor(ht.v("p (j n) -> p j n", j=8), ht.v("p (j n) -> p j n", j=8),
                                                             XT3[:, 8:16, b].unsqueeze(2).to_broadcast([128, 8, 128]), ALU.mult), reads=[ht.b, XT.b], writes=[ht.b])
            S.op("dve", lambda e, hn=hn, b=b: e.tensor_tensor(hn.v("p (g r n) -> p g r n", g=2, r=4),
                                                             bank(3, 256).rearrange("p (g n) -> p g n", g=2).unsqueeze(2).to_broadcast([128, 2, 4, 128]),
                                                             XT3[:, 0:8, b].rearrange("p (g r) -> p g r", g=2).unsqueeze(3).to_broadcast([128, 2, 4, 128]), ALU.mult),
                 reads=[PB[3], XT.b], writes=[hn.b])
            S.op("dve", lambda e, hn=hn, ht=ht: e.tensor_tensor(hn.ap, hn.ap, ht.ap, ALU.add), reads=[hn.b, ht.b], writes=[hn.b])
            S.dma("sp", [(ssms_out[b].rearrange("(j q) n -> q j n", q=128), hn.v("p (j n) -> p j n", j=8))], reads=[hn.b])
            S.op("dve", lambda e, hn=hn, ht=ht: e.tensor_tensor(ht.v("p (g r n) -> p g r n", g=2, r=4), hn.v("p (g r n) -> p g r n", g=2, r=4),
                                                               bank(3, 256, 256).rearrange("p (g n) -> p g n", g=2).unsqueeze(2).to_broadcast([128, 2, 4, 128]), ALU.mult),
                 reads=[hn.b, PB[3]], writes=[ht.b])
            S.op("dve", lambda e, ht=ht, b=b: e.tensor_reduce(yT3[:, :, b], ht.v("p (j n) -> p j n", j=8), AX.X, ALU.add), reads=[ht.b], writes=[yT.b])
        S.op("pe", [tp(ps[0:16, 2 * 512 + j * 128:2 * 512 + (j + 1) * 128], yT3[:, j, :], identf) for j in range(8)], reads=[yT.b, cF.b], writes=[PB[2], PB[3]])
        ys = A.alloc("y_s", 1024)
        dkb = A.alloc("dkb", 1024)
        S.dma("sp", [(dkb.ap[0:16, :], dskrow.to_broadcast([16, 1024]))], writes=[dkb.b])
        S.op("dve", lambda e: e.tensor_tensor(ys.ap[0:16], xa.ap[0:16, 0:1024], dkb.ap[0:16], ALU.mult), reads=[xa.b, dkb.b], writes=[ys.b])
        S.op("dve", lambda e: e.tensor_tensor(ys.ap[0:16], ys.ap[0:16], ps[0:16, 1024:2048], ALU.add), reads=[ys.b, PB[2], PB[3]], writes=[ys.b])
        zt = A.alloc("z_s", 1024)
        silu_to(zt.ap[0:16], P[0:16, 1536:2560], 1024, [pj.b], [zt.b], np_=16, tmp=tz)
        S.op("dve", lambda e: e.tensor_tensor(ys.ap[0:16], ys.ap[0:16], zt.ap[0:16], ALU.mult), reads=[ys.b, zt.b], writes=[ys.b])
        rstd2 = rms_rstd(ys.ap[0:16], 1024, [ys.b])
        S.op("dve", lambda e: e.scalar_tensor_tensor(cs16.ap[0:16, 1024:2048], ys.ap[0:16], rstd2[0:16], gb.ap[0:16, 1024:2048], ALU.mult, ALU.mult),
             reads=[ys.b, sm.b, gb.b], writes=[cs16.b])
        transpose_to(cs16, catsT3, 0, np_=16)
        S.barrier()
        A.release(m)
    m_gm = A.mark()
    gm1 = A.alloc("gm1", D)
    sh1 = A.alloc("sh1", D)
    load_mod_bc(gm1, 1)
    load_mod_bc(sh1, 0)
    mS = A.mark()
    u2T_d = dscr("u2T_d", [16, 128, NOWN], BF16)
    CATD = Buf("cat_d"); X1D = Buf("x1_d"); U2TD = Buf("u2T_d"); HTD = Buf("hT_d")
    w_dt = A.alloc("w_dt", 16 * 16, BF16)
    w_dt3 = w_dt.v("p (k n) -> p k n", k=16)
    S.dma("pool", [(w_dt3, w_in_v[:, :, 4096:4112])], writes=[w_dt.b])
    A_mrow = [A.alloc("mrow_t", 512)]
    A_xp = [A.alloc("xp%d" % i, 515) for i in range(2)]
    A_acc = [A.alloc("acc%d" % i, 512) for i in range(2)]
    A_st = [A.alloc("silt", 512)]
    A_pt = [A.alloc("ptmp", 512)]
    A_xst = [A.alloc("xs_tok", 1024)]
    A_bt = [A.alloc("Btok", 256, BF16)]
    A_xd = [A.alloc("Xd", 1024, BF16)]

    def conv_tiles(uT3, uTb, T, col_tok0, dests, cts, wfn, load_mask=True):
        for ct in cts:
            bi = 4 + ct % 4
            o = bank(bi, T)
            wb = wfn(ct, 0)[1]
            S.op("pe", [mm(o, wfn(ct, kc)[0], uT3[:, kc, 0:T], kc == 0, kc == 15) for kc in range(16)],
                 reads=[wb, uTb], writes=[PB[bi]])
            xp = A_xp[ct % 2]
            S.op("act", lambda e, xp=xp, ct=ct: e.activation(xp.ap[:, 0:3], halo3[:, ct, :], AF.Copy), reads=[halo.b], writes=[xp.b])
            S.op("act", lambda e, xp=xp, o=o: e.activation(xp.ap[:, 3:3 + T], o, AF.Copy), reads=[PB[bi]], writes=[xp.b])
            S.op("act", lambda e, xp=xp, ct=ct: e.activation(halo3[:, ct, :], xp.ap[:, T:T + 3], AF.Copy), reads=[xp.b], writes=[halo.b])
            acc = A_acc[ct % 2]
            S.op("act", lambda e, xp=xp, acc=acc, ct=ct: e.activation(acc.ap[:, 0:T], xp.ap[:, 3:3 + T], AF.Identity, bias=cbi.ap[:, ct:ct + 1], scale=cw.ap[:, ct * 4 + 3:ct * 4 + 4]),
                 reads=[xp.b, cw.b, cbi.b], writes=[acc.b])
            for i in (2, 1, 0):
                S.op("dve", lambda e, xp=xp, acc=acc, ct=ct, i=i: e.scalar_tensor_tensor(acc.ap[:, 0:T], xp.ap[:, i:i + T], cw.ap[:, ct * 4 + i:ct * 4 + i + 1], acc.ap[:, 0:T], ALU.mult, ALU.add),
                     reads=[xp.b, cw.b, acc.b], writes=[acc.b])
            dst, db = dests(ct)
            silu_to(dst, acc.ap[:, 0:T], T, [acc.b], [db], tmp=(A_st[0] if ct % 2 == 0 else A_pt[0]))

    def dt_group(uT3, uTb, nb, blk_ext0, own=False, c0=0):
        smc[0] = 0
        W = 16 * nb
        o = bank(0, W)
        for b in range(nb):
            S.op("pe", [mm(o[:, b * 16:(b + 1) * 16], uT3[:, kc, c0 + b * 128:c0 + (b + 1) * 128], w_dt3[:, kc, :], kc == 0, kc == 15) for kc in range(16)],
                 reads=[uTb, w_dt.b], writes=[PB[0]])
        x0 = small(W); ax = small(W); dt = small(W); dA = small(W)
        v3 = lambda ap: ap.rearrange("p (b h) -> p b h", b=nb)
        S.op("dve", lambda e: e.tensor_tensor(v3(x0), v3(o), dtb_bc.unsqueeze(1).to_broadcast([128, nb, 16]), ALU.add), reads=[PB[0], hv.b], writes=[sm.b])
        S.op("act", lambda e: e.activation(ax, x0, AF.Abs), reads=[sm.b], writes=[sm.b])
        S.op("act", lambda e: e.activation(ax, ax, AF.Exp, scale=-1.0), reads=[sm.b], writes=[sm.b])
        S.op("dve", lambda e: e.tensor_scalar(ax, ax, 1.0, None, ALU.add), reads=[sm.b], writes=[sm.b])
        S.op("act", lambda e: e.activation(ax, ax, AF.Ln), reads=[sm.b], writes=[sm.b])
        S.op("dve", lambda e: e.scalar_tensor_tensor(dt, x0, 0.0, ax, ALU.max, ALU.add), reads=[sm.b], writes=[sm.b])
        S.op("dve", lambda e: e.tensor_tensor(v3(dt), v3(dt), mt.ap[:, blk_ext0:blk_ext0 + nb].unsqueeze(2).to_broadcast([128, nb, 16]), ALU.mult), reads=[sm.b, mt.b], writes=[sm.b])
        S.op("dve", lambda e: e.tensor_tensor(v3(dA), v3(dt), a_bc.ap.unsqueeze(1).to_broadcast([128, nb, 16]), ALU.mult), reads=[sm.b, a_bc.b], writes=[sm.b])
        o2 = bank(0, 2 * W, 64)
        S.op("pe", [mm(o2[:, 0:W], Umat, dA, True, True), mm(o2[:, W:2 * W], onesf, dA, True, True)], reads=[cF.b, sm.b], writes=[PB[0]])
        at = small(2 * W)
        S.op("act", lambda e: e.activation(at, o2, AF.Copy), reads=[PB[0]], writes=[sm.b])
        acum, tot = at[:, 0:W], at[:, W:2 * W]
        de = small(W); cd = small(W); w1 = small(W)
        S.op("dve", lambda e: e.tensor_tensor(de, tot, acum, ALU.subtract), reads=[sm.b], writes=[sm.b])
        S.op("act", lambda e: e.activation(de, de, AF.Exp), reads=[sm.b], writes=[sm.b])
        S.op("act", lambda e: e.activation(cd, tot, AF.Exp), reads=[sm.b], writes=[sm.b])
        S.op("dve", lambda e: e.tensor_tensor(w1, dt, de, ALU.mult), reads=[sm.b], writes=[sm.b])
        ea = None
        if own:
            ea = small(W)
            S.op("act", lambda e: e.activation(ea, acum, AF.Exp), reads=[sm.b], writes=[sm.b])
        res = []
        for b in range(nb):
            sl = slice(b * 16, (b + 1) * 16)
            r = dict(dt=dt[:, sl], dA=dA[:, sl], acum=acum[:, sl], tot=tot[:, sl], cd=cd[:, sl], w1=w1[:, sl])
            if own:
                r["ea"] = ea[:, sl]
            res.append(r)
        return res

    def state_part1(xsT3, xsTb, BT3, BTb, c0):
        xt_ = A_xst[0]
        S.op("pe", [tp(bank(2 + i // 4, 128, (i % 4) * 128), xsT3[:, i, c0:c0 + 128], identf) for i in range(8)],
             reads=[xsTb, cF.b], writes=[PB[2], PB[3]])
        S.op("act", lambda e: e.activation(xt_.ap[:, 0:512], bank(2), AF.Copy), reads=[PB[2]], writes=[xt_.b])
        S.op("act", lambda e: e.activation(xt_.ap[:, 512:1024], bank(3), AF.Copy), reads=[PB[3]], writes=[xt_.b])
        S.op("pe", [tp(bank(0, 128, 128 + g * 128), BT3[:, g, c0:c0 + 128], identf) for g in range(2)], reads=[BTb, cF.b], writes=[PB[0]])
        Bt = A_bt[0]
        S.op("act", lambda e: e.activation(Bt.ap, bank(0, 256, 128), AF.Copy), reads=[PB[0]], writes=[Bt.b])
        return xt_, Bt

    def state_part2(xt_, Bt, sc_, keep_hb):
        Xd = A_xd[0]
        S.op("dve", lambda e: e.tensor_tensor(Xd.v("p (h d) -> p h d", h=16), xt_.v("p (h d) -> p h d", h=16),
                                              sc_["w1"].unsqueeze(2).to_broadcast([128, 16, 64]), ALU.mult),
             reads=[xt_.b, sm.b], writes=[Xd.b])
        S.op("pe", [mm(bank(6 + g), Bt.ap[:, g * 128:(g + 1) * 128], Xd.ap[:, g * 512:(g + 1) * 512], True, True) for g in range(2)],
             reads=[Bt.b, Xd.b], writes=[PB[6], PB[7]])
        S.op("dve", lambda e: e.tensor_tensor(hst.v("p (h d) -> p h d", h=16), hst.v("p (h d) -> p h d", h=16),
                                              sc_["cd"].unsqueeze(2).to_broadcast([128, 16, 64]), ALU.mult),
             reads=[hst.b, sm.b], writes=[hst.b])
        S.op("dve", lambda e: e.tensor_tensor(hst.ap[:, 0:512], hst.ap[:, 0:512], bank(6), ALU.add), reads=[hst.b, PB[6]], writes=[hst.b])
        S.op("dve", lambda e: e.tensor_tensor(hst.ap[:, 512:1024], hst.ap[:, 512:1024], bank(7), ALU.add), reads=[hst.b, PB[7]], writes=[hst.b])
        if keep_hb:
            S.op("act", lambda e: e.activation(hb.ap, hst.ap, AF.Copy), reads=[hst.b], writes=[hb.b])

    def load_x_norm(tok0, nblk, uT3, masked=False):
        for b in range(nblk):
            xt = xb[b % 2]
            S.dma("sp", [(xt.ap, xext[tok0 + b * 128:tok0 + (b + 1) * 128, :])], writes=[xt.b])
            blk = tok0 // 128 + b
            norm_mod_T(xt, gm1, sh1, uT3, b * 128, mask=(mt.ap[:, blk:blk + 1] if masked else None))

    mA = A.mark()
    wx = A.alloc("wx", 16 * 1536, BF16)
    wx3 = wx.v("p (k n) -> p k n", k=16)
    S.dma("pool", [(wx3[:, 0:8, :], w_in_v[:, 0:8, 2560:4096]), (wx3[:, 8:16, :], w_in_v[:, 8:16, 2560:4096])], writes=[wx.b])
    uTa = A.alloc("uTa", 16 * 512, BF16)
    uTa3 = reg3(uTa, 16)
    xsTa = A.alloc("xsTa", 8 * 512)
    xsTa3 = xsTa.v("p (k n) -> p k n", k=8)
    BTa = A.alloc("BTa", 2 * 512)
    BTa3 = BTa.v("p (k n) -> p k n", k=2)

    CTd = A.alloc("CTdummy", 2 * 512)
    CTd3 = CTd.v("p (k n) -> p k n", k=2)

    def destsA(ct):
        if ct < 8:
            return xsTa3[:, ct, 0:512], xsTa.b
        if ct < 10:
            return BTa3[:, ct - 8, 0:512], BTa.b
        return CTd3[:, ct - 10, 0:512], CTd.b

    for g in range(NPRE_G - n_pre_groups, NPRE_G):
        load_x_norm(g * 512, 4, uTa3, masked=True)
        conv_tiles(uTa3, uTa.b, 512, g * 512, destsA, range(12 if g == NPRE_G - 1 else 10), lambda ct, kc: (wx3[:, kc, ct * 128:(ct + 1) * 128], wx.b))
        scs = dt_group(uTa3, uTa.b, 4, g * 4)
        for b in range(4):
            sc_ = scs[b]
            xt_, Bt = state_part1(xsTa3, xsTa.b, BTa3, BTa.b, b * 128)
            state_part2(xt_, Bt, sc_, keep_hb=(g == NPRE_G - 1 and b == 3))
    S.barrier()
    A.release(mA)

    OWN0 = NEXT - NOWN
    def phase_B1b():
        m = A.mark()
        slots = [A.alloc("ws%d" % i, 16 * 512, BF16) for i in range(2)]
        s3 = [t.v("p (k n) -> p k n", k=16) for t in slots]
        uT = A.alloc("uTb", 16 * 256, BF16); uT3 = reg3(uT, 16)
        xsT = A.alloc("xsTb", 8 * 256); xsT3 = xsT.v("p (k n) -> p k n", k=8)
        BT = A.alloc("BTb", 2 * 256); BT3 = BT.v("p (k n) -> p k n", k=2)
        BTh = A.alloc("BTh", 2 * 256, BF16); BTh3 = BTh.v("p (k n) -> p k n", k=2)
        CTh = A.alloc("CTh", 2 * 256, BF16); CTh3 = CTh.v("p (k n) -> p k n", k=2)
        zs = A.alloc("zs", 2 * 1024); zs3 = zs.v("p (b n) -> p b n", b=2)
        gs = A.alloc("gssm", 1024)
        S.dma("sp", [(gs.ap, gssm.to_broadcast([128, 1024]))], writes=[gs.b])
        Rt = A.alloc("Rt", 2048); R3 = Rt.v("p (h l) -> p h l", h=16)
        Et = A.alloc("Et", 2048, BF16)
        MTt = A.alloc("MTt", 2048, BF16); MT3 = MTt.v("p (h l) -> p h l", h=16)
        CBm = A.alloc("CBm", 256)
        Xb = A.alloc("Xb", 1024, BF16)
        yt = A.alloc("yt", 1024)
        y2 = A.alloc("y2", 1024)
        sso = A.alloc("sso", 1024, BF16)
        si = [0]

        def next_slot(src_cols, ncols=512):
            i = si[0] % 2
            si[0] += 1
            S.dma("pool", [(s3[i][:, :, 0:ncols], w_in_v[:, :, src_cols:src_cols + ncols])], writes=[slots[i].b])
            return s3[i], slots[i].b

        def dests(ct):
            if ct < 8:
                return xsT3[:, ct, 0:256], xsT.b
            if ct < 10:
                return BT3[:, ct - 8, 0:256], BT.b
            return CTh3[:, ct - 10, 0:256], CTh.b

        for og in range(NOWN // 256):
            tok0 = OWN0 + og * 256
            load_x_norm(tok0, 2, uT3)
            for pc in range(3):
                w3, wb = next_slot(2560 + pc * 512)
                conv_tiles(uT3, uT.b, 256, tok0, dests, range(pc * 4, pc * 4 + 4),
                           lambda ct, kc, w3=w3, wb=wb, pc=pc: (w3[:, kc, (ct - pc * 4) * 128:(ct - pc * 4 + 1) * 128], wb), load_mask=(pc == 0))
            S.op("act", lambda e: e.activation(BTh.ap, BT.ap, AF.Copy), reads=[BT.b], writes=[BTh.b])
            for pc in range(2):
                w3, wb = next_slot(1536 + pc * 512)
                for b in range(2):
                    bi = 4 + (pc * 2 + b) % 4
                    S.op("pe", [mm(bank(bi), uT3[:, kc, b * 128:(b + 1) * 128], w3[:, kc, :], kc == 0, kc == 15) for kc in range(16)],
                         reads=[uT.b, wb], writes=[PB[bi]])
                    silu_to(zs3[:, b, pc * 512:(pc + 1) * 512], bank(bi), 512, [PB[bi]], [zs.b], tmp=A_st[0])
            for b in range(2):
                c0 = b * 128
                sc_ = dt_group(uT3, uT.b, 1, tok0 // 128 + b, own=True, c0=c0)[0]
                xt_, Bt = state_part1(xsT3, xsT.b, BT3, BT.b, c0)
                S.op("dve", lambda e: e.tensor_tensor(R3, Umat.unsqueeze(1).to_broadcast([128, 16, 128]),
                                                      sc_["dA"].unsqueeze(2).to_broadcast([128, 16, 128]), ALU.mult),
                     reads=[cF.b, sm.b], writes=[Rt.b])
                for q in range(4):
                    S.op("pe", mm(bank(4 + q), UTmat, Rt.ap[:, q * 512:(q + 1) * 512], True, True), reads=[cF.b, Rt.b], writes=[PB[4 + q]])
                    S.op("act", lambda e, q=q: e.activation(Et.ap[:, q * 512:(q + 1) * 512], bank(4 + q), AF.Exp), reads=[PB[4 + q]], writes=[Et.b])
                S.op("pe", [mm(bank(0, 128, 256 + g * 128), BTh3[:, g, c0:c0 + 128], CTh3[:, g, c0:c0 + 128], True, True) for g in range(2)],
                     reads=[BTh.b, CTh.b], writes=[PB[0]])
                S.op("dve", lambda e: e.tensor_tensor(CBm.v("p (g l) -> p g l", g=2), bank(0, 256, 256).rearrange("p (g l) -> p g l", g=2),
                                                      caus01.unsqueeze(1).to_broadcast([128, 2, 128]), ALU.mult),
                     reads=[PB[0], cF.b], writes=[CBm.b])
                S.op("dve", lambda e: e.tensor_tensor(MTt.v("p (g r l) -> p g r l", g=2, r=8), Et.v("p (g r l) -> p g r l", g=2, r=8),
                                                      CBm.v("p (g l) -> p g l", g=2).unsqueeze(2).to_broadcast([128, 2, 8, 128]), ALU.mult),
                     reads=[Et.b, CBm.b], writes=[MTt.b])
                S.op("dve", lambda e: e.tensor_tensor(Xb.v("p (h d) -> p h d", h=16), xt_.v("p (h d) -> p h d", h=16),
                                                      sc_["dt"].unsqueeze(2).to_broadcast([128, 16, 64]), ALU.mult),
                     reads=[xt_.b, sm.b], writes=[Xb.b])
                S.op("pe", [mm(bank(2 + h // 8, 64, (h % 8) * 64), MT3[:, h, :], Xb.ap[:, h * 64:(h + 1) * 64], True, True) for h in range(16)],
                     reads=[MTt.b, Xb.b], writes=[PB[2], PB[3]])
                S.op("pe", [mm(bank(4 + g), CTh3[:, g, c0:c0 + 128], hb.ap[:, g * 512:(g + 1) * 512], True, True) for g in range(2)],
                     reads=[CTh.b, hb.b], writes=[PB[4], PB[5]])
                for g in range(2):
                    S.op("dve", lambda e, g=g: e.tensor_tensor(yt.ap[:, g * 512:(g + 1) * 512].rearrange("p (h d) -> p h d", h=8),
                                                               bank(4 + g).rearrange("p (h d) -> p h d", h=8),
                                                               sc_["ea"][:, g * 8:(g + 1) * 8].unsqueeze(2).to_broadcast([128, 8, 64]), ALU.mult),
                         reads=[PB[4 + g], sm.b], writes=[yt.b])
                    S.op("dve", lambda e, g=g: e.tensor_tensor(yt.ap[:, g * 512:(g + 1) * 512], yt.ap[:, g * 512:(g + 1) * 512], bank(2 + g), ALU.add),
                         reads=[PB[2 + g], yt.b], writes=[yt.b])
                S.op("dve", lambda e: e.tensor_tensor(y2.v("p (h d) -> p h d", h=16), xt_.v("p (h d) -> p h d", h=16),
                                                      dsk_bc.unsqueeze(2).to_broadcast([128, 16, 64]), ALU.mult),
                     reads=[xt_.b, hv.b], writes=[y2.b])
                S.op("dve", lambda e: e.tensor_tensor(yt.ap, yt.ap, y2.ap, ALU.add), reads=[yt.b, y2.b], writes=[yt.b])
                S.op("dve", lambda e, b=b: e.tensor_tensor(yt.ap, yt.ap, zs3[:, b, :], ALU.mult), reads=[yt.b, zs.b], writes=[yt.b])
                rstd = rms_rstd(yt.ap, 1024, [yt.b])
                S.op("dve", lambda e, rstd=rstd: e.scalar_tensor_tensor(sso.ap, yt.ap, rstd, gs.ap, ALU.mult, ALU.mult), reads=[yt.b, sm.b, gs.b], writes=[sso.b])
                tk = og * 256 + c0
                S.dma("sp", [(cat_d[tk:tk + 128, 1024:2048], sso.ap)], reads=[sso.b], writes=[CATD])
                state_part2(xt_, Bt, sc_, keep_hb=True)
        so = yt
        S.op("pe", [tp(bank(2 + i // 4, 128, (i % 4) * 128), hst.ap[:, i * 128:(i + 1) * 128], identf) for i in range(8)],
             reads=[hst.b, cF.b], writes=[PB[2], PB[3]])
        S.op("act", lambda e: e.activation(so.ap[:, 0:512], bank(2), AF.Copy), reads=[PB[2]], writes=[so.b])
        S.op("act", lambda e: e.activation(so.ap[:, 512:1024], bank(3), AF.Copy), reads=[PB[3]], writes=[so.b])
        S.dma("sp", [(ssm_out.rearrange("(i p) n -> p i n", p=128), so.v("p (i n) -> p i n", i=8))], reads=[so.b])
        co = Rt
        S.op("pe", [tp(ps[0:3, 4 * 512 + ct * 128:4 * 512 + (ct + 1) * 128], halo3[:, ct, :], identf) for ct in range(12)],
             reads=[halo.b, cF.b], writes=[PB[4], PB[5], PB[6]])
        S.op("act", lambda e: e.activation(co.ap[0:3, 0:1536], ps[0:3, 4 * 512:4 * 512 + 1536], AF.Copy), reads=[PB[4], PB[5], PB[6]], writes=[co.b])
        S.dma("sp", [(conv_out, co.ap[0:3, 0:1536])], reads=[co.b])
        S.barrier()
        A.release(m)

    phase_B1b()
    A.release(mS)

    def phase_B1a():
        m = A.mark()
        slots = [A.alloc("wq%d" % i, 16 * 512, BF16) for i in range(2)]
        s3 = [t.v("p (k n) -> p k n", k=16) for t in slots]
        uT = A.alloc("uTq", 16 * 512, BF16); uT3 = reg3(uT, 16)
        uTh = A.alloc("uTh", 16 * 128, BF16); uTh3 = reg3(uTh, 16)
        qT = A.alloc("qT", 8 * 512, BF16); qT3 = qT.v("p (k n) -> p k n", k=8)
        kT = A.alloc("kT", 2 * 4 * 640, BF16); kT4 = kT.v("p (e k n) -> p e k n", e=2, k=4)
        Vt = A.alloc("Vt", 5 * 256, BF16); V3 = Vt.v("p (s n) -> p s n", s=5)
        ab1t = A.alloc("ab", 4096)
        S.dma("sp", [(ab1t.ap, abias0)], writes=[ab1t.b])
        ga = A.alloc("gatt", 1024)
        S.dma("sp", [(ga.ap, gatt.to_broadcast([128, 1024]))], writes=[ga.b])
        sk8 = A.alloc("sk8", 16)
        S.op("dve", lambda e: e.tensor_scalar(sk8.ap, sink_bc, 8.0, None, ALU.mult), reads=[hv.b], writes=[sk8.b])
        tS = A.alloc("tS", 1024)
        Pt = A.alloc("Pt", 1024, BF16)
        PT = A.alloc("PT", 1024, BF16)
        at_ = A.alloc("att", 1024)
        ao = A.alloc("atto", 1024, BF16)
        kvo = A.alloc("kvo", 512)
        si = [0]

        def kv_load():
            j = 0
            S.dma("pool", [(s3[j], w_in_v[:, :, 1024:1536])], writes=[slots[j].b])
            return (s3[j], slots[j].b)

        def k_dup(wv, eo):
            i = 1
            d4 = slots[i].v("p (k v n) -> p k v n", k=16, v=4)
            src4 = wv[0][:, :, 0:256].rearrange("p k (v n) -> p k v n", v=4)
            S.op("dve", lambda e: e.memset(slots[i].ap, 0.0), writes=[slots[i].b])
            for kc in range(16):
                if kc % 2 == 0:
                    S.op("act", lambda e, kc=kc: e.activation(d4[:, kc, :, eo * 64:eo * 64 + 64], src4[:, kc, :, :], AF.Copy), reads=[wv[1]], writes=[slots[i].b])
                else:
                    S.op("dve", lambda e, kc=kc: e.tensor_copy(d4[:, kc, :, eo * 64:eo * 64 + 64], src4[:, kc, :, :]), reads=[wv[1]], writes=[slots[i].b])
            return (s3[i], slots[i].b)

        def k_proj(wk, eo, u3, ub_, T, col0):
            for kv in range(4):
                bi = 4 + kv
                S.op("pe", [mm(bank(bi, T), wk[0][:, kc, kv * 128:(kv + 1) * 128], u3[:, kc, 0:T], kc == 0, kc == 15) for kc in range(16)],
                     reads=[wk[1], ub_], writes=[PB[bi]])
                S.op("dve" if kv % 2 else "act",
                     (lambda e, kv=kv, bi=bi: e.tensor_copy(kT4[:, eo, kv, col0:col0 + T], bank(bi, T))) if kv % 2 else
                     (lambda e, kv=kv, bi=bi: e.activation(kT4[:, eo, kv, col0:col0 + T], bank(bi, T), AF.Copy)), reads=[PB[bi]], writes=[kT.b])

        vcnt = [0]

        def v_proj(wv, u3, ub_, c0, slot, keep32=None):
            bi = 2 + vcnt[0] % 2
            vcnt[0] += 1
            S.op("pe", [mm(bank(bi, 256), u3[:, kc, c0:c0 + 128], wv[0][:, kc, 256:512], kc == 0, kc == 15) for kc in range(16)],
                 reads=[wv[1], ub_], writes=[PB[bi]])
            S.op("dve", lambda e: e.tensor_copy(V3[:, slot, :], bank(bi, 256)), reads=[PB[bi]], writes=[Vt.b])
            if keep32 is not None:
                S.op("dve", lambda e: e.tensor_copy(keep32, bank(bi, 256)), reads=[PB[bi]], writes=[kvo.b])

        import os as _os
        NSTEP = int(_os.environ.get("B1A_STEPS", "99"))
        for og in range(4):
            tok0 = OWN0 + og * 512
            if NSTEP < 2: break
            if og == 0:
                xt = xb[0]
                S.dma("sp", [(xt.ap, xext[OWN0 - 128:OWN0, :])], writes=[xt.b])
                norm_mod_T(xt, gm1, sh1, uTh3, 0)
            if NSTEP < 3: break
            load_x_norm(tok0, 4, uT3)
            if NSTEP < 4: break
            wv = kv_load()
            for eo in range(2):
                wk = k_dup(wv, eo)
                if og == 0:
                    k_proj(wk, eo, uTh3, uTh.b, 128, 0)
                k_proj(wk, eo, uT3, uT.b, 512, 128)
            if og == 0:
                v_proj(wv, uTh3, uTh.b, 0, 0)
            if NSTEP < 7: break
            for b in range(int(_os.environ.get("B1A_VN", "4"))):
                v_proj(wv, uT3, uT.b, b * 128, 1 + b, keep32=(kvo.ap[:, 256:512] if (og == 3 and b == 3) else None))
            if og == 3:
                S.op("pe", [mm(bank(3, 256), uT3[:, kc, 384:512], wv[0][:, kc, 0:256], kc == 0, kc == 15) for kc in range(16)],
                     reads=[wv[1], uT.b], writes=[PB[3]])
                S.op("dve", lambda e: e.tensor_copy(kvo.ap[:, 0:256], bank(3, 256)), reads=[PB[3]], writes=[kvo.b])
                S.dma("sp", [(k_out, kvo.ap[:, 0:256]), (v_out, kvo.ap[:, 256:512])], reads=[kvo.b])
            if NSTEP < 8: break
            for pc in range(2):
                i = 1 - pc
                S.dma("pool", [(s3[i], w_in_v[:, :, pc * 512:(pc + 1) * 512])], writes=[slots[i].b])
                for t4 in range(4):
                    bi = 4 + t4
                    S.op("pe", [mm(bank(bi), s3[i][:, kc, t4 * 128:(t4 + 1) * 128], uT3[:, kc, :], kc == 0, kc == 15) for kc in range(16)],
                         reads=[slots[i].b, uT.b], writes=[PB[bi]])
                    S.op("act" if t4 % 2 == 0 else "dve",
                         (lambda e, t4=t4, bi=bi, pc=pc: e.activation(qT3[:, pc * 4 + t4, :], bank(bi), AF.Copy)) if t4 % 2 == 0 else
                         (lambda e, t4=t4, bi=bi, pc=pc: e.tensor_copy(qT3[:, pc * 4 + t4, :], bank(bi))),
                         reads=[PB[bi]], writes=[qT.b])
            ATT = int(_os.environ.get("ATT_STEPS", "99"))
            for b in range(4):
                if stop_after == "B1a_proj":
                    break
                if ATT < 99 and (og > 0 or b > 0):
                    break
                smc[0] = 0
                abt = ab1t
                if og == 0 and b == 1:
                    S.dma("sp", [(ab1t.ap, abias1)], writes=[ab1t.b])
                c0 = b * 128
                mx = small(16); ngm = small(16); rs = small(16)
                for kvg in range(4):
                    b0 = 4 + 2 * (kvg % 2)
                    mms = []
                    for j in range(4):
                        h = kvg * 4 + j
                        mms.append(mm(bank(b0 + j // 2, 256, (j % 2) * 256), qT3[:, h // 2, c0:c0 + 128],
                                      kT4[:, h % 2, kvg, c0:c0 + 256], True, True))
                    S.op("pe", mms, reads=[qT.b, kT.b], writes=[PB[b0], PB[b0 + 1]])
                    if ATT < 1: continue
                    for hh in range(2):
                        S.op("dve", lambda e, hh=hh, kvg=kvg, b0=b0: e.tensor_tensor(tS.ap[:, hh * 512:(hh + 1) * 512], bank(b0 + hh),
                                                                                    abt.ap[:, (kvg * 4 + hh * 2) * 256:(kvg * 4 + hh * 2 + 2) * 256], ALU.add),
                             reads=[PB[b0 + hh], abt.b], writes=[tS.b])
                    if ATT < 2: continue
                    S.op("dve", lambda e, kvg=kvg: e.tensor_reduce(mx[:, kvg * 4:kvg * 4 + 4], tS.v("p (h k) -> p h k", h=4), AX.X, ALU.max),
                         reads=[tS.b], writes=[sm.b])
                    S.op("dve", lambda e, kvg=kvg: e.tensor_tensor(mx[:, kvg * 4:kvg * 4 + 4], mx[:, kvg * 4:kvg * 4 + 4], sk8.ap[:, kvg * 4:kvg * 4 + 4], ALU.max),
                         reads=[sm.b, sk8.b], writes=[sm.b])
                    S.op("dve", lambda e, kvg=kvg: e.tensor_scalar(ngm[:, kvg * 4:kvg * 4 + 4], mx[:, kvg * 4:kvg * 4 + 4], -0.125, None, ALU.mult),
                         reads=[sm.b], writes=[sm.b])
                    if ATT < 3: continue
                    for j in range(4):
                        h = kvg * 4 + j
                        S.op("act", lambda e, j=j, h=h: e.activation(Pt.ap[:, j * 256:(j + 1) * 256], tS.ap[:, j * 256:(j + 1) * 256], AF.Exp,
                                                                    bias=ngm[:, h:h + 1], scale=0.125, accum_out=rs[:, h:h + 1]),
                             reads=[tS.b, sm.b], writes=[Pt.b, sm.b])
                    if ATT < 4: continue
                    pv = bank16(1).rearrange("p (k m) -> p k m", m=128)
                    S.op("pe", [tp(pv[:, k, :], Pt.ap[:, k * 128:(k + 1) * 128], identb) for k in range(8)], reads=[Pt.b, cB.b], writes=[PB[1]])
                    S.op("act", lambda e: e.activation(PT.ap, bank16(1), AF.Copy), reads=[PB[1]], writes=[PT.b])
                    if ATT < 5: continue
                    PT3 = PT.v("p (k m) -> p k m", k=8)
                    mms = []
                    for j in range(4):
                        h = kvg * 4 + j
                        for half in range(2):
                            mms.append(mm(bank(2 + h // 8, 64, (h % 8) * 64), PT3[:, j * 2 + half, :], V3[:, b + half, kvg * 64:(kvg + 1) * 64], half == 0, half == 1))
                    S.op("pe", mms, reads=[PT.b, Vt.b], writes=[PB[2], PB[3]])
                if ATT < 6: continue
                dn = small(16)
                S.op("dve", lambda e: e.scalar_tensor_tensor(dn, sk8.ap, 0.125, ngm, ALU.mult, ALU.add), reads=[sk8.b, sm.b], writes=[sm.b])
                S.op("act", lambda e: e.activation(dn, dn, AF.Exp), reads=[sm.b], writes=[sm.b])
                S.op("dve", lambda e: e.tensor_tensor(dn, dn, rs, ALU.add), reads=[sm.b], writes=[sm.b])
                S.op("dve", lambda e: e.reciprocal(dn, dn), reads=[sm.b], writes=[sm.b])
                for g in range(2):
                    S.op("dve", lambda e, g=g: e.tensor_tensor(at_.ap[:, g * 512:(g + 1) * 512].rearrange("p (h d) -> p h d", h=8),
                                                               bank(2 + g).rearrange("p (h d) -> p h d", h=8),
                                                               dn[:, g * 8:(g + 1) * 8].unsqueeze(2).to_broadcast([128, 8, 64]), ALU.mult),
                         reads=[PB[2 + g], sm.b], writes=[at_.b])
                rstd = rms_rstd(at_.ap, 1024, [at_.b])
                S.op("dve", lambda e, rstd=rstd: e.scalar_tensor_tensor(ao.ap, at_.ap, rstd, ga.ap, ALU.mult, ALU.mult), reads=[at_.b, sm.b, ga.b], writes=[ao.b])
                tk = og * 512 + c0
                S.dma("sp", [(cat_d[tk:tk + 128, 0:1024], ao.ap)], reads=[ao.b], writes=[CATD])
            for eo in range(2):
                S.op("act", lambda e, eo=eo: e.activation(kT4[:, eo, :, 0:128], kT4[:, eo, :, 512:640], AF.Copy), reads=[kT.b], writes=[kT.b])
            S.op("act", lambda e: e.activation(V3[:, 0, :], V3[:, 4, :], AF.Copy), reads=[Vt.b], writes=[Vt.b])
        S.barrier()
        A.release(m)

    if stop_after != "B1b":
        phase_B1a()
    if do_samples and stop_after is None:
        phase_SM1()
    A.release(m_gm)

    def phase_B2():
        m = A.mark()
        wo = A.alloc("wo", 16 * 2048, BF16); wo3 = wo.v("p (k n) -> p k n", k=16)
        w_out_v = w_out.rearrange("(kc p) n -> p kc n", p=128)
        for pcs in range(4):
            S.dma("pool", [(wo3[:, :, pcs * 512:(pcs + 1) * 512], w_out_v[:, :, pcs * 512:(pcs + 1) * 512])], writes=[wo.b])
        ggt1 = A.alloc("ggt1", D); gm2 = A.alloc("gm2", D); sh2 = A.alloc("sh2", D)
        load_mod_bc(ggt1, 2); load_mod_bc(gm2, 4); load_mod_bc(sh2, 3)
        cb_ = [A.alloc("catb%d" % i, D, BF16) for i in range(2)]
        cT = A.alloc("catT", 16 * 128, BF16); cT3 = reg3(cT, 16)
        u2 = A.alloc("u2T", 16 * 128, BF16); u23 = reg3(u2, 16)
        def b2_body(cT3v, cTbuf, np_, xt, u2dst3, x1_store, u2_store):
            ss4 = small(4)
            for q in range(4):
                o = ps[0:np_, (4 + q) * 512:(5 + q) * 512]
                S.op("pe", [mm(o, cT3v[:, kc, 0:np_], wo3[:, kc, q * 512:(q + 1) * 512], kc == 0, kc == 15) for kc in range(16)],
                     reads=[cTbuf, wo.b], writes=[PB[4 + q]])
                S.op("act", lambda e, q=q, o=o: e.activation(junk.ap[0:np_, 0:512], o, AF.Square, scale=float(D) ** -0.5, accum_out=ss4[0:np_, q:q + 1]),
                     reads=[PB[4 + q]], writes=[junk.b, sm.b])
            rstd = small(1)
            S.op("dve", lambda e: e.tensor_reduce(rstd[0:np_], ss4[0:np_], AX.X, ALU.add), reads=[sm.b], writes=[sm.b])
            S.op("dve", lambda e: e.tensor_scalar(rstd[0:np_], rstd[0:np_], EPS, None, ALU.add), reads=[sm.b], writes=[sm.b])
            S.op("act", lambda e: e.activation(rstd[0:np_], rstd[0:np_], AF.Ln), reads=[sm.b], writes=[sm.b])
            S.op("act", lambda e: e.activation(rstd[0:np_], rstd[0:np_], AF.Exp, scale=-0.5), reads=[sm.b], writes=[sm.b])
            for q in range(4):
                o = ps[0:np_, (4 + q) * 512:(5 + q) * 512]
                S.op("dve", lambda e, q=q, o=o: e.scalar_tensor_tensor(tmp32.ap[0:np_, q * 512:(q + 1) * 512], o, rstd[0:np_], ggt1.ap[0:np_, q * 512:(q + 1) * 512], ALU.mult, ALU.mult),
                     reads=[PB[4 + q], sm.b, ggt1.b], writes=[tmp32.b])
            S.op("dve", lambda e: e.tensor_tensor(xt.ap[0:np_], xt.ap[0:np_], tmp32.ap[0:np_], ALU.add), reads=[xt.b, tmp32.b], writes=[xt.b])
            x1_store(xt)
            norm_mod_T(xt, gm2, sh2, u2dst3, 0, np_=np_)
            u2_store()

        for blk in range(16):
            smc[0] = 0
            cbt = cb_[blk % 2]
            S.dma("sp", [(cbt.ap, cat_d[blk * 128:(blk + 1) * 128, :])], reads=[CATD], writes=[cbt.b])
            xt = xb[blk % 2]
            S.dma("sp", [(xt.ap, xext[OWN0 + blk * 128:OWN0 + (blk + 1) * 128, :])], writes=[xt.b])
            transpose_to(cbt, cT3, 0)
            b2_body(cT3, cT.b, 128, xt,  u23,
                    lambda xt, blk=blk: S.dma("sp", [(x1_d[blk * 128:(blk + 1) * 128, :], xt.ap)], reads=[xt.b], writes=[X1D]),
                    lambda blk=blk: S.dma("sp", [(u2T_d.rearrange("k p t -> p k t")[:, :, blk * 128:(blk + 1) * 128], u23)], reads=[u2.b], writes=[U2TD]))
        if do_samples:
            smc[0] = 0
            load_mod_rows(ggt1, 2); load_mod_rows(gm2, 4); load_mod_rows(sh2, 3)
            S.dma("sp", [(x1s.ap[0:16, :], xs_d)], writes=[x1s.b])
            b2_body(catsT3, catsT.b, 16, x1s, u2sT3, lambda xt: None, lambda: None)
        S.barrier()
        A.release(m)

    if stop_after not in ("B1b", "B1a", "B1a_proj"):
        phase_B2()

    def phase_B3():
        m = A.mark()
        uT = A.alloc("u2Tall", 16 * NOWN, BF16); uT3 = uT.v("p (k n) -> p k n", k=16)
        S.dma("sp", [(uT3[:, kq * 4:(kq + 1) * 4, :], u2T_d.rearrange("k p t -> p k t")[:, kq * 4:(kq + 1) * 4, :]) for kq in range(4)], reads=[U2TD], writes=[uT.b])
        gsl = [A.alloc("wg%d" % i, 16 * 256, BF16) for i in range(2)]
        usl = [A.alloc("wu%d" % i, 16 * 256, BF16) for i in range(2)]
        hst_ = [A.alloc("hstg%d" % i, NOWN, BF16) for i in range(2)]
        et = [A.alloc("et%d" % i, 512) for i in range(2)]
        hsk = A.alloc("hs_tok", DFF, BF16)
        wg_v = w_gate.rearrange("(kc p) n -> p kc n", p=128)
        wu_v = w_up.rearrange("(kc p) n -> p kc n", p=128)
        for pj in range(NJ // 2):
            g3 = gsl[pj % 2].v("p (k n) -> p k n", k=16)
            u3 = usl[pj % 2].v("p (k n) -> p k n", k=16)
            S.dma("pool", [(g3, wg_v[:, :, pj * 256:(pj + 1) * 256])], writes=[gsl[pj % 2].b])
            S.dma("pool", [(u3, wu_v[:, :, pj * 256:(pj + 1) * 256])], writes=[usl[pj % 2].b])
            for jj in range(2):
                j = pj * 2 + jj
                hs = hst_[j % 2]
                for tg in range(4):
                    bg, bu = 4 + (tg % 2) * 2, 5 + (tg % 2) * 2
                    S.op("pe", [mm(bank(bg), g3[:, kc, jj * 128:(jj + 1) * 128], uT3[:, kc, tg * 512:(tg + 1) * 512], kc == 0, kc == 15) for kc in range(16)],
                         reads=[gsl[pj % 2].b, uT.b], writes=[PB[bg]])
                    S.op("pe", [mm(bank(bu), u3[:, kc, jj * 128:(jj + 1) * 128], uT3[:, kc, tg * 512:(tg + 1) * 512], kc == 0, kc == 15) for kc in range(16)],
                         reads=[usl[pj % 2].b, uT.b], writes=[PB[bu]])
                    e_ = et[tg % 2]
                    S.op("act", lambda e, e_=e_, bg=bg: e.activation(e_.ap, bank(bg), AF.Exp, scale=-1.0), reads=[PB[bg]], writes=[e_.b])
                    S.op("dve", lambda e, e_=e_: e.tensor_scalar(e_.ap, e_.ap, 1.0, None, ALU.add), reads=[e_.b], writes=[e_.b])
                    S.op("dve", lambda e, e_=e_: e.reciprocal(e_.ap, e_.ap), reads=[e_.b], writes=[e_.b])
                    S.op("dve", lambda e, e_=e_, bg=bg: e.tensor_tensor(e_.ap, e_.ap, bank(bg), ALU.mult), reads=[e_.b, PB[bg]], writes=[e_.b])
                    S.op("dve", lambda e, e_=e_, bu=bu, hs=hs, tg=tg: e.tensor_tensor(hs.ap[:, tg * 512:(tg + 1) * 512], e_.ap, bank(bu), ALU.mult),
                         reads=[e_.b, PB[bu]], writes=[hs.b])
                S.dma("sp", [(hT_d[j], hs.ap)], reads=[hs.b], writes=[HTD])
            if do_samples:
                og_ = ps[0:16, 2 * 512:2 * 512 + 256]
                ou_ = ps[0:16, 3 * 512:3 * 512 + 256]
                S.op("pe", [mm(og_, u2sT3[:, kc, 0:16], g3[:, kc, :], kc == 0, kc == 15) for kc in range(16)], reads=[u2sT.b, gsl[pj % 2].b], writes=[PB[2]])
                S.op("pe", [mm(ou_, u2sT3[:, kc, 0:16], u3[:, kc, :], kc == 0, kc == 15) for kc in range(16)], reads=[u2sT.b, usl[pj % 2].b], writes=[PB[3]])
                e_ = et[0]
                S.op("act", lambda e, e_=e_: e.activation(e_.ap[0:16, 0:256], og_, AF.Exp, scale=-1.0), reads=[PB[2]], writes=[e_.b])
                S.op("dve", lambda e, e_=e_: e.tensor_scalar(e_.ap[0:16, 0:256], e_.ap[0:16, 0:256], 1.0, None, ALU.add), reads=[e_.b], writes=[e_.b])
                S.op("dve", lambda e, e_=e_: e.reciprocal(e_.ap[0:16, 0:256], e_.ap[0:16, 0:256]), reads=[e_.b], writes=[e_.b])
                S.op("dve", lambda e, e_=e_: e.tensor_tensor(e_.ap[0:16, 0:256], e_.ap[0:16, 0:256], og_, ALU.mult), reads=[e_.b, PB[2]], writes=[e_.b])
                S.op("dve", lambda e, e_=e_, pj=pj: e.tensor_tensor(hsk.ap[0:16, pj * 256:(pj + 1) * 256], e_.ap[0:16, 0:256], ou_, ALU.mult), reads=[e_.b, PB[3]], writes=[hsk.b])
        if do_samples:
            for j0 in range(0, NJ, 8):
                nk = min(8, NJ - j0)
                pv = bank16(1).rearrange("p (k m) -> p k m", m=128)
                S.op("pe", [tp(pv[:, k, 0:16], hsk.ap[0:16, (j0 + k) * 128:(j0 + k + 1) * 128], identb[0:16, 0:16]) for k in range(nk)],
                     reads=[hsk.b, cB.b], writes=[PB[1]])
                S.op("act", lambda e, j0=j0, nk=nk, pv=pv: e.activation(hsT3[:, j0:j0 + nk, :], pv[:, 0:nk, 0:16], AF.Copy), reads=[PB[1]], writes=[hsT.b])
        S.barrier()
        A.release(m)

    def phase_B4():
        m = A.mark()
        ggt2 = A.alloc("ggt2", D)
        load_mod_bc(ggt2, 5)
        hT = A.alloc("hTg", NJ * 512, BF16); hT3 = hT.v("p (j t) -> p j t", j=NJ)
        wsl = [A.alloc("wd%d" % i, NJ * 256, BF16) for i in range(2)]
        ft = A.alloc("ft", 4 * D); f3 = ft.v("p (b n) -> p b n", b=4)
        wd_v = w_down.rearrange("(j p) n -> p j n", p=128)
        hT_v = hT_d.rearrange("j p t -> p j t")
        for tg in range(4):
            S.dma("sp", [(hT3[:, jq * 11:(jq + 1) * 11, :], hT_v[:, jq * 11:(jq + 1) * 11, tg * 512:(tg + 1) * 512]) for jq in range(4)], reads=[HTD], writes=[hT.b])
            for pc in range(8):
                w3 = wsl[pc % 2].v("p (j n) -> p j n", j=NJ)
                S.dma("pool", [(w3[:, jq * 11:(jq + 1) * 11, :], wd_v[:, jq * 11:(jq + 1) * 11, pc * 256:(pc + 1) * 256]) for jq in range(4)], writes=[wsl[pc % 2].b])
                for tb in range(4):
                    bi = 4 + tb
                    S.op("pe", [mm(bank(bi, 256), hT3[:, j, tb * 128:(tb + 1) * 128], w3[:, j, :], j == 0, j == NJ - 1) for j in range(NJ)],
                         reads=[hT.b, wsl[pc % 2].b], writes=[PB[bi]])
                    S.op("act", lambda e, tb=tb, bi=bi, pc=pc: e.activation(f3[:, tb, pc * 256:(pc + 1) * 256], bank(bi, 256), AF.Copy), reads=[PB[bi]], writes=[ft.b])
            for tb in range(4):
                smc[0] = 0
                blk = tg * 4 + tb
                xt = xb[tb % 2]
                S.dma("sp", [(xt.ap, x1_d[blk * 128:(blk + 1) * 128, :])], reads=[X1D], writes=[xt.b])
                rstd = rms_rstd(f3[:, tb, :], D, [ft.b])
                S.op("dve", lambda e, tb=tb, rstd=rstd: e.scalar_tensor_tensor(tmp32.ap, f3[:, tb, :], rstd, ggt2.ap, ALU.mult, ALU.mult), reads=[ft.b, sm.b, ggt2.b], writes=[tmp32.b])
                S.op("dve", lambda e, xt=xt: e.tensor_tensor(xt.ap, xt.ap, tmp32.ap, ALU.add), reads=[xt.b, tmp32.b], writes=[xt.b])
                S.dma("sp", [(y_out[blk * 128:(blk + 1) * 128, :], xt.ap)], reads=[xt.b])
        if do_samples:
            smc[0] = 0
            load_mod_rows(ggt2, 5)
            for pc in range(8):
                w3 = wsl[pc % 2].v("p (j n) -> p j n", j=NJ)
                S.dma("pool", [(w3[:, jq * 11:(jq + 1) * 11, :], wd_v[:, jq * 11:(jq + 1) * 11, pc * 256:(pc + 1) * 256]) for jq in range(4)], writes=[wsl[pc % 2].b])
                o = ps[0:16, (4 + pc % 4) * 512:(4 + pc % 4) * 512 + 256]
                S.op("pe", [mm(o, hsT3[:, j, :], w3[:, j, :], j == 0, j == NJ - 1) for j in range(NJ)], reads=[hsT.b, wsl[pc % 2].b], writes=[PB[4 + pc % 4]])
                S.op("act", lambda e, o=o, pc=pc: e.activation(f3[0:16, 0, pc * 256:(pc + 1) * 256], o, AF.Copy), reads=[PB[4 + pc % 4]], writes=[ft.b])
            rstd = rms_rstd(f3[0:16, 0, :], D, [ft.b])
            S.op("dve", lambda e: e.scalar_tensor_tensor(tmp32.ap[0:16], f3[0:16, 0, :], rstd[0:16], ggt2.ap[0:16], ALU.mult, ALU.mult), reads=[ft.b, sm.b, ggt2.b], writes=[tmp32.b])
            S.op("dve", lambda e: e.tensor_tensor(x1s.ap[0:16], x1s.ap[0:16], tmp32.ap[0:16], ALU.add), reads=[x1s.b, tmp32.b], writes=[x1s.b])
            S.dma("sp", [(ys_out, x1s.ap[0:16, :])], reads=[x1s.b])
        S.barrier()
        A.release(m)

    if stop_after is None:
        phase_B3()
        phase_B4()
    S.final_wait("sp")
    S.emit()
    st.close()
    print("[build] arena peak words", A.peak, "instr", {e: S.count[e] for e in S.count})
    return nc


def _consts():
    i = np.arange(128)
    ident = np.eye(128, dtype=np.float32)
    U = (i[:, None] <= i[None, :]).astype(np.float32)
    UT = (i[:, None] > i[None, :]).astype(np.float32)
    ones = np.ones((128, 128), np.float32)
    caus = (i[None, :] >= i[:, None]).astype(np.float32)
    cf32 = np.concatenate([ident, U, UT, ones, caus], axis=1)
    cb16 = np.concatenate([ident, ones], axis=1).astype(ml_dtypes.bfloat16)
    return cf32, cb16


def _abias(first_core):
    slopes = (2.0 ** (-8.0 * np.arange(1, 17) / 16)).astype(np.float32)
    a = np.arange(128)[:, None]
    j = np.arange(256)[None, :]
    dist = a + 128 - j
    valid = (dist >= 0) & (dist < 128)
    out = []
    for first in (True, False):
        v = valid & ((j >= 128) if (first and first_core) else True)
        b = np.where(v[None], -slopes[:, None, None] * dist[None].astype(np.float32), -30000.0) * 8.0
        out.append(np.ascontiguousarray(np.transpose(b, (1, 0, 2)).reshape(128, 16 * 256)).astype(np.float32))
    return out


def make_in_maps(inp):
    cf32, cb16 = _consts()
    xp = np.asarray(inp["x_prompt"])[0]
    maps = []
    wada = np.concatenate([np.asarray(inp["w_ada"])[0], np.asarray(inp["b_ada"])[0][None, :]], axis=0)
    gvec = np.stack([np.asarray(inp[k])[0] for k in ("g_pre_mix", "g_post_mix", "g_pre_ffn", "g_post_ffn")], 0)
    cwv = np.asarray(inp["conv_w"])[0]
    convw = np.ascontiguousarray(cwv.reshape(4, 12, 128).transpose(2, 1, 0).reshape(128, 48))
    convb = np.ascontiguousarray(np.asarray(inp["conv_b"])[0].reshape(12, 128).T)
    hvec = np.concatenate([np.asarray(inp[k])[0] for k in ("dt_bias", "a_log", "d_skip", "attn_sinks")])[None, :]
    slopes = (2.0 ** (-8.0 * np.arange(1, 17) / 16)).astype(np.float32)
    sbias = (-slopes[:, None] * (127 - np.arange(128))[None, :].astype(np.float32) * 8.0).astype(np.float32)
    selkv = (np.arange(16)[:, None] // 4 == np.arange(4)[None, :]).astype(np.float32)
    for c in range(8):
        nreal = 2048 * (c + 1)
        xext = np.zeros((NEXT, D), np.float32)
        xext[NEXT - nreal:] = xp[:nreal]
        m = np.zeros((NEXT,), np.float32)
        m[NEXT - nreal:] = 1.0
        ab0, ab1 = _abias(c == 0)
        maps.append({
            "xext": xext, "mrow": m[None, :], "mtok": np.ascontiguousarray(m.reshape(NEXT // 128, 128).T),
            "cmat": np.concatenate([np.asarray(inp["c_sample"])[16 * c:16 * c + 16], np.asarray(inp["c_prompt"])], 0),
            "wada": wada, "gvec": gvec, "w_in": np.asarray(inp["w_in"])[0], "w_out": np.asarray(inp["w_out"])[0],
            "w_gate": np.asarray(inp["w_gate"])[0], "w_up": np.asarray(inp["w_up"])[0], "w_down": np.asarray(inp["w_down"])[0],
            "convw": convw, "convb": convb, "hvec": hvec.astype(np.float32),
            "gatt": np.asarray(inp["g_attn_out"]), "gssm": np.asarray(inp["g_ssm_out"]),
            "cf32": cf32, "cb16": cb16, "abias0": ab0, "abias1": ab1,
            "xs_d": np.ascontiguousarray(np.asarray(inp["x_sample"])[16 * c:16 * c + 16, 0, :]),
            "ck_d": np.ascontiguousarray(np.asarray(inp["cache_k"])[0, 16 * c:16 * c + 16].reshape(16, 128, 256)),
            "cv_d": np.ascontiguousarray(np.asarray(inp["cache_v"])[0, 16 * c:16 * c + 16].reshape(16, 128, 256)),
            "sconv_d": np.ascontiguousarray(np.asarray(inp["state_conv"])[0, 16 * c:16 * c + 16].reshape(16, 4608)),
            "sssm_d": np.ascontiguousarray(np.asarray(inp["state_ssm"])[0, 16 * c:16 * c + 16].reshape(16, 1024, 128)),
            "convw_raw": np.ascontiguousarray(cwv.reshape(1, 6144)), "convb_raw": np.asarray(inp["conv_b"]).reshape(1, 1536),
            "sinkcol": np.asarray(inp["attn_sinks"]).reshape(16, 1), "sbias": sbias, "selkv": selkv,
            "dskrow": np.repeat(np.asarray(inp["d_skip"])[0], 64)[None, :].astype(np.float32),
        })
    return maps


_NC_CACHE = {}


def kernel(**inputs):
    if "nc" not in _NC_CACHE:
        _NC_CACHE["nc"] = build()
    nc = _NC_CACHE["nc"]
    maps = make_in_maps(inputs)
    res = run_bass_kernel_spmd(nc, maps, core_ids=list(range(8)))
    r = res.results
    f = np.float32
    y_prompt = np.concatenate([np.asarray(r[c]["y_out"]) for c in range(8)], 0)[None].astype(f)
    k_prompt = np.asarray(r[7]["k_out"]).reshape(1, 1, 128, 4, 64).astype(f)
    v_prompt = np.asarray(r[7]["v_out"]).reshape(1, 1, 128, 4, 64).astype(f)
    conv_prompt = np.asarray(r[7]["conv_out"]).reshape(1, 1, 3, 1536).astype(f)
    ssm_prompt = np.asarray(r[7]["ssm_out"]).reshape(1, 1, 16, 64, 128).astype(f)
    cat = lambda name: np.concatenate([np.asarray(r[c][name]) for c in range(8)], 0).astype(f)
    y_sample = cat("ys_out").reshape(128, 1, 2048)
    k_sample = cat("ks_out").reshape(1, 128, 128, 4, 64)
    v_sample = cat("vs_out").reshape(1, 128, 128, 4, 64)
    conv_sample = cat("convs_out").reshape(1, 128, 3, 1536)
    ssm_sample = cat("ssms_out").reshape(1, 128, 16, 64, 128)
    return (y_prompt, y_sample, k_prompt, v_prompt, conv_prompt, ssm_prompt,
            k_sample, v_sample, conv_sample, ssm_sample)
```

```python
import contextlib
import numpy as np
import ml_dtypes
import concourse.bass as bass
import concourse.mybir as mybir
from concourse.bass_utils import run_bass_kernel_spmd

F32 = mybir.dt.float32
BF16 = mybir.dt.bfloat16
ALU = mybir.AluOpType
AF = mybir.ActivationFunctionType
AX = mybir.AxisListType

ENGS = ("pe", "act", "dve", "pool", "sp")


class Buf:
    __slots__ = ("name", "last_w", "readers")

    def __init__(self, name):
        self.name = name
        self.last_w = None
        self.readers = []


class Sched:
    def __init__(self, nc, n_dma_sp=40, n_dma_pool=12, self_wait=True):
        self.nc = nc
        self.q = {e: [] for e in ENGS}
        self.count = {e: 0 for e in ENGS}
        self.seen = {e: {} for e in ENGS}
        self.self_wait = self_wait
        self.ndma = {"sp": n_dma_sp, "pool": n_dma_pool, "act": 8}
        self.dma_next = {"sp": 0, "pool": 0, "act": 0}
        self.dma_cnt = {}
        self.sems = {}

    def _deps(self, eng, reads, writes):
        deps = {}

        def add(tok):
            if tok is None:
                return
            k, v = tok
            if deps.get(k, 0) < v:
                deps[k] = v
        for b in reads:
            add(b.last_w)
        for b in writes:
            add(b.last_w)
            for r in b.readers:
                add(r)
        out = []
        for k, v in deps.items():
            if k == eng and (eng == "pe" or not self.self_wait):
                continue
            if self.seen[eng].get(k, 0) >= v:
                continue
            self.seen[eng][k] = v
            out.append((k, v))
        return out

    def _commit(self, tok, reads, writes):
        for b in writes:
            b.last_w = tok
            b.readers = []
        for b in reads:
            if b not in writes:
                b.readers.append(tok)
                if len(b.readers) > 64:
                    best = {}
                    for k, v in b.readers:
                        if best.get(k, 0) < v:
                            best[k] = v
                    b.readers = list(best.items())

    def op(self, eng, fns, reads=(), writes=()):
        if callable(fns):
            fns = [fns]
        waits = self._deps(eng, reads, writes)
        self.count[eng] += 1
        tok = (eng, self.count[eng])
        self.q[eng].append(("op", waits, fns, tok))
        self._commit(tok, reads, writes)
        return tok

    def dma(self, eng, pairs, reads=(), writes=(), **kw):
        i = self.dma_next[eng]
        self.dma_next[eng] = (i + 1) % self.ndma[eng]
        key = "d_%s_%d" % (eng, i)
        prev = self.dma_cnt.get(key, 0)
        waits = self._deps(eng, reads, writes)
        if prev and self.seen[eng].get(key, 0) < prev:
            self.seen[eng][key] = prev
            waits.append((key, prev))
        val = prev + 16 * len(pairs)
        self.dma_cnt[key] = val
        tok = (key, val)
        self.q[eng].append(("dma", waits, pairs, tok, kw))
        self._commit(tok, reads, writes)
        return tok

    def barrier(self):
        targets = [(e, self.count[e]) for e in ENGS if self.count[e] > 0]
        targets += [(k, v) for k, v in self.dma_cnt.items()]
        for e in ENGS:
            waits = []
            for k, v in targets:
                if k == e:
                    continue
                if self.seen[e].get(k, 0) >= v:
                    continue
                self.seen[e][k] = v
                waits.append((k, v))
            if waits:
                self.q[e].append(("wait", waits))

    def final_wait(self, eng="sp"):
        waits = [(k, v) for k, v in self.dma_cnt.items()]
        waits += [(e, self.count[e]) for e in ENGS if self.count[e] > 0 and e != eng]
        self.q[eng].append(("wait", waits))

    def emit(self):
        nc = self.nc
        keys = [e for e in ENGS if self.count[e] > 0] + sorted(self.dma_cnt.keys())
        with contextlib.ExitStack() as st:
            for k in keys:
                self.sems[k] = st.enter_context(nc.semaphore("s_" + k))
            block = st.enter_context(nc.Block())
            sems = self.sems

            def run(engobj, items):
                for it in items:
                    for (k, v) in it[1]:
                        engobj.wait_ge(sems[k], v)
                    if it[0] == "op":
                        _, _, fns, tok = it
                        ins = None
                        for f in fns:
                            ins = f(engobj)
                        ins.then_inc(sems[tok[0]], 1)
                    elif it[0] == "dma":
                        _, _, pairs, tok, kw = it
                        for (o, i) in pairs:
                            engobj.dma_start(out=o, in_=i, **kw).then_inc(sems[tok[0]], 16)

            if self.q["pe"]:
                @block.tensor
                def _(e):
                    run(e, self.q["pe"])
            if self.q["act"]:
                @block.scalar
                def _(e):
                    run(e, self.q["act"])
            if self.q["dve"]:
                @block.vector
                def _(e):
                    run(e, self.q["dve"])
            if self.q["pool"]:
                @block.gpsimd
                def _(e):
                    run(e, self.q["pool"])
            if self.q["sp"]:
                @block.sync
                def _(e):
                    run(e, self.q["sp"])


class Tile:
    def __init__(self, ap, name):
        self.ap = ap
        self.b = Buf(name)

    def v(self, pat, **kw):
        return self.ap.rearrange(pat, **kw)


class Arena:
    def __init__(self, base, nwords):
        self.base = base
        self.n = nwords
        self.off = 0
        self.peak = 0

    def alloc(self, name, nelem, dt=F32):
        words = nelem if dt == F32 else (nelem + 1) // 2
        wal = (words + 7) // 8 * 8
        assert self.off + wal <= self.n, ("SBUF arena overflow", name, self.off, wal, self.n)
        ap = self.base[:, self.off:self.off + words]
        if dt != F32:
            ap = ap.bitcast(dt)
        self.off += wal
        self.peak = max(self.peak, self.off)
        return Tile(ap, name)

    def mark(self):
        return self.off

    def release(self, m):
        self.off = m


def mm(out, lhsT, rhs, start, stop):
    return lambda e: e.matmul(out, lhsT, rhs, start=start, stop=stop)


def tp(out, in_, ident):
    return lambda e: e.transpose(out, in_, ident)
D = 2048
KC = 16
NEXT = 16384
NOWN = 2048
EPS = 1e-6
NPRE_G = (NEXT - NOWN) // 512
INW = 4112
DFF = 5632
NJ = DFF // 128


def build(n_pre_groups=NPRE_G, stop_after=None, do_samples=True):
    nc = bass.Bass("TRN2", target_bir_lowering=False)

    def din(name, shape, dt=F32):
        return nc.dram_tensor(name, list(shape), dt, kind="ExternalInput").ap()

    def dout(name, shape, dt=F32):
        return nc.dram_tensor(name, list(shape), dt, kind="ExternalOutput").ap()

    def dscr(name, shape, dt=F32):
        return nc.dram_tensor(name, list(shape), dt, kind="Internal").ap()

    xext = din("xext", [NEXT, D])
    mrow = din("mrow", [1, NEXT])
    mtok = din("mtok", [128, NEXT // 128])
    cmat = din("cmat", [17, D])
    wada = din("wada", [D + 1, 6 * D])
    gvec = din("gvec", [4, D])
    w_in = din("w_in", [D, INW])
    w_out = din("w_out", [D, D])
    w_gate = din("w_gate", [D, DFF])
    w_up = din("w_up", [D, DFF])
    w_down = din("w_down", [DFF, D])
    convw = din("convw", [128, 48])
    convb = din("convb", [128, 12])
    hvec = din("hvec", [1, 64])
    gatt = din("gatt", [1, 1024])
    gssm = din("gssm", [1, 1024])
    cf32 = din("cf32", [128, 5 * 128])
    cb16 = din("cb16", [128, 2 * 128], BF16)
    abias0 = din("abias0", [128, 16 * 256])
    abias1 = din("abias1", [128, 16 * 256])
    y_out = dout("y_out", [NOWN, D])
    k_out = dout("k_out", [128, 256])
    v_out = dout("v_out", [128, 256])
    conv_out = dout("conv_out", [3, 1536])
    ssm_out = dout("ssm_out", [1024, 128])
    mod_d = dscr("mod_d", [17, 6 * D])
    cat_d = dscr("cat_d", [NOWN, D], BF16)
    x1_d = dscr("x1_d", [NOWN, D])
    hT_d = dscr("hT_d", [NJ, 128, NOWN], BF16)

    st = contextlib.ExitStack()
    NW = 47616
    big = st.enter_context(nc.sbuf_tensor("big", [128, NW], F32))
    ps = st.enter_context(nc.psum_tensor("ps", [128, 4096], F32))
    A = Arena(big, NW)
    S = Sched(nc)
    PB = [Buf("psb%d" % i) for i in range(8)]

    def bank(i, n=512, off=0):
        return ps[:, i * 512 + off:i * 512 + off + n]

    def bank16(i):
        return ps[:, i * 512:(i + 1) * 512].bitcast(BF16)

    MODD = Buf("mod_d")
    w_in_v = w_in.rearrange("(kc p) n -> p kc n", p=128)

    cF = A.alloc("cF", 5 * 128)
    cB = A.alloc("cB", 2 * 128, BF16)
    S.dma("sp", [(cF.ap, cf32)], writes=[cF.b])
    S.dma("sp", [(cB.ap, cb16)], writes=[cB.b])
    identf = cF.ap[:, 0:128]
    Umat = cF.ap[:, 128:256]
    UTmat = cF.ap[:, 256:384]
    onesf = cF.ap[:, 384:512]
    caus01 = cF.ap[:, 512:640]
    identb = cB.ap[:, 0:128]
    onesb = cB.ap[:, 128:256]
    hv = A.alloc("hv", 64)
    S.dma("sp", [(hv.ap, hvec.to_broadcast([128, 64]))], writes=[hv.b])
    dtb_bc = hv.ap[:, 0:16]
    dsk_bc = hv.ap[:, 32:48]
    sink_bc = hv.ap[:, 48:64]
    a_bc = A.alloc("a_bc", 16)
    S.op("act", lambda e: e.activation(a_bc.ap, hv.ap[:, 16:32], AF.Exp), reads=[hv.b], writes=[a_bc.b])
    S.op("dve", lambda e: e.tensor_scalar(a_bc.ap, a_bc.ap, -1.0, None, ALU.mult), reads=[a_bc.b], writes=[a_bc.b])
    cw = A.alloc("cw", 48)
    cbi = A.alloc("cbi", 12)
    S.dma("sp", [(cw.ap, convw)], writes=[cw.b])
    S.dma("sp", [(cbi.ap, convb)], writes=[cbi.b])
    mt = A.alloc("mt", NEXT // 128)
    S.dma("sp", [(mt.ap, mtok)], writes=[mt.b])

    def phase0():
        m0 = A.mark()
        cm = A.alloc("cm", D)
        S.dma("sp", [(cm.ap[0:17, :], cmat)], writes=[cm.b])
        ee = A.alloc("ee", D)
        S.op("act", lambda e: e.activation(ee.ap[0:17], cm.ap[0:17], AF.Exp, scale=-1.0), reads=[cm.b], writes=[ee.b])
        S.op("dve", lambda e: e.tensor_scalar(ee.ap[0:17], ee.ap[0:17], 1.0, None, ALU.add), reads=[ee.b], writes=[ee.b])
        S.op("dve", lambda e: e.reciprocal(ee.ap[0:17], ee.ap[0:17]), reads=[ee.b], writes=[ee.b])
        sc = A.alloc("sc", D, BF16)
        S.op("dve", lambda e: e.tensor_tensor(sc.ap[0:17], cm.ap[0:17], ee.ap[0:17], ALU.mult), reads=[cm.b, ee.b], writes=[sc.b])
        cT = A.alloc("cT", 16 * 32, BF16)
        cT3 = cT.v("p (k m) -> p k m", k=16)
        pb = bank16(0).rearrange("p (k m) -> p k m", m=32)
        S.op("pe", [tp(pb[:, kc, 0:17], sc.ap[0:17, kc * 128:(kc + 1) * 128], identb[0:17, 0:17]) for kc in range(16)],
             reads=[sc.b, cB.b], writes=[PB[0]])
        S.op("act", lambda e: e.activation(cT3[:, :, 0:17], pb[:, 0:16, 0:17], AF.Copy), reads=[PB[0]], writes=[cT.b])
        gv = A.alloc("gv", 4 * D)
        S.dma("sp", [(gv.ap[0:17, g * D:(g + 1) * D], gvec[g:g + 1, :].to_broadcast([17, D])) for g in range(4)], writes=[gv.b])
        mod = A.alloc("mod", 6 * D)
        slots = [A.alloc("wa%d" % i, 17 * 512, BF16) for i in range(2)]
        wada_v = wada[0:D, :].rearrange("(kc p) n -> p kc n", p=128)
        for pc in range(24):
            sl = slots[pc % 2]
            s3 = sl.v("p (k n) -> p k n", k=17)
            S.dma("pool", [(s3[:, 0:16, :], wada_v[:, :, pc * 512:(pc + 1) * 512]),
                           (s3[0:1, 16, :], wada[D:D + 1, pc * 512:(pc + 1) * 512])], writes=[sl.b])
            bi = 1 + pc % 2
            o = ps[0:17, bi * 512:(bi + 1) * 512]
            S.op("pe", [mm(o, cT3[:, kc, 0:17], s3[:, kc, :], kc == 0, False) for kc in range(16)]
                 + [mm(o, onesb[0:1, 0:17], s3[0:1, 16, :], False, True)],
                 reads=[cT.b, sl.b, cB.b], writes=[PB[bi]])
            ch, co = pc // 4, (pc % 4) * 512
            dst = mod.ap[0:17, pc * 512:(pc + 1) * 512]
            if ch in (0, 3):
                S.op("act", lambda e, dst=dst, o=o: e.activation(dst, o, AF.Copy), reads=[PB[bi]], writes=[mod.b])
            elif ch in (1, 4):
                g = gv.ap[0:17, (0 if ch == 1 else 2) * D + co:(0 if ch == 1 else 2) * D + co + 512]
                S.op("dve", lambda e, dst=dst, o=o, g=g: e.scalar_tensor_tensor(dst, o, 1.0, g, ALU.add, ALU.mult),
                     reads=[PB[bi], gv.b], writes=[mod.b])
            else:
                g = gv.ap[0:17, (1 if ch == 2 else 3) * D + co:(1 if ch == 2 else 3) * D + co + 512]
                S.op("dve", lambda e, dst=dst, o=o, g=g: e.tensor_tensor(dst, o, g, ALU.mult),
                     reads=[PB[bi], gv.b], writes=[mod.b])
        S.dma("sp", [(mod_d, mod.ap[0:17, :])], reads=[mod.b], writes=[MODD])
        S.barrier()
        A.release(m0)

    phase0()

    def load_mod_bc(tile, ch):
        S.dma("sp", [(tile.ap, mod_d[16:17, ch * D:(ch + 1) * D].to_broadcast([128, D]))], reads=[MODD], writes=[tile.b])

    halo = A.alloc("halo", 36)
    halo3 = halo.v("p (c i) -> p c i", i=3)
    S.op("dve", lambda e: e.memset(halo.ap, 0.0), writes=[halo.b])
    hst = A.alloc("hst", 1024)
    S.op("dve", lambda e: e.memset(hst.ap, 0.0), writes=[hst.b])
    hb = A.alloc("hb", 1024, BF16)
    S.op("dve", lambda e: e.memset(hb.ap, 0.0), writes=[hb.b])
    xb = [A.alloc("xb%d" % i, D) for i in range(2)]
    junk = A.alloc("junk", D, BF16)
    ub = A.alloc("ub", D, BF16)
    tmp32 = A.alloc("tmp32", D)
    sm = A.alloc("sm", 768)
    smc = [0]

    def small(n):
        if smc[0] + n > 768:
            smc[0] = 0
        a = sm.ap[:, smc[0]:smc[0] + n]
        smc[0] += n
        return a

    def rms_rstd(src_ap, n, rd, wr_extra=()):
        ssn = small(1)
        pp = src_ap.shape[0]
        S.op("act", lambda e: e.activation(junk.ap[0:pp, 0:n], src_ap, AF.Square, scale=float(n) ** -0.5, accum_out=ssn[0:pp]),
             reads=list(rd), writes=[junk.b, sm.b])
        S.op("dve", lambda e: e.tensor_scalar(ssn[0:pp], ssn[0:pp], EPS, None, ALU.add), reads=[sm.b], writes=[sm.b])
        S.op("act", lambda e: e.activation(ssn[0:pp], ssn[0:pp], AF.Ln), reads=[sm.b], writes=[sm.b])
        S.op("act", lambda e: e.activation(ssn[0:pp], ssn[0:pp], AF.Exp, scale=-0.5), reads=[sm.b], writes=[sm.b])
        return ssn

    def norm_mod_T(x_tile, gm, sh, dstT3, col0, np_=128, mask=None):
        rstd = rms_rstd(x_tile.ap[0:np_], D, [x_tile.b])
        S.op("dve", lambda e: e.scalar_tensor_tensor(tmp32.ap[0:np_], x_tile.ap[0:np_], rstd[0:np_], gm.ap[0:np_], ALU.mult, ALU.mult),
             reads=[x_tile.b, sm.b, gm.b], writes=[tmp32.b])
        if mask is None:
            S.op("dve", lambda e: e.tensor_tensor(ub.ap[0:np_], tmp32.ap[0:np_], sh.ap[0:np_], ALU.add), reads=[tmp32.b, sh.b], writes=[ub.b])
        else:
            S.op("dve", lambda e: e.scalar_tensor_tensor(ub.ap[0:np_], sh.ap[0:np_], mask, tmp32.ap[0:np_], ALU.mult, ALU.add),
                 reads=[tmp32.b, sh.b, mt.b], writes=[ub.b])
        transpose_to(ub, dstT3, col0, np_)

    def transpose_to(src_bf, dstT3, col0, np_=128, nk=16, kofs=0):
        for half in range(nk // 8):
            bi = 1
            pv = bank16(bi).rearrange("p (k m) -> p k m", m=128)
            S.op("pe", [tp(pv[:, k, 0:np_], src_bf.ap[0:np_, (half * 8 + k) * 128:(half * 8 + k + 1) * 128], identb[0:np_, 0:np_]) for k in range(8)],
                 reads=[src_bf.b, cB.b], writes=[PB[bi]])
            eng = "act" if half % 2 == 0 else "dve"
            dst = dstT3[:, kofs + half * 8:kofs + half * 8 + 8, col0:col0 + np_]
            if eng == "act":
                S.op("act", lambda e, dst=dst, pv=pv: e.activation(dst, pv[:, 0:8, 0:np_], AF.Copy), reads=[PB[bi]], writes=[dstT3_buf[id(dstT3)]])
            else:
                S.op("dve", lambda e, dst=dst, pv=pv: e.tensor_copy(dst, pv[:, 0:8, 0:np_]), reads=[PB[bi]], writes=[dstT3_buf[id(dstT3)]])

    dstT3_buf = {}

    def reg3(tile, k):
        v3 = tile.v("p (k n) -> p k n", k=k)
        dstT3_buf[id(v3)] = tile.b
        return v3

    def silu_to(dst, src, n, rd, wr, np_=128, tmp=None):
        t = tmp if tmp is not None else tmp32
        S.op("act", lambda e: e.activation(t.ap[0:np_, 0:n], src, AF.Tanh, scale=0.5), reads=list(rd), writes=[t.b])
        S.op("dve", lambda e: e.scalar_tensor_tensor(t.ap[0:np_, 0:n], t.ap[0:np_, 0:n], 1.0, src, ALU.add, ALU.mult), reads=list(rd) + [t.b], writes=[t.b])
        S.op("act", lambda e: e.activation(dst, t.ap[0:np_, 0:n], AF.Copy, scale=0.5), reads=[t.b], writes=list(wr))
    xs_d = din("xs_d", [16, D])
    ck_d = din("ck_d", [16, 128, 256])
    cv_d = din("cv_d", [16, 128, 256])
    sconv_d = din("sconv_d", [16, 3 * 1536])
    sssm_d = din("sssm_d", [16, 1024, 128])
    convw_raw = din("convw_raw", [1, 4 * 1536])
    convb_raw = din("convb_raw", [1, 1536])
    sinkcol = din("sinkcol", [16, 1])
    sbias = din("sbias", [16, 128])
    selkv = din("selkv", [16, 4])
    dskrow = din("dskrow", [1, 1024])
    ys_out = dout("ys_out", [16, D])
    ks_out = dout("ks_out", [16, 128, 256])
    vs_out = dout("vs_out", [16, 128, 256])
    convs_out = dout("convs_out", [16, 3 * 1536])
    ssms_out = dout("ssms_out", [16, 1024, 128])
    att_d = dscr("att_d", [16, 1024])
    KSO = Buf("ks_out"); VSO = Buf("vs_out"); ATTD = Buf("att_d")
    catsT = A.alloc("catsT", 16 * 16, BF16); catsT3 = reg3(catsT, 16)
    u2sT = A.alloc("u2sT", 16 * 16, BF16); u2sT3 = reg3(u2sT, 16)
    hsT = A.alloc("hsT", NJ * 16, BF16); hsT3 = hsT.v("p (j b) -> p j b", j=NJ)
    x1s = A.alloc("x1s", D)

    def load_mod_rows(tile, ch):
        S.dma("sp", [(tile.ap[0:16, :], mod_d[0:16, ch * D:(ch + 1) * D])], reads=[MODD], writes=[tile.b])

    def phase_SM1():
        m = A.mark()
        smc[0] = 0
        load_mod_rows(gm1, 1)
        load_mod_rows(sh1, 0)
        xt = xb[0]
        S.dma("sp", [(xt.ap[0:16, :], xs_d)], writes=[xt.b])
        usT = A.alloc("usT", 16 * 16, BF16); usT3 = reg3(usT, 16)
        norm_mod_T(xt, gm1, sh1, usT3, 0, np_=16)
        pj = A.alloc("proj_s", INW)
        sel = A.alloc("sel", 16 * 128); sel3 = sel.v("p (b m) -> p b m", b=16)
        cs = A.alloc("cat_s", D)
        cs16 = A.alloc("cat_s16", D, BF16)
        gb = A.alloc("g_bc", 2048)
        xa = A.alloc("xbc_s", 1536)
        S.op("dve", lambda e: e.tensor_copy(sel3[0:16], identf[0:16, 0:16].unsqueeze(2).to_broadcast([16, 16, 128])), reads=[cF.b], writes=[sel.b])
        m1 = A.mark()
        wsl = [A.alloc("wss%d" % i, 16 * 512, BF16) for i in range(2)]
        w3s = [t.v("p (k n) -> p k n", k=16) for t in wsl]
        for pc in range(9):
            n = 512 if pc < 8 else 16
            i = pc % 2
            S.dma("pool", [(w3s[i][:, :, 0:n], w_in_v[:, :, pc * 512:pc * 512 + n])], writes=[wsl[i].b])
            bi = 4 + pc % 4
            o = ps[0:16, bi * 512:bi * 512 + n]
            S.op("pe", [mm(o, usT3[:, kc, 0:16], w3s[i][:, kc, 0:n], kc == 0, kc == 15) for kc in range(16)], reads=[usT.b, wsl[i].b], writes=[PB[bi]])
            S.op("act", lambda e, o=o, pc=pc, n=n: e.activation(pj.ap[0:16, pc * 512:pc * 512 + n], o, AF.Copy), reads=[PB[bi]], writes=[pj.b])
        P = pj.ap
        S.barrier()
        A.release(m1)
        S.dma("sp", [(ks_out[:, 0:127, :], ck_d[:, 1:128, :]), (ks_out[:, 127, :], P[0:16, 1024:1280])], reads=[pj.b], writes=[KSO])
        S.dma("sp", [(vs_out[:, 0:127, :], cv_d[:, 1:128, :]), (vs_out[:, 127, :], P[0:16, 1280:1536])], reads=[pj.b], writes=[VSO])
        S.dma("sp", [(convs_out[:, 0:3072], sconv_d[:, 1536:4608]), (convs_out[:, 3072:4608], P[0:16, 2560:4096])], reads=[pj.b])
        Ka = A.alloc("Ka", 16 * 256); Ka3 = Ka.v("p (b n) -> p b n", b=16)
        S.dma("sp", [(Ka3, ks_out.rearrange("b s n -> s b n"))], reads=[KSO], writes=[Ka.b])
        Vh = A.alloc("Vh", 16 * 256, BF16); Vh3 = Vh.v("p (b n) -> p b n", b=16)
        S.dma("pool", [(Vh3, vs_out.rearrange("b s n -> s b n"))], reads=[VSO], writes=[Vh.b])
        cst = A.alloc("scst", 128 + 8)
        S.dma("sp", [(cst.ap[0:16, 0:128], sbias), (cst.ap[0:16, 128:129], sinkcol), (cst.ap[0:16, 129:133], selkv)], writes=[cst.b])
        sk8c = cst.ap[0:16, 133:134]
        S.op("dve", lambda e: e.tensor_scalar(sk8c, cst.ap[0:16, 128:129], 8.0, None, ALU.mult), reads=[cst.b], writes=[cst.b])
        prod = A.alloc("prod", 1024)
        STt = A.alloc("STt", 16 * 16); ST3 = STt.v("p (b h) -> p b h", b=16)
        for b in range(16):
            for hf in range(2):
                S.op("pe", mm(bank(2 + hf), sel3[0:16, b, :], P[0:16, hf * 512:(hf + 1) * 512], True, True), reads=[sel.b, pj.b], writes=[PB[2 + hf]])
                S.op("dve", lambda e, b=b, hf=hf: e.tensor_tensor(prod.ap[:, hf * 512:(hf + 1) * 512].rearrange("p (k g d) -> p k g d", k=2, g=4),
                                                                 bank(2 + hf).rearrange("p (k g d) -> p k g d", k=2, g=4),
                                                                 Ka3[:, b, hf * 128:(hf + 1) * 128].rearrange("p (k d) -> p k d", k=2).unsqueeze(2).to_broadcast([128, 2, 4, 64]), ALU.mult),
                     reads=[PB[2 + hf], Ka.b], writes=[prod.b])
            S.op("dve", lambda e, b=b: e.tensor_reduce(ST3[:, b, :], prod.v("p (h d) -> p h d", h=16), AX.X, ALU.add), reads=[prod.b], writes=[STt.b])
        for b in range(16):
            S.op("pe", tp(ps[0:16, 4 * 512 + b * 128:4 * 512 + (b + 1) * 128], ST3[:, b, :], identf), reads=[STt.b, cF.b], writes=[PB[4 + b // 4]])
        tsm = A.alloc("tsm", 2048); t3 = tsm.v("p (b s) -> p b s", b=16)
        S.op("dve", lambda e: e.tensor_tensor(t3[0:16], ps[0:16, 2048:4096].rearrange("p (b s) -> p b s", b=16),
                                              cst.ap[0:16, 0:128].unsqueeze(1).to_broadcast([16, 16, 128]), ALU.add),
             reads=[PB[4], PB[5], PB[6], PB[7], cst.b], writes=[tsm.b])
        mxs = small(16); ngs = small(16); rss = small(16); dns = small(16)
        S.op("dve", lambda e: e.tensor_reduce(mxs[0:16], t3[0:16], AX.X, ALU.max), reads=[tsm.b], writes=[sm.b])
        S.op("dve", lambda e: e.tensor_scalar(mxs[0:16], mxs[0:16], sk8c, None, ALU.max), reads=[sm.b, cst.b], writes=[sm.b])
        S.op("dve", lambda e: e.tensor_scalar(ngs[0:16], mxs[0:16], -0.125, None, ALU.mult), reads=[sm.b], writes=[sm.b])
        S.op("dve", lambda e: e.scalar_tensor_tensor(t3[0:16], t3[0:16], 0.125, ngs[0:16].unsqueeze(2).to_broadcast([16, 16, 128]), ALU.mult, ALU.add),
             reads=[tsm.b, sm.b], writes=[tsm.b])
        S.op("act", lambda e: e.activation(tsm.ap[0:16], tsm.ap[0:16], AF.Exp), reads=[tsm.b], writes=[tsm.b])
        S.op("dve", lambda e: e.tensor_reduce(rss[0:16], t3[0:16], AX.X, ALU.add), reads=[tsm.b], writes=[sm.b])
        S.op("dve", lambda e: e.tensor_scalar(dns[0:16], ngs[0:16], cst.ap[0:16, 128:129], None, ALU.add), reads=[sm.b, cst.b], writes=[sm.b])
        S.op("act", lambda e: e.activation(dns[0:16], dns[0:16], AF.Exp), reads=[sm.b], writes=[sm.b])
        S.op("dve", lambda e: e.tensor_tensor(dns[0:16], dns[0:16], rss[0:16], ALU.add), reads=[sm.b], writes=[sm.b])
        S.op("dve", lambda e: e.reciprocal(dns[0:16], dns[0:16]), reads=[sm.b], writes=[sm.b])
        Pb = A.alloc("Pb", 2048, BF16); Pb3 = Pb.v("p (b s) -> p b s", b=16)
        S.op("dve", lambda e: e.tensor_tensor(Pb3[0:16], t3[0:16], dns[0:16].unsqueeze(2).to_broadcast([16, 16, 128]), ALU.mult), reads=[tsm.b, sm.b], writes=[Pb.b])
        pvb = bank16(1).rearrange("p (b h) -> p b h", h=16)
        S.op("pe", [tp(pvb[:, b, :], Pb3[0:16, b, :], identb[0:16, 0:16]) for b in range(16)], reads=[Pb.b, cB.b], writes=[PB[1]])
        PTs = A.alloc("PTs", 256, BF16); PTs3 = PTs.v("p (b h) -> p b h", b=16)
        S.op("act", lambda e: e.activation(PTs3, pvb[:, 0:16, :], AF.Copy), reads=[PB[1]], writes=[PTs.b])
        ah = A.alloc("ah", 16 * 64); ah3 = ah.v("p (b d) -> p b d", b=16)
        t4 = A.alloc("t4s", 8 * 256)
        for half in range(2):
            for bb in range(8):
                b = half * 8 + bb
                S.op("pe", mm(ps[0:16, 4 * 512 + bb * 256:4 * 512 + (bb + 1) * 256], PTs3[:, b, :], Vh3[:, b, :], True, True), reads=[PTs.b, Vh.b], writes=[PB[4 + bb // 2]])
            S.op("dve", lambda e: e.tensor_tensor(t4.ap[0:16].rearrange("p (b k d) -> p b k d", b=8, k=4),
                                                  ps[0:16, 2048:4096].rearrange("p (b k d) -> p b k d", b=8, k=4),
                                                  cst.ap[0:16, 129:133].unsqueeze(1).unsqueeze(3).to_broadcast([16, 8, 4, 64]), ALU.mult),
                 reads=[PB[4], PB[5], PB[6], PB[7], cst.b], writes=[t4.b])
            S.op("dve", lambda e, half=half: e.tensor_reduce(ah3[0:16, half * 8:(half + 1) * 8, :], t4.ap[0:16].rearrange("p (b k d) -> p b d k", b=8, k=4), AX.X, ALU.add),
                 reads=[t4.b], writes=[ah.b])
        S.dma("sp", [(att_d.rearrange("b (h d) -> h b d", h=16), ah3[0:16])], reads=[ah.b], writes=[ATTD])
        S.dma("sp", [(cs.ap[0:16, 0:1024], att_d)], reads=[ATTD], writes=[cs.b])
        S.barrier()
        A.release(m1)
        S.dma("sp", [(gb.ap[0:16, 0:1024], gatt.to_broadcast([16, 1024])), (gb.ap[0:16, 1024:2048], gssm.to_broadcast([16, 1024]))], writes=[gb.b])
        rstd = rms_rstd(cs.ap[0:16, 0:1024], 1024, [cs.b])
        S.op("dve", lambda e: e.scalar_tensor_tensor(cs16.ap[0:16, 0:1024], cs.ap[0:16, 0:1024], rstd[0:16], gb.ap[0:16, 0:1024], ALU.mult, ALU.mult),
             reads=[cs.b, sm.b, gb.b], writes=[cs16.b])
        cwb = A.alloc("cwb", 5 * 1536)
        S.dma("sp", [(cwb.ap[0:16, 0:6144], convw_raw.to_broadcast([16, 6144])), (cwb.ap[0:16, 6144:7680], convb_raw.to_broadcast([16, 1536]))], writes=[cwb.b])
        sc_ = A.alloc("sconv", 3 * 1536)
        S.dma("sp", [(sc_.ap[0:16, :], sconv_d)], writes=[sc_.b])
        xc = A.alloc("xc", 1536); xc2 = A.alloc("xc2", 1536)
        S.op("dve", lambda e: e.tensor_tensor(xc.ap[0:16], P[0:16, 2560:4096], cwb.ap[0:16, 3 * 1536:4 * 1536], ALU.mult), reads=[pj.b, cwb.b], writes=[xc.b])
        S.op("dve", lambda e: e.tensor_tensor(xc.ap[0:16], xc.ap[0:16], cwb.ap[0:16, 6144:7680], ALU.add), reads=[xc.b, cwb.b], writes=[xc.b])
        for i in range(3):
            S.op("dve", lambda e, i=i: e.tensor_tensor(xc2.ap[0:16], sc_.ap[0:16, i * 1536:(i + 1) * 1536], cwb.ap[0:16, i * 1536:(i + 1) * 1536], ALU.mult), reads=[sc_.b, cwb.b], writes=[xc2.b])
            S.op("dve", lambda e: e.tensor_tensor(xc.ap[0:16], xc.ap[0:16], xc2.ap[0:16], ALU.add), reads=[xc.b, xc2.b], writes=[xc.b])
        silu_to(xa.ap[0:16], xc.ap[0:16], 1536, [xc.b], [xa.b], np_=16, tmp=xc2)
        S.barrier()
        A.release(m1)
        tz = A.alloc("tmpz", 1536)
        x0 = small(16); ax = small(16); dts = small(16); dAs = small(16)
        S.op("dve", lambda e: e.tensor_tensor(x0[0:16], P[0:16, 4096:4112], dtb_bc[0:16], ALU.add), reads=[pj.b, hv.b], writes=[sm.b])
        S.op("act", lambda e: e.activation(ax[0:16], x0[0:16], AF.Abs), reads=[sm.b], writes=[sm.b])
        S.op("act", lambda e: e.activation(ax[0:16], ax[0:16], AF.Exp, scale=-1.0), reads=[sm.b], writes=[sm.b])
        S.op("dve", lambda e: e.tensor_scalar(ax[0:16], ax[0:16], 1.0, None, ALU.add), reads=[sm.b], writes=[sm.b])
        S.op("act", lambda e: e.activation(ax[0:16], ax[0:16], AF.Ln), reads=[sm.b], writes=[sm.b])
        S.op("dve", lambda e: e.scalar_tensor_tensor(dts[0:16], x0[0:16], 0.0, ax[0:16], ALU.max, ALU.add), reads=[sm.b], writes=[sm.b])
        S.op("dve", lambda e: e.tensor_tensor(dAs[0:16], dts[0:16], a_bc.ap[0:16], ALU.mult), reads=[sm.b, a_bc.b], writes=[sm.b])
        S.op("act", lambda e: e.activation(dAs[0:16], dAs[0:16], AF.Exp), reads=[sm.b], writes=[sm.b])
        XE = A.alloc("XE", 2048)
        S.op("dve", lambda e: e.tensor_tensor(XE.ap[0:16, 0:1024].rearrange("p (h d) -> p h d", h=16), xa.ap[0:16, 0:1024].rearrange("p (h d) -> p h d", h=16),
                                              dts[0:16].unsqueeze(2).to_broadcast([16, 16, 64]), ALU.mult), reads=[xa.b, sm.b], writes=[XE.b])
        S.op("dve", lambda e: e.tensor_copy(XE.ap[0:16, 1024:2048].rearrange("p (h d) -> p h d", h=16), dAs[0:16].unsqueeze(2).to_broadcast([16, 16, 64])),
             reads=[sm.b], writes=[XE.b])
        XT = A.alloc("XT", 16 * 16); XT3 = XT.v("p (j b) -> p j b", j=16)
        S.op("pe", [tp(bank(0, 16, j * 16), XE.ap[0:16, j * 128:(j + 1) * 128], identf[0:16, 0:16]) for j in range(16)], reads=[XE.b, cF.b], writes=[PB[0]])
        S.op("act", lambda e: e.activation(XT.ap, bank(0, 256), AF.Copy), reads=[PB[0]], writes=[XT.b])
        yT = A.alloc("yT", 8 * 16); yT3 = yT.v("p (j b) -> p j b", j=8)
        h0 = [A.alloc("h0_%d" % i, 1024) for i in range(2)]
        h1 = [A.alloc("h1_%d" % i, 1024) for i in range(2)]
        for b in range(16):
            ht = h0[b % 2]; hn = h1[b % 2]
            S.dma("sp", [(ht.v("p (j n) -> p j n", j=8), sssm_d[b].rearrange("(j q) n -> q j n", q=128))], writes=[ht.b])
            S.op("pe", mm(bank(3), sel3[0:16, b, :], xa.ap[0:16, 1024:1536], True, True), reads=[sel.b, xa.b], writes=[PB[3]])
            S.op("dve", lambda e, ht=ht, b=b: e.tensor_tensor(ht.v("p (j n) -> p j n", j=8), ht.v("p (j n) -> p j n", j=8),
                                                             XT3[:, 8:16, b].unsqueeze(2).to_broadcast([128, 8, 128]), ALU.mult), reads=[ht.b, XT.b], writes=[ht.b])
            S.op("dve", lambda e, hn=hn, b=b: e.tensor_tensor(hn.v("p (g r n) -> p g r n", g=2, r=4),
                                                             bank(3, 256).rearrange("p (g n) -> p g n", g=2).unsqueeze(2).to_broadcast([128, 2, 4, 128]),
                                                             XT3[:, 0:8, b].rearrange("p (g r) -> p g r", g=2).unsqueeze(3).to_broadcast([128, 2, 4, 128]), ALU.mult),
                 reads=[PB[3], XT.b], writes=[hn.b])
            S.op("dve", lambda e, hn=hn, ht=ht: e.tensor_tensor(hn.ap, hn.ap, ht.ap, ALU.add), reads=[hn.b, ht.b], writes=[hn.b])
            S.dma("sp", [(ssms_out[b].rearrange("(j q) n -> q j n", q=128), hn.v("p (j n) -> p j n", j=8))], reads=[hn.b])
            S.op("dve", lambda e, hn=hn, ht=ht: e.tensor_tensor(ht.v("p (g r n) -> p g r n", g=2, r=4), hn.v("p (g r n) -> p g r n", g=2, r=4),
                                                               bank(3, 256, 256).rearrange("p (g n) -> p g n", g=2).unsqueeze(2).to_broadcast([128, 2, 4, 128]), ALU.mult),
                 reads=[hn.b, PB[3]], writes=[ht.b])
            S.op("dve", lambda e, ht=ht, b=b: e.tensor_reduce(yT3[:, :, b], ht.v("p (j n) -> p j n", j=8), AX.X, ALU.add), reads=[ht.b], writes=[yT.b])
        S.op("pe", [tp(ps[0:16, 2 * 512 + j * 128:2 * 512 + (j + 1) * 128], yT3[:, j, :], identf) for j in range(8)], reads=[yT.b, cF.b], writes=[PB[2], PB[3]])
        ys = A.alloc("y_s", 1024)
        dkb = A.alloc("dkb", 1024)
        S.dma("sp", [(dkb.ap[0:16, :], dskrow.to_broadcast([16, 1024]))], writes=[dkb.b])
        S.op("dve", lambda e: e.tensor_tensor(ys.ap[0:16], xa.ap[0:16, 0:1024], dkb.ap[0:16], ALU.mult), reads=[xa.b, dkb.b], writes=[ys.b])
        S.op("dve", lambda e: e.tensor_tensor(ys.ap[0:16], ys.ap[0:16], ps[0:16, 1024:2048], ALU.add), reads=[ys.b, PB[2], PB[3]], writes=[ys.b])
        zt = A.alloc("z_s", 1024)
        silu_to(zt.ap[0:16], P[0:16, 1536:2560], 1024, [pj.b], [zt.b], np_=16, tmp=tz)
        S.op("dve", lambda e: e.tensor_tensor(ys.ap[0:16], ys.ap[0:16], zt.ap[0:16], ALU.mult), reads=[ys.b, zt.b], writes=[ys.b])
        rstd2 = rms_rstd(ys.ap[0:16], 1024, [ys.b])
        S.op("dve", lambda e: e.scalar_tensor_tensor(cs16.ap[0:16, 1024:2048], ys.ap[0:16], rstd2[0:16], gb.ap[0:16, 1024:2048], ALU.mult, ALU.mult),
             reads=[ys.b, sm.b, gb.b], writes=[cs16.b])
        transpose_to(cs16, catsT3, 0, np_=16)
        S.barrier()
        A.release(m)
    m_gm = A.mark()
    gm1 = A.alloc("gm1", D)
    sh1 = A.alloc("sh1", D)
    load_mod_bc(gm1, 1)
    load_mod_bc(sh1, 0)
    mS = A.mark()
    u2T_d = dscr("u2T_d", [16, 128, NOWN], BF16)
    CATD = Buf("cat_d"); X1D = Buf("x1_d"); U2TD = Buf("u2T_d"); HTD = Buf("hT_d")
    w_dt = A.alloc("w_dt", 16 * 16, BF16)
    w_dt3 = w_dt.v("p (k n) -> p k n", k=16)
    S.dma("pool", [(w_dt3, w_in_v[:, :, 4096:4112])], writes=[w_dt.b])
    A_mrow = [A.alloc("mrow_t", 512)]
    A_xp = [A.alloc("xp%d" % i, 515) for i in range(2)]
    A_acc = [A.alloc("acc%d" % i, 512) for i in range(2)]
    A_st = [A.alloc("silt", 512)]
    A_pt = [A.alloc("ptmp", 512)]
    A_xst = [A.alloc("xs_tok", 1024)]
    A_bt = [A.alloc("Btok", 256, BF16)]
    A_xd = [A.alloc("Xd", 1024, BF16)]

    def conv_tiles(uT3, uTb, T, col_tok0, dests, cts, wfn, load_mask=True):
        for ct in cts:
            bi = 4 + ct % 4
            o = bank(bi, T)
            wb = wfn(ct, 0)[1]
            S.op("pe", [mm(o, wfn(ct, kc)[0], uT3[:, kc, 0:T], kc == 0, kc == 15) for kc in range(16)],
                 reads=[wb, uTb], writes=[PB[bi]])
            xp = A_xp[ct % 2]
            S.op("act", lambda e, xp=xp, ct=ct: e.activation(xp.ap[:, 0:3], halo3[:, ct, :], AF.Copy), reads=[halo.b], writes=[xp.b])
            S.op("act", lambda e, xp=xp, o=o: e.activation(xp.ap[:, 3:3 + T], o, AF.Copy), reads=[PB[bi]], writes=[xp.b])
            S.op("act", lambda e, xp=xp, ct=ct: e.activation(halo3[:, ct, :], xp.ap[:, T:T + 3], AF.Copy), reads=[xp.b], writes=[halo.b])
            acc = A_acc[ct % 2]
            S.op("act", lambda e, xp=xp, acc=acc, ct=ct: e.activation(acc.ap[:, 0:T], xp.ap[:, 3:3 + T], AF.Identity, bias=cbi.ap[:, ct:ct + 1], scale=cw.ap[:, ct * 4 + 3:ct * 4 + 4]),
                 reads=[xp.b, cw.b, cbi.b], writes=[acc.b])
            for i in (2, 1, 0):
                S.op("dve", lambda e, xp=xp, acc=acc, ct=ct, i=i: e.scalar_tensor_tensor(acc.ap[:, 0:T], xp.ap[:, i:i + T], cw.ap[:, ct * 4 + i:ct * 4 + i + 1], acc.ap[:, 0:T], ALU.mult, ALU.add),
                     reads=[xp.b, cw.b, acc.b], writes=[acc.b])
            dst, db = dests(ct)
            silu_to(dst, acc.ap[:, 0:T], T, [acc.b], [db], tmp=(A_st[0] if ct % 2 == 0 else A_pt[0]))

    def dt_group(uT3, uTb, nb, blk_ext0, own=False, c0=0):
        smc[0] = 0
        W = 16 * nb
        o = bank(0, W)
        for b in range(nb):
            S.op("pe", [mm(o[:, b * 16:(b + 1) * 16], uT3[:, kc, c0 + b * 128:c0 + (b + 1) * 128], w_dt3[:, kc, :], kc == 0, kc == 15) for kc in range(16)],
                 reads=[uTb, w_dt.b], writes=[PB[0]])
        x0 = small(W); ax = small(W); dt = small(W); dA = small(W)
        v3 = lambda ap: ap.rearrange("p (b h) -> p b h", b=nb)
        S.op("dve", lambda e: e.tensor_tensor(v3(x0), v3(o), dtb_bc.unsqueeze(1).to_broadcast([128, nb, 16]), ALU.add), reads=[PB[0], hv.b], writes=[sm.b])
        S.op("act", lambda e: e.activation(ax, x0, AF.Abs), reads=[sm.b], writes=[sm.b])
        S.op("act", lambda e: e.activation(ax, ax, AF.Exp, scale=-1.0), reads=[sm.b], writes=[sm.b])
        S.op("dve", lambda e: e.tensor_scalar(ax, ax, 1.0, None, ALU.add), reads=[sm.b], writes=[sm.b])
        S.op("act", lambda e: e.activation(ax, ax, AF.Ln), reads=[sm.b], writes=[sm.b])
        S.op("dve", lambda e: e.scalar_tensor_tensor(dt, x0, 0.0, ax, ALU.max, ALU.add), reads=[sm.b], writes=[sm.b])
        S.op("dve", lambda e: e.tensor_tensor(v3(dt), v3(dt), mt.ap[:, blk_ext0:blk_ext0 + nb].unsqueeze(2).to_broadcast([128, nb, 16]), ALU.mult), reads=[sm.b, mt.b], writes=[sm.b])
        S.op("dve", lambda e: e.tensor_tensor(v3(dA), v3(dt), a_bc.ap.unsqueeze(1).to_broadcast([128, nb, 16]), ALU.mult), reads=[sm.b, a_bc.b], writes=[sm.b])
        o2 = bank(0, 2 * W, 64)
        S.op("pe", [mm(o2[:, 0:W], Umat, dA, True, True), mm(o2[:, W:2 * W], onesf, dA, True, True)], reads=[cF.b, sm.b], writes=[PB[0]])
        at = small(2 * W)
        S.op("act", lambda e: e.activation(at, o2, AF.Copy), reads=[PB[0]], writes=[sm.b])
        acum, tot = at[:, 0:W], at[:, W:2 * W]
        de = small(W); cd = small(W); w1 = small(W)
        S.op("dve", lambda e: e.tensor_tensor(de, tot, acum, ALU.subtract), reads=[sm.b], writes=[sm.b])
        S.op("act", lambda e: e.activation(de, de, AF.Exp), reads=[sm.b], writes=[sm.b])
        S.op("act", lambda e: e.activation(cd, tot, AF.Exp), reads=[sm.b], writes=[sm.b])
        S.op("dve", lambda e: e.tensor_tensor(w1, dt, de, ALU.mult), reads=[sm.b], writes=[sm.b])
        ea = None
        if own:
            ea = small(W)
            S.op("act", lambda e: e.activation(ea, acum, AF.Exp), reads=[sm.b], writes=[sm.b])
        res = []
        for b in range(nb):
            sl = slice(b * 16, (b + 1) * 16)
            r = dict(dt=dt[:, sl], dA=dA[:, sl], acum=acum[:, sl], tot=tot[:, sl], cd=cd[:, sl], w1=w1[:, sl])
            if own:
                r["ea"] = ea[:, sl]
            res.append(r)
        return res

    def state_part1(xsT3, xsTb, BT3, BTb, c0):
        xt_ = A_xst[0]
        S.op("pe", [tp(bank(2 + i // 4, 128, (i % 4) * 128), xsT3[:, i, c0:c0 + 128], identf) for i in range(8)],
             reads=[xsTb, cF.b], writes=[PB[2], PB[3]])
        S.op("act", lambda e: e.activation(xt_.ap[:, 0:512], bank(2), AF.Copy), reads=[PB[2]], writes=[xt_.b])
        S.op("act", lambda e: e.activation(xt_.ap[:, 512:1024], bank(3), AF.Copy), reads=[PB[3]], writes=[xt_.b])
        S.op("pe", [tp(bank(0, 128, 128 + g * 128), BT3[:, g, c0:c0 + 128], identf) for g in range(2)], reads=[BTb, cF.b], writes=[PB[0]])
        Bt = A_bt[0]
        S.op("act", lambda e: e.activation(Bt.ap, bank(0, 256, 128), AF.Copy), reads=[PB[0]], writes=[Bt.b])
        return xt_, Bt

    def state_part2(xt_, Bt, sc_, keep_hb):
        Xd = A_xd[0]
        S.op("dve", lambda e: e.tensor_tensor(Xd.v("p (h d) -> p h d", h=16), xt_.v("p (h d) -> p h d", h=16),
                                              sc_["w1"].unsqueeze(2).to_broadcast([128, 16, 64]), ALU.mult),
             reads=[xt_.b, sm.b], writes=[Xd.b])
        S.op("pe", [mm(bank(6 + g), Bt.ap[:, g * 128:(g + 1) * 128], Xd.ap[:, g * 512:(g + 1) * 512], True, True) for g in range(2)],
             reads=[Bt.b, Xd.b], writes=[PB[6], PB[7]])
        S.op("dve", lambda e: e.tensor_tensor(hst.v("p (h d) -> p h d", h=16), hst.v("p (h d) -> p h d", h=16),
                                              sc_["cd"].unsqueeze(2).to_broadcast([128, 16, 64]), ALU.mult),
             reads=[hst.b, sm.b], writes=[hst.b])
        S.op("dve", lambda e: e.tensor_tensor(hst.ap[:, 0:512], hst.ap[:, 0:512], bank(6), ALU.add), reads=[hst.b, PB[6]], writes=[hst.b])
        S.op("dve", lambda e: e.tensor_tensor(hst.ap[:, 512:1024], hst.ap[:, 512:1024], bank(7), ALU.add), reads=[hst.b, PB[7]], writes=[hst.b])
        if keep_hb:
            S.op("act", lambda e: e.activation(hb.ap, hst.ap, AF.Copy), reads=[hst.b], writes=[hb.b])

    def load_x_norm(tok0, nblk, uT3, masked=False):
        for b in range(nblk):
            xt = xb[b % 2]
            S.dma("sp", [(xt.ap, xext[tok0 + b * 128:tok0 + (b + 1) * 128, :])], writes=[xt.b])
            blk = tok0 // 128 + b
            norm_mod_T(xt, gm1, sh1, uT3, b * 128, mask=(mt.ap[:, blk:blk + 1] if masked else None))

    mA = A.mark()
    wx = A.alloc("wx", 16 * 1536, BF16)
    wx3 = wx.v("p (k n) -> p k n", k=16)
    S.dma("pool", [(wx3[:, 0:8, :], w_in_v[:, 0:8, 2560:4096]), (wx3[:, 8:16, :], w_in_v[:, 8:16, 2560:4096])], writes=[wx.b])
    uTa = A.alloc("uTa", 16 * 512, BF16)
    uTa3 = reg3(uTa, 16)
    xsTa = A.alloc("xsTa", 8 * 512)
    xsTa3 = xsTa.v("p (k n) -> p k n", k=8)
    BTa = A.alloc("BTa", 2 * 512)
    BTa3 = BTa.v("p (k n) -> p k n", k=2)

    CTd = A.alloc("CTdummy", 2 * 512)
    CTd3 = CTd.v("p (k n) -> p k n", k=2)

    def destsA(ct):
        if ct < 8:
            return xsTa3[:, ct, 0:512], xsTa.b
        if ct < 10:
            return BTa3[:, ct - 8, 0:512], BTa.b
        return CTd3[:, ct - 10, 0:512], CTd.b

    for g in range(NPRE_G - n_pre_groups, NPRE_G):
        load_x_norm(g * 512, 4, uTa3, masked=True)
        conv_tiles(uTa3, uTa.b, 512, g * 512, destsA, range(12 if g == NPRE_G - 1 else 10), lambda ct, kc: (wx3[:, kc, ct * 128:(ct + 1) * 128], wx.b))
        scs = dt_group(uTa3, uTa.b, 4, g * 4)
        for b in range(4):
            sc_ = scs[b]
            xt_, Bt = state_part1(xsTa3, xsTa.b, BTa3, BTa.b, b * 128)
            state_part2(xt_, Bt, sc_, keep_hb=(g == NPRE_G - 1 and b == 3))
    S.barrier()
    A.release(mA)

    OWN0 = NEXT - NOWN
    def phase_B1b():
        m = A.mark()
        slots = [A.alloc("ws%d" % i, 16 * 512, BF16) for i in range(2)]
        s3 = [t.v("p (k n) -> p k n", k=16) for t in slots]
        uT = A.alloc("uTb", 16 * 256, BF16); uT3 = reg3(uT, 16)
        xsT = A.alloc("xsTb", 8 * 256); xsT3 = xsT.v("p (k n) -> p k n", k=8)
        BT = A.alloc("BTb", 2 * 256); BT3 = BT.v("p (k n) -> p k n", k=2)
        BTh = A.alloc("BTh", 2 * 256, BF16); BTh3 = BTh.v("p (k n) -> p k n", k=2)
        CTh = A.alloc("CTh", 2 * 256, BF16); CTh3 = CTh.v("p (k n) -> p k n", k=2)
        zs = A.alloc("zs", 2 * 1024); zs3 = zs.v("p (b n) -> p b n", b=2)
        gs = A.alloc("gssm", 1024)
        S.dma("sp", [(gs.ap, gssm.to_broadcast([128, 1024]))], writes=[gs.b])
        Rt = A.alloc("Rt", 2048); R3 = Rt.v("p (h l) -> p h l", h=16)
        Et = A.alloc("Et", 2048, BF16)
        MTt = A.alloc("MTt", 2048, BF16); MT3 = MTt.v("p (h l) -> p h l", h=16)
        CBm = A.alloc("CBm", 256)
        Xb = A.alloc("Xb", 1024, BF16)
        yt = A.alloc("yt", 1024)
        y2 = A.alloc("y2", 1024)
        sso = A.alloc("sso", 1024, BF16)
        si = [0]

        def next_slot(src_cols, ncols=512):
            i = si[0] % 2
            si[0] += 1
            S.dma("pool", [(s3[i][:, :, 0:ncols], w_in_v[:, :, src_cols:src_cols + ncols])], writes=[slots[i].b])
            return s3[i], slots[i].b

        def dests(ct):
            if ct < 8:
                return xsT3[:, ct, 0:256], xsT.b
            if ct < 10:
                return BT3[:, ct - 8, 0:256], BT.b
            return CTh3[:, ct - 10, 0:256], CTh.b

        for og in range(NOWN // 256):
            tok0 = OWN0 + og * 256
            load_x_norm(tok0, 2, uT3)
            for pc in range(3):
                w3, wb = next_slot(2560 + pc * 512)
                conv_tiles(uT3, uT.b, 256, tok0, dests, range(pc * 4, pc * 4 + 4),
                           lambda ct, kc, w3=w3, wb=wb, pc=pc: (w3[:, kc, (ct - pc * 4) * 128:(ct - pc * 4 + 1) * 128], wb), load_mask=(pc == 0))
            S.op("act", lambda e: e.activation(BTh.ap, BT.ap, AF.Copy), reads=[BT.b], writes=[BTh.b])
            for pc in range(2):
                w3, wb = next_slot(1536 + pc * 512)
                for b in range(2):
                    bi = 4 + (pc * 2 + b) % 4
                    S.op("pe", [mm(bank(bi), uT3[:, kc, b * 128:(b + 1) * 128], w3[:, kc, :], kc == 0, kc == 15) for kc in range(16)],
                         reads=[uT.b, wb], writes=[PB[bi]])
                    silu_to(zs3[:, b, pc * 512:(pc + 1) * 512], bank(bi), 512, [PB[bi]], [zs.b], tmp=A_st[0])
            for b in range(2):
                c0 = b * 128
                sc_ = dt_group(uT3, uT.b, 1, tok0 // 128 + b, own=True, c0=c0)[0]
                xt_, Bt = state_part1(xsT3, xsT.b, BT3, BT.b, c0)
                S.op("dve", lambda e: e.tensor_tensor(R3, Umat.unsqueeze(1).to_broadcast([128, 16, 128]),
                                                      sc_["dA"].unsqueeze(2).to_broadcast([128, 16, 128]), ALU.mult),
                     reads=[cF.b, sm.b], writes=[Rt.b])
                for q in range(4):
                    S.op("pe", mm(bank(4 + q), UTmat, Rt.ap[:, q * 512:(q + 1) * 512], True, True), reads=[cF.b, Rt.b], writes=[PB[4 + q]])
                    S.op("act", lambda e, q=q: e.activation(Et.ap[:, q * 512:(q + 1) * 512], bank(4 + q), AF.Exp), reads=[PB[4 + q]], writes=[Et.b])
                S.op("pe", [mm(bank(0, 128, 256 + g * 128), BTh3[:, g, c0:c0 + 128], CTh3[:, g, c0:c0 + 128], True, True) for g in range(2)],
                     reads=[BTh.b, CTh.b], writes=[PB[0]])
                S.op("dve", lambda e: e.tensor_tensor(CBm.v("p (g l) -> p g l", g=2), bank(0, 256, 256).rearrange("p (g l) -> p g l", g=2),
                                                      caus01.unsqueeze(1).to_broadcast([128, 2, 128]), ALU.mult),
                     reads=[PB[0], cF.b], writes=[CBm.b])
                S.op("dve", lambda e: e.tensor_tensor(MTt.v("p (g r l) -> p g r l", g=2, r=8), Et.v("p (g r l) -> p g r l", g=2, r=8),
                                                      CBm.v("p (g l) -> p g l", g=2).unsqueeze(2).to_broadcast([128, 2, 8, 128]), ALU.mult),
                     reads=[Et.b, CBm.b], writes=[MTt.b])
                S.op("dve", lambda e: e.tensor_tensor(Xb.v("p (h d) -> p h d", h=16), xt_.v("p (h d) -> p h d", h=16),
                                                      sc_["dt"].unsqueeze(2).to_broadcast([128, 16, 64]), ALU.mult),
                     reads=[xt_.b, sm.b], writes=[Xb.b])
                S.op("pe", [mm(bank(2 + h // 8, 64, (h % 8) * 64), MT3[:, h, :], Xb.ap[:, h * 64:(h + 1) * 64], True, True) for h in range(16)],
                     reads=[MTt.b, Xb.b], writes=[PB[2], PB[3]])
                S.op("pe", [mm(bank(4 + g), CTh3[:, g, c0:c0 + 128], hb.ap[:, g * 512:(g + 1) * 512], True, True) for g in range(2)],
                     reads=[CTh.b, hb.b], writes=[PB[4], PB[5]])
                for g in range(2):
                    S.op("dve", lambda e, g=g: e.tensor_tensor(yt.ap[:, g * 512:(g + 1) * 512].rearrange("p (h d) -> p h d", h=8),
                                                               bank(4 + g).rearrange("p (h d) -> p h d", h=8),
                                                               sc_["ea"][:, g * 8:(g + 1) * 8].unsqueeze(2).to_broadcast([128, 8, 64]), ALU.mult),
                         reads=[PB[4 + g], sm.b], writes=[yt.b])
                    S.op("dve", lambda e, g=g: e.tensor_tensor(yt.ap[:, g * 512:(g + 1) * 512], yt.ap[:, g * 512:(g + 1) * 512], bank(2 + g), ALU.add),
                         reads=[PB[2 + g], yt.b], writes=[yt.b])
                S.op("dve", lambda e: e.tensor_tensor(y2.v("p (h d) -> p h d", h=16), xt_.v("p (h d) -> p h d", h=16),
                                                      dsk_bc.unsqueeze(2).to_broadcast([128, 16, 64]), ALU.mult),
                     reads=[xt_.b, hv.b], writes=[y2.b])
                S.op("dve", lambda e: e.tensor_tensor(yt.ap, yt.ap, y2.ap, ALU.add), reads=[yt.b, y2.b], writes=[yt.b])
                S.op("dve", lambda e, b=b: e.tensor_tensor(yt.ap, yt.ap, zs3[:, b, :], ALU.mult), reads=[yt.b, zs.b], writes=[yt.b])
                rstd = rms_rstd(yt.ap, 1024, [yt.b])
                S.op("dve", lambda e, rstd=rstd: e.scalar_tensor_tensor(sso.ap, yt.ap, rstd, gs.ap, ALU.mult, ALU.mult), reads=[yt.b, sm.b, gs.b], writes=[sso.b])
                tk = og * 256 + c0
                S.dma("sp", [(cat_d[tk:tk + 128, 1024:2048], sso.ap)], reads=[sso.b], writes=[CATD])
                state_part2(xt_, Bt, sc_, keep_hb=True)
        so = yt
        S.op("pe", [tp(bank(2 + i // 4, 128, (i % 4) * 128), hst.ap[:, i * 128:(i + 1) * 128], identf) for i in range(8)],
             reads=[hst.b, cF.b], writes=[PB[2], PB[3]])
        S.op("act", lambda e: e.activation(so.ap[:, 0:512], bank(2), AF.Copy), reads=[PB[2]], writes=[so.b])
        S.op("act", lambda e: e.activation(so.ap[:, 512:1024], bank(3), AF.Copy), reads=[PB[3]], writes=[so.b])
        S.dma("sp", [(ssm_out.rearrange("(i p) n -> p i n", p=128), so.v("p (i n) -> p i n", i=8))], reads=[so.b])
        co = Rt
        S.op("pe", [tp(ps[0:3, 4 * 512 + ct * 128:4 * 512 + (ct + 1) * 128], halo3[:, ct, :], identf) for ct in range(12)],
             reads=[halo.b, cF.b], writes=[PB[4], PB[5], PB[6]])
        S.op("act", lambda e: e.activation(co.ap[0:3, 0:1536], ps[0:3, 4 * 512:4 * 512 + 1536], AF.Copy), reads=[PB[4], PB[5], PB[6]], writes=[co.b])
        S.dma("sp", [(conv_out, co.ap[0:3, 0:1536])], reads=[co.b])
        S.barrier()
        A.release(m)

    phase_B1b()
    A.release(mS)

    def phase_B1a():
        m = A.mark()
        slots = [A.alloc("wq%d" % i, 16 * 512, BF16) for i in range(2)]
        s3 = [t.v("p (k n) -> p k n", k=16) for t in slots]
        uT = A.alloc("uTq", 16 * 512, BF16); uT3 = reg3(uT, 16)
        uTh = A.alloc("uTh", 16 * 128, BF16); uTh3 = reg3(uTh, 16)
        qT = A.alloc("qT", 8 * 512, BF16); qT3 = qT.v("p (k n) -> p k n", k=8)
        kT = A.alloc("kT", 2 * 4 * 640, BF16); kT4 = kT.v("p (e k n) -> p e k n", e=2, k=4)
        Vt = A.alloc("Vt", 5 * 256, BF16); V3 = Vt.v("p (s n) -> p s n", s=5)
        ab1t = A.alloc("ab", 4096)
        S.dma("sp", [(ab1t.ap, abias0)], writes=[ab1t.b])
        ga = A.alloc("gatt", 1024)
        S.dma("sp", [(ga.ap, gatt.to_broadcast([128, 1024]))], writes=[ga.b])
        sk8 = A.alloc("sk8", 16)
        S.op("dve", lambda e: e.tensor_scalar(sk8.ap, sink_bc, 8.0, None, ALU.mult), reads=[hv.b], writes=[sk8.b])
        tS = A.alloc("tS", 1024)
        Pt = A.alloc("Pt", 1024, BF16)
        PT = A.alloc("PT", 1024, BF16)
        at_ = A.alloc("att", 1024)
        ao = A.alloc("atto", 1024, BF16)
        kvo = A.alloc("kvo", 512)
        si = [0]

        def kv_load():
            j = 0
            S.dma("pool", [(s3[j], w_in_v[:, :, 1024:1536])], writes=[slots[j].b])
            return (s3[j], slots[j].b)

        def k_dup(wv, eo):
            i = 1
            d4 = slots[i].v("p (k v n) -> p k v n", k=16, v=4)
            src4 = wv[0][:, :, 0:256].rearrange("p k (v n) -> p k v n", v=4)
            S.op("dve", lambda e: e.memset(slots[i].ap, 0.0), writes=[slots[i].b])
            for kc in range(16):
                if kc % 2 == 0:
                    S.op("act", lambda e, kc=kc: e.activation(d4[:, kc, :, eo * 64:eo * 64 + 64], src4[:, kc, :, :], AF.Copy), reads=[wv[1]], writes=[slots[i].b])
                else:
                    S.op("dve", lambda e, kc=kc: e.tensor_copy(d4[:, kc, :, eo * 64:eo * 64 + 64], src4[:, kc, :, :]), reads=[wv[1]], writes=[slots[i].b])
            return (s3[i], slots[i].b)

        def k_proj(wk, eo, u3, ub_, T, col0):
            for kv in range(4):
                bi = 4 + kv
                S.op("pe", [mm(bank(bi, T), wk[0][:, kc, kv * 128:(kv + 1) * 128], u3[:, kc, 0:T], kc == 0, kc == 15) for kc in range(16)],
                     reads=[wk[1], ub_], writes=[PB[bi]])
                S.op("dve" if kv % 2 else "act",
                     (lambda e, kv=kv, bi=bi: e.tensor_copy(kT4[:, eo, kv, col0:col0 + T], bank(bi, T))) if kv % 2 else
                     (lambda e, kv=kv, bi=bi: e.activation(kT4[:, eo, kv, col0:col0 + T], bank(bi, T), AF.Copy)), reads=[PB[bi]], writes=[kT.b])

        vcnt = [0]

        def v_proj(wv, u3, ub_, c0, slot, keep32=None):
            bi = 2 + vcnt[0] % 2
            vcnt[0] += 1
            S.op("pe", [mm(bank(bi, 256), u3[:, kc, c0:c0 + 128], wv[0][:, kc, 256:512], kc == 0, kc == 15) for kc in range(16)],
                 reads=[wv[1], ub_], writes=[PB[bi]])
            S.op("dve", lambda e: e.tensor_copy(V3[:, slot, :], bank(bi, 256)), reads=[PB[bi]], writes=[Vt.b])
            if keep32 is not None:
                S.op("dve", lambda e: e.tensor_copy(keep32, bank(bi, 256)), reads=[PB[bi]], writes=[kvo.b])

        import os as _os
        NSTEP = int(_os.environ.get("B1A_STEPS", "99"))
        for og in range(4):
            tok0 = OWN0 + og * 512
            if NSTEP < 2: break
            if og == 0:
                xt = xb[0]
                S.dma("sp", [(xt.ap, xext[OWN0 - 128:OWN0, :])], writes=[xt.b])
                norm_mod_T(xt, gm1, sh1, uTh3, 0)
            if NSTEP < 3: break
            load_x_norm(tok0, 4, uT3)
            if NSTEP < 4: break
            wv = kv_load()
            for eo in range(2):
                wk = k_dup(wv, eo)
                if og == 0:
                    k_proj(wk, eo, uTh3, uTh.b, 128, 0)
                k_proj(wk, eo, uT3, uT.b, 512, 128)
            if og == 0:
                v_proj(wv, uTh3, uTh.b, 0, 0)
            if NSTEP < 7: break
            for b in range(int(_os.environ.get("B1A_VN", "4"))):
                v_proj(wv, uT3, uT.b, b * 128, 1 + b, keep32=(kvo.ap[:, 256:512] if (og == 3 and b == 3) else None))
            if og == 3:
                S.op("pe", [mm(bank(3, 256), uT3[:, kc, 384:512], wv[0][:, kc, 0:256], kc == 0, kc == 15) for kc in range(16)],
                     reads=[wv[1], uT.b], writes=[PB[3]])
                S.op("dve", lambda e: e.tensor_copy(kvo.ap[:, 0:256], bank(3, 256)), reads=[PB[3]], writes=[kvo.b])
                S.dma("sp", [(k_out, kvo.ap[:, 0:256]), (v_out, kvo.ap[:, 256:512])], reads=[kvo.b])
            if NSTEP < 8: break
            for pc in range(2):
                i = 1 - pc
                S.dma("pool", [(s3[i], w_in_v[:, :, pc * 512:(pc + 1) * 512])], writes=[slots[i].b])
                for t4 in range(4):
                    bi = 4 + t4
                    S.op("pe", [mm(bank(bi), s3[i][:, kc, t4 * 128:(t4 + 1) * 128], uT3[:, kc, :], kc == 0, kc == 15) for kc in range(16)],
                         reads=[slots[i].b, uT.b], writes=[PB[bi]])
                    S.op("act" if t4 % 2 == 0 else "dve",
                         (lambda e, t4=t4, bi=bi, pc=pc: e.activation(qT3[:, pc * 4 + t4, :], bank(bi), AF.Copy)) if t4 % 2 == 0 else
                         (lambda e, t4=t4, bi=bi, pc=pc: e.tensor_copy(qT3[:, pc * 4 + t4, :], bank(bi))),
                         reads=[PB[bi]], writes=[qT.b])
            ATT = int(_os.environ.get("ATT_STEPS", "99"))
            for b in range(4):
                if stop_after == "B1a_proj":
                    break
                if ATT < 99 and (og > 0 or b > 0):
                    break
                smc[0] = 0
                abt = ab1t
                if og == 0 and b == 1:
                    S.dma("sp", [(ab1t.ap, abias1)], writes=[ab1t.b])
                c0 = b * 128
                mx = small(16); ngm = small(16); rs = small(16)
                for kvg in range(4):
                    b0 = 4 + 2 * (kvg % 2)
                    mms = []
                    for j in range(4):
                        h = kvg * 4 + j
                        mms.append(mm(bank(b0 + j // 2, 256, (j % 2) * 256), qT3[:, h // 2, c0:c0 + 128],
                                      kT4[:, h % 2, kvg, c0:c0 + 256], True, True))
                    S.op("pe", mms, reads=[qT.b, kT.b], writes=[PB[b0], PB[b0 + 1]])
                    if ATT < 1: continue
                    for hh in range(2):
                        S.op("dve", lambda e, hh=hh, kvg=kvg, b0=b0: e.tensor_tensor(tS.ap[:, hh * 512:(hh + 1) * 512], bank(b0 + hh),
                                                                                    abt.ap[:, (kvg * 4 + hh * 2) * 256:(kvg * 4 + hh * 2 + 2) * 256], ALU.add),
                             reads=[PB[b0 + hh], abt.b], writes=[tS.b])
                    if ATT < 2: continue
                    S.op("dve", lambda e, kvg=kvg: e.tensor_reduce(mx[:, kvg * 4:kvg * 4 + 4], tS.v("p (h k) -> p h k", h=4), AX.X, ALU.max),
                         reads=[tS.b], writes=[sm.b])
                    S.op("dve", lambda e, kvg=kvg: e.tensor_tensor(mx[:, kvg * 4:kvg * 4 + 4], mx[:, kvg * 4:kvg * 4 + 4], sk8.ap[:, kvg * 4:kvg * 4 + 4], ALU.max),
                         reads=[sm.b, sk8.b], writes=[sm.b])
                    S.op("dve", lambda e, kvg=kvg: e.tensor_scalar(ngm[:, kvg * 4:kvg * 4 + 4], mx[:, kvg * 4:kvg * 4 + 4], -0.125, None, ALU.mult),
                         reads=[sm.b], writes=[sm.b])
                    if ATT < 3: continue
                    for j in range(4):
                        h = kvg * 4 + j
                        S.op("act", lambda e, j=j, h=h: e.activation(Pt.ap[:, j * 256:(j + 1) * 256], tS.ap[:, j * 256:(j + 1) * 256], AF.Exp,
                                                                    bias=ngm[:, h:h + 1], scale=0.125, accum_out=rs[:, h:h + 1]),
                             reads=[tS.b, sm.b], writes=[Pt.b, sm.b])
                    if ATT < 4: continue
                    pv = bank16(1).rearrange("p (k m) -> p k m", m=128)
                    S.op("pe", [tp(pv[:, k, :], Pt.ap[:, k * 128:(k + 1) * 128], identb) for k in range(8)], reads=[Pt.b, cB.b], writes=[PB[1]])
                    S.op("act", lambda e: e.activation(PT.ap, bank16(1), AF.Copy), reads=[PB[1]], writes=[PT.b])
                    if ATT < 5: continue
                    PT3 = PT.v("p (k m) -> p k m", k=8)
                    mms = []
                    for j in range(4):
                        h = kvg * 4 + j
                        for half in range(2):
                            mms.append(mm(bank(2 + h // 8, 64, (h % 8) * 64), PT3[:, j * 2 + half, :], V3[:, b + half, kvg * 64:(kvg + 1) * 64], half == 0, half == 1))
                    S.op("pe", mms, reads=[PT.b, Vt.b], writes=[PB[2], PB[3]])
                if ATT < 6: continue
                dn = small(16)
                S.op("dve", lambda e: e.scalar_tensor_tensor(dn, sk8.ap, 0.125, ngm, ALU.mult, ALU.add), reads=[sk8.b, sm.b], writes=[sm.b])
                S.op("act", lambda e: e.activation(dn, dn, AF.Exp), reads=[sm.b], writes=[sm.b])
                S.op("dve", lambda e: e.tensor_tensor(dn, dn, rs, ALU.add), reads=[sm.b], writes=[sm.b])
                S.op("dve", lambda e: e.reciprocal(dn, dn), reads=[sm.b], writes=[sm.b])
                for g in range(2):
                    S.op("dve", lambda e, g=g: e.tensor_tensor(at_.ap[:, g * 512:(g + 1) * 512].rearrange("p (h d) -> p h d", h=8),
                                                               bank(2 + g).rearrange("p (h d) -> p h d", h=8),
                                                               dn[:, g * 8:(g + 1) * 8].unsqueeze(2).to_broadcast([128, 8, 64]), ALU.mult),
                         reads=[PB[2 + g], sm.b], writes=[at_.b])
                rstd = rms_rstd(at_.ap, 1024, [at_.b])
                S.op("dve", lambda e, rstd=rstd: e.scalar_tensor_tensor(ao.ap, at_.ap, rstd, ga.ap, ALU.mult, ALU.mult), reads=[at_.b, sm.b, ga.b], writes=[ao.b])
                tk = og * 512 + c0
                S.dma("sp", [(cat_d[tk:tk + 128, 0:1024], ao.ap)], reads=[ao.b], writes=[CATD])
            for eo in range(2):
                S.op("act", lambda e, eo=eo: e.activation(kT4[:, eo, :, 0:128], kT4[:, eo, :, 512:640], AF.Copy), reads=[kT.b], writes=[kT.b])
            S.op("act", lambda e: e.activation(V3[:, 0, :], V3[:, 4, :], AF.Copy), reads=[Vt.b], writes=[Vt.b])
        S.barrier()
        A.release(m)

    if stop_after != "B1b":
        phase_B1a()
    if do_samples and stop_after is None:
        phase_SM1()
    A.release(m_gm)

    def phase_B2():
        m = A.mark()
        wo = A.alloc("wo", 16 * 2048, BF16); wo3 = wo.v("p (k n) -> p k n", k=16)
        w_out_v = w_out.rearrange("(kc p) n -> p kc n", p=128)
        for pcs in range(4):
            S.dma("pool", [(wo3[:, :, pcs * 512:(pcs + 1) * 512], w_out_v[:, :, pcs * 512:(pcs + 1) * 512])], writes=[wo.b])
        ggt1 = A.alloc("ggt1", D); gm2 = A.alloc("gm2", D); sh2 = A.alloc("sh2", D)
        load_mod_bc(ggt1, 2); load_mod_bc(gm2, 4); load_mod_bc(sh2, 3)
        cb_ = [A.alloc("catb%d" % i, D, BF16) for i in range(2)]
        cT = A.alloc("catT", 16 * 128, BF16); cT3 = reg3(cT, 16)
        u2 = A.alloc("u2T", 16 * 128, BF16); u23 = reg3(u2, 16)
        def b2_body(cT3v, cTbuf, np_, xt, u2dst3, x1_store, u2_store):
            ss4 = small(4)
            for q in range(4):
                o = ps[0:np_, (4 + q) * 512:(5 + q) * 512]
                S.op("pe", [mm(o, cT3v[:, kc, 0:np_], wo3[:, kc, q * 512:(q + 1) * 512], kc == 0, kc == 15) for kc in range(16)],
                     reads=[cTbuf, wo.b], writes=[PB[4 + q]])
                S.op("act", lambda e, q=q, o=o: e.activation(junk.ap[0:np_, 0:512], o, AF.Square, scale=float(D) ** -0.5, accum_out=ss4[0:np_, q:q + 1]),
                     reads=[PB[4 + q]], writes=[junk.b, sm.b])
            rstd = small(1)
            S.op("dve", lambda e: e.tensor_reduce(rstd[0:np_], ss4[0:np_], AX.X, ALU.add), reads=[sm.b], writes=[sm.b])
            S.op("dve", lambda e: e.tensor_scalar(rstd[0:np_], rstd[0:np_], EPS, None, ALU.add), reads=[sm.b], writes=[sm.b])
            S.op("act", lambda e: e.activation(rstd[0:np_], rstd[0:np_], AF.Ln), reads=[sm.b], writes=[sm.b])
            S.op("act", lambda e: e.activation(rstd[0:np_], rstd[0:np_], AF.Exp, scale=-0.5), reads=[sm.b], writes=[sm.b])
            for q in range(4):
                o = ps[0:np_, (4 + q) * 512:(5 + q) * 512]
                S.op("dve", lambda e, q=q, o=o: e.scalar_tensor_tensor(tmp32.ap[0:np_, q * 512:(q + 1) * 512], o, rstd[0:np_], ggt1.ap[0:np_, q * 512:(q + 1) * 512], ALU.mult, ALU.mult),
                     reads=[PB[4 + q], sm.b, ggt1.b], writes=[tmp32.b])
            S.op("dve", lambda e: e.tensor_tensor(xt.ap[0:np_], xt.ap[0:np_], tmp32.ap[0:np_], ALU.add), reads=[xt.b, tmp32.b], writes=[xt.b])
            x1_store(xt)
            norm_mod_T(xt, gm2, sh2, u2dst3, 0, np_=np_)
            u2_store()

        for blk in range(16):
            smc[0] = 0
            cbt = cb_[blk % 2]
            S.dma("sp", [(cbt.ap, cat_d[blk * 128:(blk + 1) * 128, :])], reads=[CATD], writes=[cbt.b])
            xt = xb[blk % 2]
            S.dma("sp", [(xt.ap, xext[OWN0 + blk * 128:OWN0 + (blk + 1) * 128, :])], writes=[xt.b])
            transpose_to(cbt, cT3, 0)
            b2_body(cT3, cT.b, 128, xt,  u23,
                    lambda xt, blk=blk: S.dma("sp", [(x1_d[blk * 128:(blk + 1) * 128, :], xt.ap)], reads=[xt.b], writes=[X1D]),
                    lambda blk=blk: S.dma("sp", [(u2T_d.rearrange("k p t -> p k t")[:, :, blk * 128:(blk + 1) * 128], u23)], reads=[u2.b], writes=[U2TD]))
        if do_samples:
            smc[0] = 0
            load_mod_rows(ggt1, 2); load_mod_rows(gm2, 4); load_mod_rows(sh2, 3)
            S.dma("sp", [(x1s.ap[0:16, :], xs_d)], writes=[x1s.b])
            b2_body(catsT3, catsT.b, 16, x1s, u2sT3, lambda xt: None, lambda: None)
        S.barrier()
        A.release(m)

    if stop_after not in ("B1b", "B1a", "B1a_proj"):
        phase_B2()

    def phase_B3():
        m = A.mark()
        uT = A.alloc("u2Tall", 16 * NOWN, BF16); uT3 = uT.v("p (k n) -> p k n", k=16)
        S.dma("sp", [(uT3[:, kq * 4:(kq + 1) * 4, :], u2T_d.rearrange("k p t -> p k t")[:, kq * 4:(kq + 1) * 4, :]) for kq in range(4)], reads=[U2TD], writes=[uT.b])
        gsl = [A.alloc("wg%d" % i, 16 * 256, BF16) for i in range(2)]
        usl = [A.alloc("wu%d" % i, 16 * 256, BF16) for i in range(2)]
        hst_ = [A.alloc("hstg%d" % i, NOWN, BF16) for i in range(2)]
        et = [A.alloc("et%d" % i, 512) for i in range(2)]
        hsk = A.alloc("hs_tok", DFF, BF16)
        wg_v = w_gate.rearrange("(kc p) n -> p kc n", p=128)
        wu_v = w_up.rearrange("(kc p) n -> p kc n", p=128)
        for pj in range(NJ // 2):
            g3 = gsl[pj % 2].v("p (k n) -> p k n", k=16)
            u3 = usl[pj % 2].v("p (k n) -> p k n", k=16)
            S.dma("pool", [(g3, wg_v[:, :, pj * 256:(pj + 1) * 256])], writes=[gsl[pj % 2].b])
            S.dma("pool", [(u3, wu_v[:, :, pj * 256:(pj + 1) * 256])], writes=[usl[pj % 2].b])
            for jj in range(2):
                j = pj * 2 + jj
                hs = hst_[j % 2]
                for tg in range(4):
                    bg, bu = 4 + (tg % 2) * 2, 5 + (tg % 2) * 2
                    S.op("pe", [mm(bank(bg), g3[:, kc, jj * 128:(jj + 1) * 128], uT3[:, kc, tg * 512:(tg + 1) * 512], kc == 0, kc == 15) for kc in range(16)],
                         reads=[gsl[pj % 2].b, uT.b], writes=[PB[bg]])
                    S.op("pe", [mm(bank(bu), u3[:, kc, jj * 128:(jj + 1) * 128], uT3[:, kc, tg * 512:(tg + 1) * 512], kc == 0, kc == 15) for kc in range(16)],
                         reads=[usl[pj % 2].b, uT.b], writes=[PB[bu]])
                    e_ = et[tg % 2]
                    S.op("act", lambda e, e_=e_, bg=bg: e.activation(e_.ap, bank(bg), AF.Tanh, scale=0.5), reads=[PB[bg]], writes=[e_.b])
                    S.op("dve", lambda e, e_=e_, bg=bg: e.scalar_tensor_tensor(e_.ap, e_.ap, 1.0, bank(bg), ALU.add, ALU.mult), reads=[e_.b, PB[bg]], writes=[e_.b])
                    S.op("dve", lambda e, e_=e_, bu=bu, hs=hs, tg=tg: e.scalar_tensor_tensor(hs.ap[:, tg * 512:(tg + 1) * 512], e_.ap, 0.5, bank(bu), ALU.mult, ALU.mult),
                         reads=[e_.b, PB[bu]], writes=[hs.b])
                S.dma("sp", [(hT_d[j], hs.ap)], reads=[hs.b], writes=[HTD])
            if do_samples:
                og_ = ps[0:16, 2 * 512:2 * 512 + 256]
                ou_ = ps[0:16, 3 * 512:3 * 512 + 256]
                S.op("pe", [mm(og_, u2sT3[:, kc, 0:16], g3[:, kc, :], kc == 0, kc == 15) for kc in range(16)], reads=[u2sT.b, gsl[pj % 2].b], writes=[PB[2]])
                S.op("pe", [mm(ou_, u2sT3[:, kc, 0:16], u3[:, kc, :], kc == 0, kc == 15) for kc in range(16)], reads=[u2sT.b, usl[pj % 2].b], writes=[PB[3]])
                e_ = et[0]
                S.op("act", lambda e, e_=e_: e.activation(e_.ap[0:16, 0:256], og_, AF.Tanh, scale=0.5), reads=[PB[2]], writes=[e_.b])
                S.op("dve", lambda e, e_=e_: e.scalar_tensor_tensor(e_.ap[0:16, 0:256], e_.ap[0:16, 0:256], 1.0, og_, ALU.add, ALU.mult), reads=[e_.b, PB[2]], writes=[e_.b])
                S.op("dve", lambda e, e_=e_, pj=pj: e.scalar_tensor_tensor(hsk.ap[0:16, pj * 256:(pj + 1) * 256], e_.ap[0:16, 0:256], 0.5, ou_, ALU.mult, ALU.mult), reads=[e_.b, PB[3]], writes=[hsk.b])
        if do_samples:
            for j0 in range(0, NJ, 8):
                nk = min(8, NJ - j0)
                pv = bank16(1).rearrange("p (k m) -> p k m", m=128)
                S.op("pe", [tp(pv[:, k, 0:16], hsk.ap[0:16, (j0 + k) * 128:(j0 + k + 1) * 128], identb[0:16, 0:16]) for k in range(nk)],
                     reads=[hsk.b, cB.b], writes=[PB[1]])
                S.op("act", lambda e, j0=j0, nk=nk, pv=pv: e.activation(hsT3[:, j0:j0 + nk, :], pv[:, 0:nk, 0:16], AF.Copy), reads=[PB[1]], writes=[hsT.b])
        S.barrier()
        A.release(m)

    def phase_B4():
        m = A.mark()
        ggt2 = A.alloc("ggt2", D)
        load_mod_bc(ggt2, 5)
        hT = A.alloc("hTg", NJ * 512, BF16); hT3 = hT.v("p (j t) -> p j t", j=NJ)
        wsl = [A.alloc("wd%d" % i, NJ * 256, BF16) for i in range(2)]
        ft = A.alloc("ft", 4 * D); f3 = ft.v("p (b n) -> p b n", b=4)
        wd_v = w_down.rearrange("(j p) n -> p j n", p=128)
        hT_v = hT_d.rearrange("j p t -> p j t")
        for tg in range(4):
            S.dma("sp", [(hT3[:, jq * 11:(jq + 1) * 11, :], hT_v[:, jq * 11:(jq + 1) * 11, tg * 512:(tg + 1) * 512]) for jq in range(4)], reads=[HTD], writes=[hT.b])
            for pc in range(8):
                w3 = wsl[pc % 2].v("p (j n) -> p j n", j=NJ)
                S.dma("pool", [(w3[:, jq * 11:(jq + 1) * 11, :], wd_v[:, jq * 11:(jq + 1) * 11, pc * 256:(pc + 1) * 256]) for jq in range(4)], writes=[wsl[pc % 2].b])
                for tb in range(4):
                    bi = 4 + tb
                    S.op("pe", [mm(bank(bi, 256), hT3[:, j, tb * 128:(tb + 1) * 128], w3[:, j, :], j == 0, j == NJ - 1) for j in range(NJ)],
                         reads=[hT.b, wsl[pc % 2].b], writes=[PB[bi]])
                    S.op("act", lambda e, tb=tb, bi=bi, pc=pc: e.activation(f3[:, tb, pc * 256:(pc + 1) * 256], bank(bi, 256), AF.Copy), reads=[PB[bi]], writes=[ft.b])
            for tb in range(4):
                smc[0] = 0
                blk = tg * 4 + tb
                xt = xb[tb % 2]
                S.dma("sp", [(xt.ap, x1_d[blk * 128:(blk + 1) * 128, :])], reads=[X1D], writes=[xt.b])
                rstd = rms_rstd(f3[:, tb, :], D, [ft.b])
                S.op("dve", lambda e, tb=tb, rstd=rstd: e.scalar_tensor_tensor(tmp32.ap, f3[:, tb, :], rstd, ggt2.ap, ALU.mult, ALU.mult), reads=[ft.b, sm.b, ggt2.b], writes=[tmp32.b])
                S.op("dve", lambda e, xt=xt: e.tensor_tensor(xt.ap, xt.ap, tmp32.ap, ALU.add), reads=[xt.b, tmp32.b], writes=[xt.b])
                S.dma("sp", [(y_out[blk * 128:(blk + 1) * 128, :], xt.ap)], reads=[xt.b])
        if do_samples:
            smc[0] = 0
            load_mod_rows(ggt2, 5)
            for pc in range(8):
                w3 = wsl[pc % 2].v("p (j n) -> p j n", j=NJ)
                S.dma("pool", [(w3[:, jq * 11:(jq + 1) * 11, :], wd_v[:, jq * 11:(jq + 1) * 11, pc * 256:(pc + 1) * 256]) for jq in range(4)], writes=[wsl[pc % 2].b])
                o = ps[0:16, (4 + pc % 4) * 512:(4 + pc % 4) * 512 + 256]
                S.op("pe", [mm(o, hsT3[:, j, :], w3[:, j, :], j == 0, j == NJ - 1) for j in range(NJ)], reads=[hsT.b, wsl[pc % 2].b], writes=[PB[4 + pc % 4]])
                S.op("act", lambda e, o=o, pc=pc: e.activation(f3[0:16, 0, pc * 256:(pc + 1) * 256], o, AF.Copy), reads=[PB[4 + pc % 4]], writes=[ft.b])
            rstd = rms_rstd(f3[0:16, 0, :], D, [ft.b])
            S.op("dve", lambda e: e.scalar_tensor_tensor(tmp32.ap[0:16], f3[0:16, 0, :], rstd[0:16], ggt2.ap[0:16], ALU.mult, ALU.mult), reads=[ft.b, sm.b, ggt2.b], writes=[tmp32.b])
            S.op("dve", lambda e: e.tensor_tensor(x1s.ap[0:16], x1s.ap[0:16], tmp32.ap[0:16], ALU.add), reads=[x1s.b, tmp32.b], writes=[x1s.b])
            S.dma("sp", [(ys_out, x1s.ap[0:16, :])], reads=[x1s.b])
        S.barrier()
        A.release(m)

    if stop_after is None:
        phase_B3()
        phase_B4()
    S.final_wait("sp")
    S.emit()
    st.close()
    print("[build] arena peak words", A.peak, "instr", {e: S.count[e] for e in S.count})
    return nc


def _consts():
    i = np.arange(128)
    ident = np.eye(128, dtype=np.float32)
    U = (i[:, None] <= i[None, :]).astype(np.float32)
    UT = (i[:, None] > i[None, :]).astype(np.float32)
    ones = np.ones((128, 128), np.float32)
    caus = (i[None, :] >= i[:, None]).astype(np.float32)
    cf32 = np.concatenate([ident, U, UT, ones, caus], axis=1)
    cb16 = np.concatenate([ident, ones], axis=1).astype(ml_dtypes.bfloat16)
    return cf32, cb16


def _abias(first_core):
    slopes = (2.0 ** (-8.0 * np.arange(1, 17) / 16)).astype(np.float32)
    a = np.arange(128)[:, None]
    j = np.arange(256)[None, :]
    dist = a + 128 - j
    valid = (dist >= 0) & (dist < 128)
    out = []
    for first in (True, False):
        v = valid & ((j >= 128) if (first and first_core) else True)
        b = np.where(v[None], -slopes[:, None, None] * dist[None].astype(np.float32), -30000.0) * 8.0
        out.append(np.ascontiguousarray(np.transpose(b, (1, 0, 2)).reshape(128, 16 * 256)).astype(np.float32))
    return out


def make_in_maps(inp):
    cf32, cb16 = _consts()
    xp = np.asarray(inp["x_prompt"])[0]
    maps = []
    wada = np.concatenate([np.asarray(inp["w_ada"])[0], np.asarray(inp["b_ada"])[0][None, :]], axis=0)
    gvec = np.stack([np.asarray(inp[k])[0] for k in ("g_pre_mix", "g_post_mix", "g_pre_ffn", "g_post_ffn")], 0)
    cwv = np.asarray(inp["conv_w"])[0]
    convw = np.ascontiguousarray(cwv.reshape(4, 12, 128).transpose(2, 1, 0).reshape(128, 48))
    convb = np.ascontiguousarray(np.asarray(inp["conv_b"])[0].reshape(12, 128).T)
    hvec = np.concatenate([np.asarray(inp[k])[0] for k in ("dt_bias", "a_log", "d_skip", "attn_sinks")])[None, :]
    slopes = (2.0 ** (-8.0 * np.arange(1, 17) / 16)).astype(np.float32)
    sbias = (-slopes[:, None] * (127 - np.arange(128))[None, :].astype(np.float32) * 8.0).astype(np.float32)
    selkv = (np.arange(16)[:, None] // 4 == np.arange(4)[None, :]).astype(np.float32)
    for c in range(8):
        nreal = 2048 * (c + 1)
        xext = np.zeros((NEXT, D), np.float32)
        xext[NEXT - nreal:] = xp[:nreal]
        m = np.zeros((NEXT,), np.float32)
        m[NEXT - nreal:] = 1.0
        ab0, ab1 = _abias(c == 0)
        maps.append({
            "xext": xext, "mrow": m[None, :], "mtok": np.ascontiguousarray(m.reshape(NEXT // 128, 128).T),
            "cmat": np.concatenate([np.asarray(inp["c_sample"])[16 * c:16 * c + 16], np.asarray(inp["c_prompt"])], 0),
            "wada": wada, "gvec": gvec, "w_in": np.asarray(inp["w_in"])[0], "w_out": np.asarray(inp["w_out"])[0],
            "w_gate": np.asarray(inp["w_gate"])[0], "w_up": np.asarray(inp["w_up"])[0], "w_down": np.asarray(inp["w_down"])[0],
            "convw": convw, "convb": convb, "hvec": hvec.astype(np.float32),
            "gatt": np.asarray(inp["g_attn_out"]), "gssm": np.asarray(inp["g_ssm_out"]),
            "cf32": cf32, "cb16": cb16, "abias0": ab0, "abias1": ab1,
            "xs_d": np.ascontiguousarray(np.asarray(inp["x_sample"])[16 * c:16 * c + 16, 0, :]),
            "ck_d": np.ascontiguousarray(np.asarray(inp["cache_k"])[0, 16 * c:16 * c + 16].reshape(16, 128, 256)),
            "cv_d": np.ascontiguousarray(np.asarray(inp["cache_v"])[0, 16 * c:16 * c + 16].reshape(16, 128, 256)),
            "sconv_d": np.ascontiguousarray(np.asarray(inp["state_conv"])[0, 16 * c:16 * c + 16].reshape(16, 4608)),
            "sssm_d": np.ascontiguousarray(np.asarray(inp["state_ssm"])[0, 16 * c:16 * c + 16].reshape(16, 1024, 128)),
            "convw_raw": np.ascontiguousarray(cwv.reshape(1, 6144)), "convb_raw": np.asarray(inp["conv_b"]).reshape(1, 1536),
            "sinkcol": np.asarray(inp["attn_sinks"]).reshape(16, 1), "sbias": sbias, "selkv": selkv,
            "dskrow": np.repeat(np.asarray(inp["d_skip"])[0], 64)[None, :].astype(np.float32),
        })
    return maps


_NC_CACHE = {}


def kernel(**inputs):
    if "nc" not in _NC_CACHE:
        _NC_CACHE["nc"] = build()
    nc = _NC_CACHE["nc"]
    maps = make_in_maps(inputs)
    res = run_bass_kernel_spmd(nc, maps, core_ids=list(range(8)))
    r = res.results
    f = np.float32
    y_prompt = np.concatenate([np.asarray(r[c]["y_out"]) for c in range(8)], 0)[None].astype(f)
    k_prompt = np.asarray(r[7]["k_out"]).reshape(1, 1, 128, 4, 64).astype(f)
    v_prompt = np.asarray(r[7]["v_out"]).reshape(1, 1, 128, 4, 64).astype(f)
    conv_prompt = np.asarray(r[7]["conv_out"]).reshape(1, 1, 3, 1536).astype(f)
    ssm_prompt = np.asarray(r[7]["ssm_out"]).reshape(1, 1, 16, 64, 128).astype(f)
    cat = lambda name: np.concatenate([np.asarray(r[c][name]) for c in range(8)], 0).astype(f)
    y_sample = cat("ys_out").reshape(128, 1, 2048)
    k_sample = cat("ks_out").reshape(1, 128, 128, 4, 64)
    v_sample = cat("vs_out").reshape(1, 128, 128, 4, 64)
    conv_sample = cat("convs_out").reshape(1, 128, 3, 1536)
    ssm_sample = cat("ssms_out").reshape(1, 128, 16, 64, 128)
    return (y_prompt, y_sample, k_prompt, v_prompt, conv_prompt, ssm_prompt,
            k_sample, v_sample, conv_sample, ssm_sample)
```
